# Optimizing a Trainium2 kernel written in Bass

```python
import math
import jax, jax.numpy as jnp
from jax import lax
import numpy as np

D_MODEL = 1024
BATCH = 8
SEQ = 4096
DEPTH = 4

HG_HEADS = 8
HG_DK = 128
HG_DV = 128
HG_WIDTH = HG_HEADS * HG_DK
HG_CHUNK = 64
LOG_FLOOR = 1e-30
FOX_HEADS = 16
FOX_DH = 64
FOX_WIDTH = FOX_HEADS * FOX_DH
FOX_QBLOCK = 128
FOX_F_BIAS_INIT = 2.0
MASK_VALUE = -1e30
N_EXPERTS = 32
TOP_K = 4
D_EXPERT = 1024
SWIGLU_LIMIT = 7.0
SWIGLU_ALPHA = 1.702
MOE_BLOCK = 256
LN_EPS = 1e-5
RMS_EPS = 1e-6
DEEPNORM_ALPHA = (2 * DEPTH) ** 0.25
DEEPNORM_BETA = (8 * DEPTH) ** -0.25
IN_SIZES = (HG_WIDTH, HG_WIDTH, HG_WIDTH, HG_WIDTH, FOX_WIDTH, FOX_WIDTH, FOX_WIDTH, FOX_HEADS, D_MODEL, D_MODEL)
P_IN = sum(IN_SIZES)

kernel_name = "hgrn2_fox_gated_moe_deepnorm_adaln"


def layer_norm(x, g, b):
    xf = x.astype(jnp.float32)
    mu = jnp.mean(xf, -1, keepdims=True)
    var = jnp.mean(jnp.square(xf - mu), -1, keepdims=True)
    return ((xf - mu) * lax.rsqrt(var + LN_EPS) * g + b).astype(x.dtype)


def in_projection(h, w):
    outs, off = [], 0
    for n in IN_SIZES:
        outs.append(jnp.einsum('bsd,de->bse', h, w[:, off:off + n]))
        off += n
    return outs


def hgrn2_mixer(q, f_raw, i, g_out, lb, norm_w):
    B, S, _ = q.shape
    f32 = jnp.float32
    n_chunks = S // HG_CHUNK
    lb = lb.astype(f32)
    fr = f_raw.astype(f32)
    sig = jax.nn.sigmoid(fr)
    f = lb + (1.0 - lb) * sig
    log_f = jnp.log(jnp.maximum(f, LOG_FLOOR))
    k = (1.0 - lb) * (1.0 - sig)

    def to_chunks(t, d):
        return t.reshape(B, n_chunks, HG_CHUNK, HG_HEADS, d).transpose(1, 0, 3, 2, 4)

    qc = to_chunks(jax.nn.silu(q.astype(f32)), HG_DK)
    kc = to_chunks(k, HG_DK)
    gc = to_chunks(log_f, HG_DK)
    ic = to_chunks(i.astype(f32), HG_DV)
    causal = jnp.tril(jnp.ones((HG_CHUNK, HG_CHUNK), bool))[:, :, None]

    def step(state, inp):
        qb, kb, gb, ib = inp
        b = jnp.cumsum(gb, axis=2)
        diff = b[:, :, :, None, :] - b[:, :, None, :, :]
        decay = jnp.where(causal, jnp.exp(jnp.where(causal, diff, 0.0)), 0.0)
        scores = jnp.einsum('bhtd,bhsd,bhtsd->bhts', qb, kb, decay)
        o = (jnp.einsum('bhts,bhsv->bhtv', scores, ib)
             + jnp.einsum('bhtd,bhdv->bhtv', qb * jnp.exp(b), state))
        b_last = b[:, :, -1:, :]
        new_state = (state * jnp.exp(b[:, :, -1])[..., None]
                     + jnp.einsum('bhsd,bhsv->bhdv', kb * jnp.exp(b_last - b), ib))
        return new_state, o

    state0 = jnp.zeros((B, HG_HEADS, HG_DK, HG_DV), f32)
    _, oc = lax.scan(step, state0, (qc, kc, gc, ic))
    o = oc.transpose(1, 0, 3, 2, 4).reshape(B, S, HG_HEADS, HG_DV)
    o = o * lax.rsqrt(jnp.mean(o * o, -1, keepdims=True) + RMS_EPS) * norm_w.astype(f32)
    o = o.reshape(B, S, HG_WIDTH) * jax.nn.silu(g_out.astype(f32))
    return o.astype(q.dtype)


def fox_mixer(q, k, v, f_logit):
    B, S, _ = q.shape
    f32 = jnp.float32
    heads = lambda t: t.reshape(B, S, FOX_HEADS, FOX_DH).transpose(0, 2, 1, 3)
    qh, kh, vh = heads(q), heads(k), heads(v)
    F = jnp.cumsum(jax.nn.log_sigmoid(f_logit.astype(f32)), axis=1).transpose(0, 2, 1)
    n_blocks = S // FOX_QBLOCK
    q_blocks = qh.reshape(B, FOX_HEADS, n_blocks, FOX_QBLOCK, FOX_DH).transpose(2, 0, 1, 3, 4)
    F_blocks = F.reshape(B, FOX_HEADS, n_blocks, FOX_QBLOCK).transpose(2, 0, 1, 3)
    starts = jnp.arange(n_blocks, dtype=jnp.int32) * FOX_QBLOCK
    key_pos = jnp.arange(S, dtype=jnp.int32)
    scale = FOX_DH ** -0.5

    def block(args):
        q_blk, F_blk, start = args
        logits = jnp.einsum('bhqd,bhkd->bhqk', q_blk, kh, preferred_element_type=f32) * scale
        logits = logits + (F_blk[..., :, None] - F[..., None, :])
        q_pos = start + jnp.arange(FOX_QBLOCK, dtype=jnp.int32)
        logits = jnp.where(key_pos[None, :] <= q_pos[:, None], logits, MASK_VALUE)
        p = jax.nn.softmax(logits, axis=-1)
        return jnp.einsum('bhqk,bhkd->bhqd', p.astype(vh.dtype), vh)

    o = lax.map(block, (q_blocks, F_blocks, starts))
    return o.transpose(1, 0, 3, 2, 4).reshape(B, S, FOX_WIDTH)


def moe_ffn(h, w_router, b_router, w_gate, b_gate, w_up, b_up, w_down, b_down):
    B, S, D = h.shape
    N = B * S
    f32 = jnp.float32
    xt = h.reshape(N, D)
    logits = (xt @ w_router + b_router).astype(f32)
    top_vals, top_idx = lax.top_k(logits, TOP_K)
    top_w = jax.nn.softmax(top_vals, axis=-1)
    n_assign = N * TOP_K
    e_flat = top_idx.reshape(-1).astype(jnp.int32)
    tok_flat = jnp.arange(n_assign, dtype=jnp.int32) // TOP_K
    w_flat = top_w.reshape(-1)
    order = jnp.argsort(e_flat)
    e_sorted = e_flat[order]
    counts = jnp.bincount(e_flat, length=N_EXPERTS)
    padded = ((counts + MOE_BLOCK - 1) // MOE_BLOCK) * MOE_BLOCK
    start = jnp.cumsum(counts) - counts
    pad_end = jnp.cumsum(padded)
    pad_start = pad_end - padded
    dest = pad_start[e_sorted] + (jnp.arange(n_assign, dtype=jnp.int32) - start[e_sorted])
    cap = ((n_assign + MOE_BLOCK - 1) // MOE_BLOCK) * MOE_BLOCK + N_EXPERTS * MOE_BLOCK
    n_blocks = cap // MOE_BLOCK
    row_tok = jnp.full((cap,), N, jnp.int32).at[dest].set(tok_flat[order])
    row_w = jnp.zeros((cap,), f32).at[dest].set(w_flat[order])
    block_start = jnp.arange(n_blocks, dtype=jnp.int32) * MOE_BLOCK
    block_expert = jnp.minimum(jnp.searchsorted(pad_end, block_start, side='right'), N_EXPERTS - 1)
    x_pad = jnp.concatenate([xt, jnp.zeros((1, D), xt.dtype)], axis=0)
    x_rows = x_pad[row_tok].reshape(n_blocks, MOE_BLOCK, D)

    def expert_block(args):
        xb, e = args
        g = jnp.minimum(xb @ w_gate[e] + b_gate[e], SWIGLU_LIMIT)
        u = jnp.clip(xb @ w_up[e] + b_up[e], -SWIGLU_LIMIT, SWIGLU_LIMIT)
        a = (u + 1.0) * (g * jax.nn.sigmoid(SWIGLU_ALPHA * g))
        return a @ w_down[e] + b_down[e]

    y_rows = lax.map(expert_block, (x_rows, block_expert)).reshape(cap, D)
    y = jax.ops.segment_sum(y_rows * row_w[:, None], row_tok, num_segments=N + 1)[:N]
    return y.reshape(B, S, D).astype(h.dtype)


def setup_inputs(seed: int = 0) -> dict:
    key = jax.random.key(seed)
    ks = jax.random.split(key, 22)
    f32 = jnp.float32
    nrm = lambda k, shape, s: jax.random.normal(k, shape, f32) * s
    beta = DEEPNORM_BETA
    col_scales = (1.0, 1.0, beta, 1.0, 1.0, 1.0, beta, 1.0, 1.0, 1.0)
    col_scale = jnp.concatenate([jnp.full((n,), s, f32) for n, s in zip(IN_SIZES, col_scales)])
    D = D_MODEL
    return {
        "x": nrm(ks[0], (BATCH, SEQ, D), 1.0),
        "c": nrm(ks[1], (BATCH, D), 1.0),
        "w_in": nrm(ks[2], (DEPTH, D, P_IN), D ** -0.5) * col_scale,
        "fox_f_bias": FOX_F_BIAS_INIT + nrm(ks[3], (DEPTH, FOX_HEADS), 0.1),
        "hg_lb_logits": nrm(ks[4], (DEPTH, HG_WIDTH), 0.1),
        "hg_norm_w": 1.0 + nrm(ks[5], (DEPTH, HG_DV), 0.02),
        "w_branch": nrm(ks[6], (DEPTH, HG_WIDTH + FOX_WIDTH, D), HG_WIDTH ** -0.5),
        "w_out": nrm(ks[7], (DEPTH, D, D), D ** -0.5 * beta),
        "ada_w": nrm(ks[8], (DEPTH, D, 6 * D), 0.1 * D ** -0.5),
        "ada_b": nrm(ks[9], (DEPTH, 6 * D), 0.01),
        "ln1_g": 1.0 + nrm(ks[10], (DEPTH, D), 0.02),
        "ln1_b": nrm(ks[11], (DEPTH, D), 0.02),
        "w_router": nrm(ks[12], (DEPTH, D, N_EXPERTS), D ** -0.5),
        "b_router": nrm(ks[13], (DEPTH, N_EXPERTS), 0.01),
        "w_gate": nrm(ks[14], (DEPTH, N_EXPERTS, D, D_EXPERT), D ** -0.5 * beta),
        "b_gate": nrm(ks[15], (DEPTH, N_EXPERTS, D_EXPERT), 0.02),
        "w_up": nrm(ks[16], (DEPTH, N_EXPERTS, D, D_EXPERT), D ** -0.5 * beta),
        "b_up": nrm(ks[17], (DEPTH, N_EXPERTS, D_EXPERT), 0.02),
        "w_down": nrm(ks[18], (DEPTH, N_EXPERTS, D_EXPERT, D), D_EXPERT ** -0.5 * beta),
        "b_down": nrm(ks[19], (DEPTH, N_EXPERTS, D), 0.02),
        "ln2_g": 1.0 + nrm(ks[20], (DEPTH, D), 0.02),
        "ln2_b": nrm(ks[21], (DEPTH, D), 0.02),
    }


def reference(x, c, w_in, fox_f_bias, hg_lb_logits, hg_norm_w, w_branch, w_out, ada_w, ada_b,
              ln1_g, ln1_b, w_router, b_router, w_gate, b_gate, w_up, b_up, w_down, b_down,
              ln2_g, ln2_b):
    p_lb = jax.nn.softmax(hg_lb_logits.astype(jnp.float32), axis=0)
    lb_all = jnp.clip(jnp.cumsum(p_lb, axis=0) - p_lb[0], 0.0, 1.0)
    c_act = jax.nn.silu(c)
    for l in range(DEPTH):
        mod = c_act @ ada_w[l] + ada_b[l]
        shift1, scale1, gate1, shift2, scale2, gate2 = [m[:, None, :] for m in jnp.split(mod, 6, axis=-1)]
        h = x * (1.0 + scale1) + shift1
        hq, hf, hi, hg, aq, ak, av, af, ga, gb = in_projection(h, w_in[l])
        o_a = hgrn2_mixer(hq, hf, hi, hg, lb_all[l], hg_norm_w[l])
        o_b = fox_mixer(aq, ak, av, af + fox_f_bias[l])
        p_a = jnp.einsum('bse,ed->bsd', o_a, w_branch[l, :HG_WIDTH])
        p_b = jnp.einsum('bse,ed->bsd', o_b, w_branch[l, HG_WIDTH:])
        merged = jax.nn.sigmoid(ga) * p_a + jax.nn.sigmoid(gb) * p_b
        y = jnp.einsum('bsd,de->bse', merged, w_out[l])
        x = layer_norm(DEEPNORM_ALPHA * x + (1.0 + gate1) * y, ln1_g[l], ln1_b[l])
        h2 = x * (1.0 + scale2) + shift2
        y2 = moe_ffn(h2, w_router[l], b_router[l], w_gate[l], b_gate[l], w_up[l], b_up[l], w_down[l], b_down[l])
        x = layer_norm(DEEPNORM_ALPHA * x + (1.0 + gate2) * y2, ln2_g[l], ln2_b[l])
    return x
```

```python
from contextlib import ExitStack
import numpy as np
import concourse.bass as bass
import concourse.mybir as mybir
from concourse.bass_utils import run_bass_kernel_spmd

F32 = mybir.dt.float32
F32R = mybir.dt.float32r
BF16 = mybir.dt.bfloat16
I32 = mybir.dt.int32
U32 = mybir.dt.uint32
AF = mybir.ActivationFunctionType
ALU = mybir.AluOpType

S = 4096
D = 1024
DEPTH = 4
NT = S // 128
HG_H = 8
FX_H = 16
NE = 32
PIN = 9232
ALPHA = float((2 * DEPTH) ** 0.25)
LN_EPS = 1e-5
RMS_EPS = 1e-6
CT = 8
CAP = CT * 128
NCST = 1056
OFF = dict(hq=0, hf=1024, hi=2048, hg=3072, aq=4096, ak=5120, av=6144, af=7168, ga=7184, gb=8208)

ENGS = ("pe", "act", "dve", "pool", "sp")
NDSEM = 8


class Prog:
    def __init__(self, nc, stack):
        self.nc = nc
        self.sem = {n: stack.enter_context(nc.semaphore("s_" + n)) for n in ENGS}
        self.dsem = {}
        for q in ("sp", "act", "pool"):
            for i in range(NDSEM):
                self.dsem[(q, i)] = stack.enter_context(nc.semaphore("d_%s%d" % (q, i)))
        self.cnt = {k: 0 for k in list(self.sem) + list(self.dsem)}
        self.seen = {n: {} for n in ENGS}
        self.last_w = {}
        self.readers = {}
        self.q = {n: [] for n in ENGS}
        self.dma_i = {"sp": 0, "act": 0, "pool": 0}
        self.dma_pending = {}
        self.n_inst = 0

    def all_sems(self):
        return list(self.sem.values()) + list(self.dsem.values())

    def _deps(self, reads, writes):
        deps = []
        for k in reads:
            if k in self.last_w:
                deps.append(self.last_w[k])
        for k in writes:
            if k in self.last_w:
                deps.append(self.last_w[k])
            deps.extend(self.readers.get(k, ()))
        return deps

    def _commit(self, stamp, reads, writes):
        for k in reads:
            self.readers.setdefault(k, []).append(stamp)
        for k in writes:
            self.last_w[k] = stamp
            self.readers[k] = []

    def _waits(self, eng, deps):
        need = {}
        for (sk, v) in deps:
            if sk == "pe" and eng == "pe":
                continue
            if self.seen[eng].get(sk, 0) >= v:
                continue
            if need.get(sk, 0) < v:
                need[sk] = v
        for sk, v in need.items():
            self.seen[eng][sk] = v
        return list(need.items())

    def _semh(self, sk):
        return self.sem[sk] if isinstance(sk, str) else self.dsem[sk]

    def op(self, eng, fn, reads=(), writes=()):
        deps = self._deps(reads, writes)
        waits = self._waits(eng, deps)
        self.cnt[eng] += 1
        stamp = (eng, self.cnt[eng])
        self._commit(stamp, reads, writes)
        semh = self.sem[eng]
        wl = [(self._semh(sk), v) for sk, v in waits]

        def run(E, fn=fn, wl=wl, semh=semh):
            for h, v in wl:
                E.wait_ge(h, v)
            fn(E).then_inc(semh, 1)
        self.q[eng].append(run)
        self.n_inst += 1
        return stamp

    def dma(self, q, fn, reads=(), writes=()):
        i = self.dma_i[q] % NDSEM
        self.dma_i[q] += 1
        sk = (q, i)
        deps = self._deps(reads, writes)
        if sk in self.dma_pending:
            deps.append(self.dma_pending[sk])
        waits = self._waits(q, deps)
        self.cnt[sk] += 16
        stamp = (sk, self.cnt[sk])
        self.dma_pending[sk] = stamp
        self._commit(stamp, reads, writes)
        semh = self.dsem[sk]
        wl = [(self._semh(s), v) for s, v in waits]

        def run(E, fn=fn, wl=wl, semh=semh):
            for h, v in wl:
                E.wait_ge(h, v)
            fn(E).then_inc(semh, 16)
        self.q[q].append(run)
        self.n_inst += 1
        return stamp

    def barrier(self):
        for eng in ENGS:
            wl = []
            for sk, v in self.cnt.items():
                if v == 0 or sk == eng:
                    continue
                if self.seen[eng].get(sk, 0) >= v:
                    continue
                self.seen[eng][sk] = v
                wl.append((self._semh(sk), v))
            if wl:
                def run(E, wl=wl):
                    for h, v in wl:
                        E.wait_ge(h, v)
                self.q[eng].append(run)
        self.last_w.clear()
        self.readers.clear()

    def flush(self, block):
        for eng, dec in (("sp", block.sync), ("pe", block.tensor), ("act", block.scalar),
                         ("dve", block.vector), ("pool", block.gpsimd)):
            lst = self.q[eng]
            if not lst:
                continue

            def body(E, lst=lst):
                for r in lst:
                    r(E)
            dec(body)
            self.q[eng] = []


STAGES = ["proj", "hgrn", "fox", "merge", "ln1", "experts", "ln2"]


class K:
    def __init__(self, layers=DEPTH, upto="all", dbg=()):
        self.layers = L = layers
        self.upto = upto
        self.nst = len(STAGES) if upto == "all" else STAGES.index(upto) + 1
        nc = self.nc = bass.Bass("TRN2", target_bir_lowering=False)
        IN = lambda n, s, dt=F32: nc.dram_tensor(n, list(s), dt, kind="ExternalInput").ap()
        SC = lambda n, s, dt=F32: nc.dram_tensor(n, list(s), dt, kind=("ExternalOutput" if n in dbg else "Internal")).ap()
        self.x_in = IN("x", [S, D])
        self.cT = IN("cT", [128, 8])
        self.w_in = IN("w_in", [L, D, PIN])
        self.ffb = IN("ffb", [16, DEPTH])
        self.lbl = IN("lbl", [128, DEPTH, 8])
        self.normw = IN("normw", [128, DEPTH])
        self.w_branch = IN("w_branch", [L, 2 * D, D])
        self.w_out = IN("w_out", [L, D, D])
        self.ada_w = IN("ada_w", [L, D, 6 * D])
        self.adaB = IN("adaB", [128, DEPTH, 48])
        self.ln1_g = IN("ln1_g", [DEPTH, D]); self.ln1_b = IN("ln1_b", [DEPTH, D])
        self.ln2_g = IN("ln2_g", [DEPTH, D]); self.ln2_b = IN("ln2_b", [DEPTH, D])
        self.w_router = IN("w_router", [DEPTH, D, NE])
        self.b_router = IN("b_router", [DEPTH, NE])
        if self.nst >= 6:
            self.w_gate = IN("w_gate", [L, NE, D, D])
            self.w_up = IN("w_up", [L, NE, D, D])
            self.w_down = IN("w_down", [L, NE, D, D])
        self.bgu = IN("bgu", [DEPTH, NE, 128, 16])
        self.b_down = IN("b_down", [DEPTH, NE, D])
        self.cst = IN("cst", [128, NCST])
        self.cstu = IN("cstu", [128, 128], U32)
        self.out = nc.dram_tensor("out", [S, D], F32, kind="ExternalOutput").ap()
        self.XS = SC("XS", [S, D])
        self.MODT = SC("MODT", [DEPTH, 48, 128])
        self.PT = {n: SC("PT_" + n, [D, S]) for n in ("hq", "hf", "hg", "aq", "ak", "ga", "gb")}
        self.PT["af"] = SC("PT_af", [16, S])
        self.PR = {n: SC("PR_" + n, [S, D]) for n in ("hi", "av")}
        self.OAT = SC("OAT", [D, S])
        self.OBT = SC("OBT", [D, S])
        self.YS = SC("YS", [S, D])
        self.XG = SC("XG", [NE * CAP, D])
        self.YG = SC("YG", [NE * CAP, D])
        self.build()

    def dma(self, q, out, in_, r=(), w=(), **kw):
        self.P.dma(q, lambda E, o=out, i=in_, kw=kw: E.dma_start(out=o, in_=i, **kw), reads=r, writes=w)

    def act(self, out, in_, func, r=(), w=(), **kw):
        self.P.op("act", lambda E, o=out, i=in_, f=func, kw=kw: E.activation(out=o, in_=i, func=f, **kw), reads=r, writes=w)

    def mm(self, out, lhsT, rhs, start, stop, r=(), w=()):
        self.P.op("pe", lambda E, o=out, a=lhsT, b=rhs, s=start, t=stop: E.matmul(o, a, b, start=s, stop=t), reads=r, writes=w)

    def tr(self, out, in_, ident, r=(), w=()):
        self.P.op("pe", lambda E, o=out, i=in_, d=ident: E.transpose(o, i, d), reads=r, writes=w)

    def tt(self, eng, out, in0, in1, op, r=(), w=()):
        self.P.op(eng, lambda E, o=out, a=in0, b=in1, p=op: E.tensor_tensor(out=o, in0=a, in1=b, op=p), reads=r, writes=w)

    def ts(self, eng, out, in0, s1, op0, s2=None, op1=None, r=(), w=()):
        if op1 is None:
            self.P.op(eng, lambda E, o=out, a=in0, s1=s1, p0=op0: E.tensor_scalar(out=o, in0=a, scalar1=s1, scalar2=None, op0=p0), reads=r, writes=w)
        else:
            self.P.op(eng, lambda E, o=out, a=in0, s1=s1, s2=s2, p0=op0, p1=op1: E.tensor_scalar(out=o, in0=a, scalar1=s1, scalar2=s2, op0=p0, op1=p1), reads=r, writes=w)

    def stt(self, out, in0, scalar, in1, op0, op1, r=(), w=()):
        self.P.op("dve", lambda E, o=out, a=in0, s=scalar, b=in1, p0=op0, p1=op1: E.scalar_tensor_tensor(out=o, in0=a, scalar=s, in1=b, op0=p0, op1=p1), reads=r, writes=w)

    def cp(self, eng, out, in_, r=(), w=()):
        self.P.op(eng, lambda E, o=out, i=in_: E.tensor_copy(out=o, in_=i), reads=r, writes=w)

    def ms(self, eng, out, val, w=()):
        self.P.op(eng, lambda E, o=out, v=val: E.memset(o, v), writes=w)

    def build(self):
        nc = self.nc
        with ExitStack() as st:
            P = self.P = Prog(nc, st)
            with nc.Block() as b0:
                @b0.sync
                def _(E):
                    for h in P.all_sems():
                        E.sem_clear(h)
            self.ps = [st.enter_context(nc.psum_tensor("psb%d" % i, [128, 512], F32)) for i in range(8)]
            A = lambda n, s, dt=F32: st.enter_context(nc.sbuf_tensor(self.uid(n), list(s), dt))
            self.c_cst = A("c_cst", [128, NCST])
            self.c_ident = self.c_cst[:, 0:128]
            self.c_ones = self.c_cst[:, 128:256]
            self.c_identb = A("c_identb", [128, 128], BF16)
            self.c_onesb = A("c_onesb", [128, 128], BF16)
            self.c_trib = A("c_trib", [128, 128], BF16)
            self.c_strb = A("c_strb", [128, 128], BF16)
            self.c_onesr = A("c_onesr", [128, 128])
            self.c_bdc = A("c_bdc", [128, 128], U32)
            self.c_eps = A("c_eps", [128, 2])
            self.modc = A("modc", [128, DEPTH, 48])
            self.sc1p = A("sc1p", [128, DEPTH, 8])
            self.sc2p = A("sc2p", [128, DEPTH, 8])
            self.lbc = A("lbc", [128, DEPTH, 8])
            self.oml = A("oml", [128, DEPTH, 8])
            self.noml = A("noml", [128, DEPTH, 8])
            self.nw = A("nw", [128, DEPTH])
            self.fb = A("fb", [16, DEPTH])
            self.slot = A("slot", [128, NT, 4], I32)
            self.wk = A("wk", [128, NT, 4])
            with nc.Block() as blk:
                self.blk = blk
                self.stage0()
                fns = [self.st_proj, self.st_hgrn, self.st_fox, self.st_merge, self.st_ln1, self.st_experts, self.st_ln2]
                for l in range(self.layers):
                    for f in fns[:self.nst]:
                        f(l)
                P.barrier()
                P.flush(blk)

    def bc_reg(self, E):
        if getattr(self, "_bc", None) is None:
            self._bc = E.to_reg(NE * CAP - 1)
        return self._bc

    def uid(self, n):
        self._uid = getattr(self, "_uid", 0) + 1
        return "%s_u%d" % (n, self._uid)

    def end_stage(self):
        self.P.barrier()
        self.P.flush(self.blk)

    def stage0(self):
        nc, P = self.nc, self.P
        with ExitStack() as st:
            A = lambda n, s, dt=F32: st.enter_context(nc.sbuf_tensor(self.uid(n), list(s), dt))
            self.dma("sp", self.c_cst[:], self.cst, w=["cst"])
            self.dma("sp", self.c_bdc[:], self.cstu, w=["bdc"])
            self.dma("sp", self.nw[:], self.normw, w=["nw"])
            self.dma("sp", self.fb[:], self.ffb, w=["fb"])
            self.cp("dve", self.c_identb[:], self.c_ident, r=["cst"], w=["identb"])
            self.cp("dve", self.c_onesb[:], self.c_ones, r=["cst"], w=["onesb"])
            self.cp("dve", self.c_trib[:], self.c_cst[:, 256:384], r=["cst"], w=["trib"])
            self.cp("dve", self.c_strb[:], self.c_cst[:, 384:512], r=["cst"], w=["strb"])
            self.act(self.c_onesr[:].bitcast(F32R), self.c_ones, AF.Identity, r=["cst"], w=["onesr"])
            self.ms("dve", self.c_eps[:, 0:1], LN_EPS, w=["eps"])
            self.ms("dve", self.c_eps[:, 1:2], RMS_EPS, w=["eps"])
            for i in range(8):
                self.dma("sp" if i % 2 == 0 else "act", self.XS[i * 512:(i + 1) * 512, :], self.x_in[i * 512:(i + 1) * 512, :], w=["XS"])
            lg = A("lg", [128, DEPTH, 8]); ex = A("ex", [128, DEPTH, 8]); mx = A("mx", [128, 8]); sm = A("sm", [128, 8])
            self.dma("sp", lg[:], self.lbl, w=["lg"])
            self.tt("dve", mx[:], lg[:, 0, :], lg[:, 1, :], ALU.max, r=["lg"], w=["mx"])
            for l in (2, 3):
                self.tt("dve", mx[:], mx[:], lg[:, l, :], ALU.max, r=["lg", "mx"], w=["mx"])
            for l in range(DEPTH):
                self.tt("dve", lg[:, l, :], lg[:, l, :], mx[:], ALU.subtract, r=["lg", "mx"], w=["lg"])
            self.act(ex[:], lg[:], AF.Exp, r=["lg"], w=["ex"])
            self.tt("dve", sm[:], ex[:, 0, :], ex[:, 1, :], ALU.add, r=["ex"], w=["sm"])
            for l in (2, 3):
                self.tt("dve", sm[:], sm[:], ex[:, l, :], ALU.add, r=["ex", "sm"], w=["sm"])
            P.op("dve", lambda E: E.reciprocal(out=sm[:], in_=sm[:]), reads=["sm"], writes=["sm"])
            for l in range(DEPTH):
                self.tt("dve", ex[:, l, :], ex[:, l, :], sm[:], ALU.mult, r=["ex", "sm"], w=["ex"])
            self.ms("dve", self.lbc[:, 0, :], 0.0, w=["lbc"])
            self.cp("dve", self.lbc[:, 1, :], ex[:, 1, :], r=["ex"], w=["lbc"])
            for l in (2, 3):
                self.tt("dve", self.lbc[:, l, :], self.lbc[:, l - 1, :], ex[:, l, :], ALU.add, r=["ex", "lbc"], w=["lbc"])
            self.ts("dve", self.lbc[:], self.lbc[:], 0.0, ALU.max, 1.0, ALU.min, r=["lbc"], w=["lbc"])
            self.ts("dve", self.oml[:], self.lbc[:], -1.0, ALU.mult, 1.0, ALU.add, r=["lbc"], w=["oml"])
            self.ts("dve", self.noml[:], self.lbc[:], 1.0, ALU.mult, -1.0, ALU.add, r=["lbc"], w=["noml"])
            ct = A("ct", [128, 8]); ca = A("ca", [128, 8])
            self.dma("sp", ct[:], self.cT, w=["ct"])
            self.act(ca[:], ct[:], AF.Silu, r=["ct"], w=["ca"])
            self.dma("sp", self.modc[:], self.adaB, w=["modc_b"])
            wbuf = [A("adaw%d" % i, [128, 8, 1024]) for i in range(2)]
            mps = self.ps[0]
            n = 0
            for l in range(self.layers):
                for g in range(6):
                    wb = wbuf[n % 2]; key = "adaw%d" % (n % 2)
                    src = self.ada_w[l, :, g * 1024:(g + 1) * 1024].rearrange("(k p) f -> p k f", p=128)
                    self.dma("sp" if n % 2 == 0 else "act", wb[:], src, w=[key])
                    for m in range(8):
                        col = l * 48 + g * 8 + m
                        for k in range(8):
                            self.mm(mps[:, col:col + 1], wb[:, k, m * 128:(m + 1) * 128], ca[:, k:k + 1], k == 0, k == 7, r=[key, "ca"], w=["mps"])
                    n += 1
            L = self.layers
            self.tt("dve", self.modc[:, 0:L, :], self.modc[:, 0:L, :], mps[:, 0:L * 48].rearrange("p (l j) -> p l j", j=48), ALU.add,
                    r=["mps", "modc_b"], w=["modc"])
            self.ts("dve", self.sc1p[:, 0:L, :], self.modc[:, 0:L, 8:16], 1.0, ALU.add, r=["modc"], w=["sc1p"])
            self.ts("dve", self.sc2p[:, 0:L, :], self.modc[:, 0:L, 32:40], 1.0, ALU.add, r=["modc"], w=["sc2p"])
            mt = A("mt", [48, 128])
            for l in range(L):
                tp = self.ps[1]
                self.tr(tp[0:48, 0:128], self.modc[:, l, :], self.c_ident, r=["modc", "cst"], w=["tp"])
                self.act(mt[:], tp[0:48, 0:128], AF.Identity, r=["tp"], w=["mt"])
                self.dma("sp", self.MODT[l], mt[:], r=["mt"], w=["MODT"])
            self.end_stage()

    def st_proj(self, l):
        nc, P = self.nc, self.P
        HALF = S // 2
        NG = HALF // 512
        with ExitStack() as st:
            A = lambda n, s, dt=F32: st.enter_context(nc.sbuf_tensor(self.uid(n), list(s), dt))
            hT = A("hT", [128, 8, HALF])
            xb = [A("xb%d" % i, [128, 4, D]) for i in range(2)]
            wb = [A("wb%d" % i, [128, 8, 512]) for i in range(2)]
            sg = [A("sg%d" % i, [128, HALF]) for i in range(2)]
            so = [A("so%d" % i, [128, 512]) for i in range(2)]
            cnt = dict(w=0, sg=0, so=0, ps=0)

            def nxt(k, n):
                v = cnt[k] % n; cnt[k] += 1
                return v

            def load_w(c0, ncol):
                i = nxt("w", 2)
                src = self.w_in[l, :, c0:c0 + ncol].rearrange("(k p) f -> p k f", p=128)
                self.dma("pool", wb[i][:, :, 0:ncol].bitcast(F32R), src, w=["wb%d" % i])
                return wb[i], "wb%d" % i

            for half in range(2):
                t0 = half * HALF
                for g in range(NG):
                    xi = g % 2
                    src = self.XS[t0 + g * 512:t0 + (g + 1) * 512, :].rearrange("(j p) d -> p j d", p=128)
                    self.dma("sp", xb[xi][:], src, w=["xb%d" % xi])
                    for k in range(8):
                        pi = nxt("ps", 4); pb = self.ps[pi]; pk = "ps%d" % pi
                        for j in range(4):
                            self.tr(pb[:, j * 128:(j + 1) * 128], xb[xi][:, j, k * 128:(k + 1) * 128], self.c_ident, r=["xb%d" % xi], w=[pk])
                        self.act(hT[:, k, g * 512:(g + 1) * 512].bitcast(F32R), pb[:], AF.Identity, r=[pk], w=["hT%d" % g],
                                 scale=self.sc1p[:, l, k:k + 1], bias=self.modc[:, l, k:k + 1])
                fm = [("hq", AF.Silu), ("hf", AF.Sigmoid), ("hg", AF.Silu), ("aq", AF.Identity), ("ak", AF.Identity),
                      ("ga", AF.Sigmoid), ("gb", AF.Sigmoid)]
                for name, fn in fm:
                    for c4 in range(2):
                        w, wkey = load_w(OFF[name] + c4 * 512, 512)
                        for cc in range(4):
                            si = nxt("sg", 2)
                            for g in range(NG):
                                pi = nxt("ps", 4); pb = self.ps[pi]; pk = "ps%d" % pi
                                for k in range(8):
                                    self.mm(pb[:], w[:, k, cc * 128:(cc + 1) * 128].bitcast(F32R), hT[:, k, g * 512:(g + 1) * 512].bitcast(F32R),
                                            k == 0, k == 7, r=[wkey, "hT%d" % g], w=[pk])
                                self.act(sg[si][:, g * 512:(g + 1) * 512], pb[:], fn, r=[pk], w=["sg%d_%d" % (si, g)])
                            r0 = c4 * 512 + cc * 128
                            self.dma("sp", self.PT[name][r0:r0 + 128, t0:t0 + HALF], sg[si][:], r=["sg%d_%d" % (si, g) for g in range(NG)], w=["PT_" + name])
                w, wkey = load_w(OFF["af"], 16)
                si = nxt("sg", 2)
                for g in range(NG):
                    pi = nxt("ps", 4); pb = self.ps[pi]; pk = "ps%d" % pi
                    for k in range(8):
                        self.mm(pb[0:16, :], w[:, k, 0:16].bitcast(F32R), hT[:, k, g * 512:(g + 1) * 512].bitcast(F32R), k == 0, k == 7,
                                r=[wkey, "hT%d" % g], w=[pk])
                    self.act(sg[si][0:16, g * 512:(g + 1) * 512], pb[0:16, :], AF.Identity, r=[pk], w=["sg%d_%d" % (si, g)])
                self.dma("sp", self.PT["af"][:, t0:t0 + HALF], sg[si][0:16, :], r=["sg%d_%d" % (si, g) for g in range(NG)], w=["PT_af"])
                for name in ("hi", "av"):
                    for c4 in range(2):
                        w, wkey = load_w(OFF[name] + c4 * 512, 512)
                        for tt_ in range(HALF // 128):
                            g = tt_ // 4
                            pi = nxt("ps", 4); pb = self.ps[pi]; pk = "ps%d" % pi
                            for k in range(8):
                                self.mm(pb[:], hT[:, k, tt_ * 128:(tt_ + 1) * 128].bitcast(F32R), w[:, k, :].bitcast(F32R), k == 0, k == 7,
                                        r=[wkey, "hT%d" % g], w=[pk])
                            oi = nxt("so", 2)
                            self.cp("dve", so[oi][:], pb[:], r=[pk], w=["so%d" % oi])
                            r0 = t0 + tt_ * 128
                            self.dma("sp", self.PR[name][r0:r0 + 128, c4 * 512:(c4 + 1) * 512], so[oi][:], r=["so%d" % oi], w=["PR_" + name])
            self.end_stage()

    def st_hgrn(self, l):
        nc, P = self.nc, self.P
        NCH = 2
        with ExitStack() as st:
            A = lambda n, s, dt=F32: st.enter_context(nc.sbuf_tensor(self.uid(n), list(s), dt))
            Sst = [A("S%d" % h, [128, 128]) for h in range(HG_H)]
            Sbf = [A("Sb%d" % h, [128, 128], BF16) for h in range(HG_H)]
            ATm = [[A("ATm%d_%d" % (c, i), [128, 128], BF16) for i in range(2)] for c in range(NCH)]
            khT = [[A("khT%d_%d" % (c, i), [128, 128], BF16) for i in range(2)] for c in range(NCH)]
            ptn = ["SQ", "SF", "kk", "gg", "BB", "D1", "D4", "E1", "E2", "E3", "E4"]
            PTs = [{n: A("%s_%d" % (n, i), [128, 512]) for n in ptn} for i in range(2)]
            US = []
            for i in range(2 * NCH):
                d = {n: A("%s_%d" % (n, i), [128, 512], BF16) for n in ("qt", "kt", "qh", "kh")}
                d["dec"] = A("dec_%d" % i, [128, 8])
                for n in ("I", "IA", "IB"):
                    d[n] = A("%s_%d" % (n, i), [128, 4, 128], BF16)
                d["SG"] = A("SG_%d" % i, [128, 512]); d["Ob"] = A("Ob_%d" % i, [128, 512])
                US.append(d)
            EP = [{n: A("%s_%d" % (n, i), [128, 512]) for n in ("sq", "rs", "on")} for i in range(2)]
            for h in range(HG_H):
                self.ms("dve", Sst[h][:], 0.0, w=["S%d" % h])
                self.ms("pool", Sbf[h][:], 0.0, w=["Sb%d" % h])
            for c in range(NCH):
                for i in range(2):
                    self.ms("pool", ATm[c][i][:], 0.0, w=["ATm%d_%d" % (c, i)])
            for i in range(2 * NCH):
                self.ms("pool", US[i]["IA"][:], 0.0, w=["IA_%d" % i])
                self.ms("pool", US[i]["IB"][:], 0.0, w=["IB_%d" % i])
            scanmask = self.c_cst[:, 512:1024]
            NU = (S // 512) * HG_H

            def unit(n):
                bi, h = divmod(n, HG_H)
                return bi, h, slice(bi * 512, (bi + 1) * 512), slice(h * 128, (h + 1) * 128)

            def prologue(n):
                bi, h, cols, rows = unit(n)
                t = PTs[n % 2]; u = US[n % (2 * NCH)]
                tk = lambda nm: "%s_%d" % (nm, n % 2)
                uk = lambda nm: "%s_%d" % (nm, n % (2 * NCH))
                self.dma("sp", t["SQ"][:], self.PT["hq"][rows, cols], w=[tk("SQ")])
                self.dma("sp", t["SF"][:], self.PT["hf"][rows, cols], w=[tk("SF")])
                self.dma("sp", u["SG"][:], self.PT["hg"][rows, cols], w=[uk("SG")])
                isrc = self.PR["hi"][cols, rows].rearrange("(j p) v -> p j v", p=128)
                self.dma("pool", u["I"][:], isrc, w=[uk("I")])
                self.dma("pool", u["IA"][0:64, :, :], isrc[0:64], w=[uk("IA")])
                self.dma("pool", u["IB"][64:128, :, :], isrc[64:128], w=[uk("IB")])
                oml = self.oml[:, l, h:h + 1]; noml = self.noml[:, l, h:h + 1]; lb = self.lbc[:, l, h:h + 1]
                self.ts("dve", t["kk"][:], t["SF"][:], noml, ALU.mult, oml, ALU.add, r=[tk("SF")], w=[tk("kk")])
                self.act(t["gg"][:], t["SF"][:], AF.Ln, r=[tk("SF")], w=[tk("gg")], scale=oml, bias=lb)
                P.op("dve", lambda E, o=t["BB"][:], m=scanmask, g=t["gg"][:]: E.tensor_tensor_scan(out=o, data0=m, data1=g, initial=0.0, op0=ALU.mult, op1=ALU.add),
                     reads=[tk("gg")], writes=[tk("BB")])
                B3 = t["BB"][:].rearrange("p (c t) -> p c t", t=64)
                self.tt("pool", t["D1"][:].rearrange("p (c t) -> p c t", t=64), B3, B3[:, :, 31:32].to_broadcast([128, 8, 64]), ALU.subtract, r=[tk("BB")], w=[tk("D1")])
                self.tt("pool", t["D4"][:].rearrange("p (c t) -> p c t", t=64), B3, B3[:, :, 63:64].to_broadcast([128, 8, 64]), ALU.subtract, r=[tk("BB")], w=[tk("D4")])
                self.act(t["E1"][:], t["D1"][:], AF.Exp, r=[tk("D1")], w=[tk("E1")])
                self.act(t["E2"][:], t["D1"][:], AF.Exp, r=[tk("D1")], w=[tk("E2")], scale=-1.0)
                self.act(t["E3"][:], t["BB"][:], AF.Exp, r=[tk("BB")], w=[tk("E3")])
                self.act(t["E4"][:], t["D4"][:], AF.Exp, r=[tk("D4")], w=[tk("E4")], scale=-1.0)
                self.tt("dve", u["qt"][:], t["SQ"][:], t["E1"][:], ALU.mult, r=[tk("SQ"), tk("E1")], w=[uk("qt")])
                self.tt("pool", u["kt"][:], t["kk"][:], t["E2"][:], ALU.mult, r=[tk("kk"), tk("E2")], w=[uk("kt")])
                self.tt("dve", u["qh"][:], t["SQ"][:], t["E3"][:], ALU.mult, r=[tk("SQ"), tk("E3")], w=[uk("qh")])
                self.tt("pool", u["kh"][:], t["kk"][:], t["E4"][:], ALU.mult, r=[tk("kk"), tk("E4")], w=[uk("kh")])
                self.cp("dve", u["dec"][:], t["E3"][:].rearrange("p (c t) -> p c t", t=64)[:, :, 63], r=[tk("E3")], w=[uk("dec")])

            def tile_steps(n, c, j):
                bi, h, cols, rows = unit(n)
                u = US[n % (2 * NCH)]
                uk = lambda nm: "%s_%d" % (nm, n % (2 * NCH))
                jp = j % 2
                c0 = j * 128
                cb = 3 * c
                at_ps = self.ps[cb][:, 0:128]; atk = "ps%d" % cb
                kh_ps = self.ps[cb + 1][:, 0:64].bitcast(BF16); khk = "ps%d" % (cb + 1)
                su = self.ps[cb + 1][:, 128:256]; suk = khk
                o_ps = self.ps[cb + 2][:, 0:128]; ok = "ps%d" % (cb + 2)
                AT = ATm[c][jp]; ATk = "ATm%d_%d" % (c, jp)
                KT = khT[c][jp]; KTk = "khT%d_%d" % (c, jp)
                Sk, Sbk = "S%d" % h, "Sb%d" % h

                def s1():
                    self.mm(at_ps, u["kt"][:, c0:c0 + 128], u["qt"][:, c0:c0 + 128], True, True, r=[uk("kt"), uk("qt")], w=[atk])
                    self.tr(kh_ps, u["kh"][:, c0:c0 + 128], self.c_identb[:], r=[uk("kh")], w=[khk])

                def s2():
                    P.op("dve", lambda E, o=AT[:], m=self.c_bdc[:], d=at_ps: E.copy_predicated(out=o, mask=m, data=d), reads=[atk], writes=[ATk])
                    self.act(KT[:], kh_ps, AF.Identity, r=[khk], w=[KTk])

                def s3():
                    self.mm(o_ps, u["I"][:, j, :], AT[:], True, False, r=[uk("I"), ATk], w=[ok])
                    self.mm(o_ps[:, 0:64], Sbf[h][:], u["qh"][:, c0:c0 + 64], False, False, r=[Sbk, uk("qh")], w=[ok])
                    self.mm(su, KT[:], u["IA"][:, j, :], True, True, r=[KTk, uk("IA")], w=[suk])

                def s4():
                    self.stt(Sst[h][:], Sst[h][:], u["dec"][:, 2 * j:2 * j + 1], su, ALU.mult, ALU.add, r=[suk, uk("dec"), Sk], w=[Sk])
                    self.act(Sbf[h][:], Sst[h][:], AF.Identity, r=[Sk], w=[Sbk])

                def s5():
                    self.mm(o_ps[:, 64:128], Sbf[h][:], u["qh"][:, c0 + 64:c0 + 128], False, True, r=[Sbk, uk("qh")], w=[ok])
                    self.mm(su, KT[:], u["IB"][:, j, :], True, True, r=[KTk, uk("IB")], w=[suk])

                def s6():
                    self.act(u["Ob"][:, c0:c0 + 128], o_ps, AF.Identity, r=[ok], w=[uk("Ob") + "_%d" % j])
                    self.stt(Sst[h][:], Sst[h][:], u["dec"][:, 2 * j + 1:2 * j + 2], su, ALU.mult, ALU.add, r=[suk, uk("dec"), Sk], w=[Sk])
                    self.act(Sbf[h][:], Sst[h][:], AF.Identity, r=[Sk], w=[Sbk])
                return [s1, s2, s3, s4, s5, s6]

            def epilogue_steps(n, c):
                bi, h, cols, rows = unit(n)
                u = US[n % (2 * NCH)]; e = EP[c % 2]
                uk = lambda nm: "%s_%d" % (nm, n % (2 * NCH))
                ek = lambda nm: "%s_%d" % (nm, c % 2)
                obk = [uk("Ob") + "_%d" % j for j in range(4)]

                def e1():
                    self.act(e["sq"][:].bitcast(F32R), u["Ob"][:], AF.Square, r=obk, w=[ek("sq")])

                def e2():
                    self.mm(self.ps[6 + c % 2][:], self.c_onesr[:].bitcast(F32R), e["sq"][:].bitcast(F32R), True, True, r=[ek("sq")], w=["ps%d" % (6 + c % 2)])

                def e3():
                    self.act(e["rs"][:], self.ps[6 + c % 2][:], AF.Ln, r=["ps%d" % (6 + c % 2)], w=[ek("rs")], scale=1.0 / 128.0, bias=self.c_eps[:, 1:2])
                    self.act(e["rs"][:], e["rs"][:], AF.Exp, r=[ek("rs")], w=[ek("rs")], scale=-0.5)

                def e4():
                    self.tt("dve", e["on"][:], u["Ob"][:], e["rs"][:], ALU.mult, r=obk + [ek("rs")], w=[ek("on")])
                    self.stt(e["on"][:], e["on"][:], self.nw[:, l:l + 1], u["SG"][:], ALU.mult, ALU.mult, r=[ek("on"), uk("SG")], w=[ek("on")])
                    self.dma("sp", self.OAT[rows, cols], e["on"][:], r=[ek("on")], w=["OAT"])
                return [e1, e2, e3, e4]

            for n in range(min(NCH, NU)):
                prologue(n)
            for g0 in range(0, NU, NCH):
                grp = list(range(g0, min(g0 + NCH, NU)))
                for j in range(4):
                    steps = [tile_steps(n, c, j) for c, n in enumerate(grp)]
                    for si in range(6):
                        for stp in steps:
                            stp[si]()
                    nxt = g0 + NCH + j
                    if j < NCH and nxt < NU:
                        prologue(nxt)
                steps = [epilogue_steps(n, c) for c, n in enumerate(grp)]
                for si in range(4):
                    for stp in steps:
                        stp[si]()
            self.end_stage()

    def st_fox(self, l):
        nc, P = self.nc, self.P
        with ExitStack() as st:
            A = lambda n, s, dt=F32: st.enter_context(nc.sbuf_tensor(self.uid(n), list(s), dt))
            ft = A("ft", [16, S]); Fc = A("Fc", [16, S])
            Ftok = A("Ftok", [128, NT, 16]); FrefB = A("FrefB", [128, 16, 8]); rbd = A("rbd", [16, 16, 8])
            vaug = [A("vaug%d" % i, [128, NT, 128], BF16) for i in range(2)]
            QA = [A("QA%d" % i, [128, S], BF16) for i in range(2)]
            QB = [A("QB%d" % i, [128, S], BF16) for i in range(2)]
            KP = [A("KP%d" % i, [128, S], BF16) for i in range(2)]
            e64 = A("e64", [128, 128]); e64r = A("e64r", [128, 128])
            bias = [A("bias%d" % i, [128, NT, 8]) for i in range(2)]
            Pb = [A("Pb%d" % i, [128, 512], BF16) for i in range(4)]
            Osb = [A("Osb%d" % i, [64, 512]) for i in range(2)]
            rc = [A("rc%d" % i, [128, 512]) for i in range(2)]
            zz = A("zz", [128, 512])
            ob = [A("ob%d" % i, [64, 512]) for i in range(2)]
            for i in range(2):
                self.ms("pool", vaug[i][:], 0.0, w=["vaug%d" % i])
                self.ms("pool", vaug[i][:, :, 64:65], 1.0, w=["vaug%d" % i])
                self.ms("pool", QA[i][64:128, :], 0.0, w=["QA%d" % i])
                self.ms("pool", QB[i][0:64, :], 0.0, w=["QB%d" % i])
            self.ms("dve", zz[:], 0.0, w=["zz"])
            self.ms("dve", e64[:], 0.0, w=["e64"])
            self.ms("dve", e64[64:65, :], 1.0, w=["e64"])
            self.act(e64r[:].bitcast(F32R), e64[:], AF.Identity, r=["e64"], w=["e64r"])
            for i in range(2):
                self.act(rc[i][:].bitcast(F32R), zz[:], AF.Identity, r=["zz"], w=["rc%d" % i])
            self.dma("sp", ft[:], self.PT["af"], w=["ft"])
            self.act(ft[:], ft[:], AF.Sigmoid, r=["ft"], w=["ft"], bias=self.fb[:, l:l + 1])
            self.act(ft[:], ft[:], AF.Ln, r=["ft"], w=["ft"])
            P.op("dve", lambda E: E.tensor_tensor_scan(out=Fc[:], data0=self.c_ones[0:16, 0:1].to_broadcast([16, S]), data1=ft[:], initial=0.0, op0=ALU.mult, op1=ALU.add),
                 reads=["ft"], writes=["Fc"])
            for i in range(NT):
                self.tr(self.ps[0][:, i * 16:(i + 1) * 16], Fc[0:16, i * 128:(i + 1) * 128], self.c_ident[0:16, 0:16], r=["Fc"], w=["ps0"])
            self.act(Ftok[:].rearrange("p i h -> p (i h)"), self.ps[0][:], AF.Identity, r=["ps0"], w=["Ftok"])
            fref = Fc[:].rearrange("h (j t) -> h j t", t=512)[:, :, 256]
            self.tt("dve", rbd[:], self.c_ident[0:16, 0:16].unsqueeze(2).to_broadcast([16, 16, 8]), fref.unsqueeze(1).to_broadcast([16, 16, 8]), ALU.mult,
                    r=["Fc"], w=["rbd"])
            self.mm(self.ps[1][:, 0:128], self.c_ones[0:16, :], rbd[:].rearrange("h a j -> h (a j)"), True, True, r=["rbd"], w=["ps1"])
            self.act(FrefB[:].rearrange("p a j -> p (a j)"), self.ps[1][:, 0:128], AF.Identity, r=["ps1"], w=["FrefB"])
            LA = 2
            its = []
            for h in range(FX_H):
                for j in range(S // 512):
                    n_i = 4 * j + 4
                    for i in range(n_i):
                        its.append((h, j, i, n_i))
            hstate = {}

            def head_loads(h):
                par = h % 2; pp = (h // 2) % 2
                rows = slice(h * 64, (h + 1) * 64)
                if h % 2 == 0:
                    self.dma("pool", KP[pp][:], self.PT["ak"][h * 64:(h + 2) * 64, :], w=["KP%d" % pp])
                    self.dma("pool", QA[pp][0:64, :], self.PT["aq"][rows, :], w=["QA%d" % pp])
                    qsrc = QA[pp]; qkey = "QA%d" % pp
                else:
                    self.dma("pool", QB[pp][64:128, :], self.PT["aq"][rows, :], w=["QB%d" % pp])
                    qsrc = QB[pp]; qkey = "QB%d" % pp
                self.dma("pool", vaug[par][:, :, 0:64], self.PR["av"][:, rows].rearrange("(i p) v -> p i v", p=128), w=["vaug%d" % par])
                self.tt("dve", bias[par][:], FrefB[:, h, :].unsqueeze(1).to_broadcast([128, NT, 8]), Ftok[:, :, h:h + 1].to_broadcast([128, NT, 8]), ALU.subtract,
                        r=["FrefB", "Ftok"], w=["bias%d" % par])
                hstate[h] = (par, pp, qsrc, qkey, rows)

            def emit_S(n):
                h, j, i, n_i = its[n]
                if h not in hstate:
                    head_loads(h)
                par, pp, qsrc, qkey, rows = hstate[h]
                cs = max(i - 4 * j, 0) * 128
                si = n % 4
                self.mm(self.ps[si][:, cs:512], KP[pp][:, i * 128:(i + 1) * 128], qsrc[:, j * 512 + cs:(j + 1) * 512], True, True,
                        r=["KP%d" % pp, qkey], w=["ps%d" % si])

            deferred = []

            def run_deferred(force=False):
                for d_ in list(deferred):
                    d_[0] -= 1
                    if d_[0] <= 0 or force:
                        d_[1]()
                        deferred.remove(d_)

            nblk = 0
            for n in range(min(LA, len(its))):
                emit_S(n)
            for n in range(len(its)):
                h, j, i, n_i = its[n]
                par, pp, qsrc, qkey, rows = hstate[h]
                if i == 0 and j == 0 and h + 1 < FX_H and (h + 1) not in hstate:
                    head_loads(h + 1)
                if n + LA < len(its):
                    emit_S(n + LA)
                r_ = i - 4 * j
                cs = max(r_, 0) * 128
                si = n % 4; skey = "ps%d" % si
                if i == 0:
                    oi = nblk % 2; nblk += 1
                o_ps = self.ps[4 + oi]; okey = "ps%d" % (4 + oi)
                pi = n % 4; pkey = "Pb%d" % pi
                pt = Pb[pi][:, cs:512]
                self.act(pt, self.ps[si][:, cs:512], AF.Exp, r=[skey, "bias%d" % par], w=[pkey], scale=0.125, bias=bias[par][:, i, j:j + 1])
                if r_ >= 0:
                    self.tt("pool", Pb[pi][:, cs:cs + 128], Pb[pi][:, cs:cs + 128], self.c_trib[:], ALU.mult, r=[pkey], w=[pkey])
                self.mm(o_ps[:, cs:512], vaug[par][:, i, :], pt, i == 0, i == n_i - 1, r=["vaug%d" % par, pkey], w=[okey])
                run_deferred()
                if i == n_i - 1:
                    self.act(Osb[oi][:], o_ps[0:64, :], AF.Identity, r=[okey], w=["Osb%d" % oi])

                    def _rcp(E, o=rc[oi][64:65, :].bitcast(F32R), i_=o_ps[64:65, :]):
                        with nc.allow_low_precision(reason="fp32r operand for the broadcast matmul"):
                            return E.reciprocal(out=o, in_=i_)
                    P.op("dve", _rcp, reads=[okey], writes=["rc%d" % oi])

                    def part_b(oi=oi, rows=rows, j=j):
                        self.mm(self.ps[6][:], e64r[:].bitcast(F32R), rc[oi][:].bitcast(F32R), True, True, r=["rc%d" % oi, "e64r"], w=["ps6"])
                        self.tt("dve", ob[oi][:], Osb[oi][:], self.ps[6][0:64, :], ALU.mult, r=["Osb%d" % oi, "ps6"], w=["ob%d" % oi])
                        self.dma("sp", self.OBT[rows, j * 512:(j + 1) * 512], ob[oi][:], r=["ob%d" % oi], w=["OBT"])
                    deferred.append([3, part_b])
            run_deferred(force=True)
            self.end_stage()

    def st_merge(self, l):
        nc, P = self.nc, self.P
        with ExitStack() as st:
            A = lambda n, s, dt=F32: st.enter_context(nc.sbuf_tensor(self.uid(n), list(s), dt))
            wbr = A("wbr", [128, 16, D], BF16)
            wo = A("wo", [128, 8, D], BF16)
            inb = []
            for i in range(2):
                inb.append({n: A("%s%d" % (n, i), [128, 8, 512], BF16) for n in ("oa", "ob", "ga", "gb")})
            mg = [A("mg%d" % i, [128, 8, 512], BF16) for i in range(2)]
            t1 = [A("t1_%d" % i, [128, 512]) for i in range(2)]
            t2 = [A("t2_%d" % i, [128, 512]) for i in range(2)]
            ysb = [A("ysb%d" % i, [128, D]) for i in range(2)]
            for k2 in range(4):
                self.dma("pool", wbr[:, k2 * 4:(k2 + 1) * 4, :], self.w_branch[l, k2 * 512:(k2 + 1) * 512, :].rearrange("(k p) d -> p k d", p=128), w=["wbr"])
            for k2 in range(2):
                self.dma("pool", wo[:, k2 * 4:(k2 + 1) * 4, :], self.w_out[l, k2 * 512:(k2 + 1) * 512, :].rearrange("(k p) d -> p k d", p=128), w=["wo"])
            nt_ = 0; ny = 0
            for bi in range(S // 512):
                par = bi % 2
                cols = slice(bi * 512, (bi + 1) * 512)
                ib = inb[par]
                for nm, src in (("oa", self.OAT), ("ob", self.OBT), ("ga", self.PT["ga"]), ("gb", self.PT["gb"])):
                    self.dma("pool", ib[nm][:], src[:, cols].rearrange("(k p) t -> p k t", p=128), w=["%s%d" % (nm, par)])
                for m in range(8):
                    pa = self.ps[(2 * m) % 4]; pak = "ps%d" % ((2 * m) % 4)
                    pb_ = self.ps[(2 * m + 1) % 4]; pbk = "ps%d" % ((2 * m + 1) % 4)
                    for k in range(8):
                        self.mm(pa[:], wbr[:, k, m * 128:(m + 1) * 128], ib["oa"][:, k, :], k == 0, k == 7, r=["wbr", "oa%d" % par], w=[pak])
                    for k in range(8):
                        self.mm(pb_[:], wbr[:, 8 + k, m * 128:(m + 1) * 128], ib["ob"][:, k, :], k == 0, k == 7, r=["wbr", "ob%d" % par], w=[pbk])
                    ti = nt_ % 2; nt_ += 1
                    self.tt("dve", t1[ti][:], pa[:], ib["ga"][:, m, :], ALU.mult, r=[pak, "ga%d" % par], w=["t1_%d" % ti])
                    self.tt("dve", t2[ti][:], pb_[:], ib["gb"][:, m, :], ALU.mult, r=[pbk, "gb%d" % par], w=["t2_%d" % ti])
                    self.tt("pool", mg[par][:, m, :], t1[ti][:], t2[ti][:], ALU.add, r=["t1_%d" % ti, "t2_%d" % ti], w=["mg%d_%d" % (par, m)])
                for r_ in range(4):
                    yi = ny % 2; ny += 1
                    for hh in range(2):
                        yp = self.ps[4 + 2 * yi + hh]; ypk = "ps%d" % (4 + 2 * yi + hh)
                        for k in range(8):
                            self.mm(yp[:], mg[par][:, k, r_ * 128:(r_ + 1) * 128], wo[:, k, hh * 512:(hh + 1) * 512], k == 0, k == 7,
                                    r=["mg%d_%d" % (par, k), "wo"], w=[ypk])
                        self.act(ysb[yi][:, hh * 512:(hh + 1) * 512], yp[:], AF.Identity, r=[ypk], w=["ysb%d_%d" % (yi, hh)])
                    r0 = bi * 512 + r_ * 128
                    self.dma("sp", self.YS[r0:r0 + 128, :], ysb[yi][:], r=["ysb%d_0" % yi, "ysb%d_1" % yi], w=["YS"])
            self.end_stage()

    def layer_norm(self, z, zk, xh, xhk, tmp, mvk):
        P = self.P
        stt_, mv, sd = tmp["st"], tmp["mv"], tmp["sd"]
        for c in range(2):
            P.op("dve", lambda E, o=stt_[:, c * 6:(c + 1) * 6], i=z[:, c * 512:(c + 1) * 512]: E.bn_stats(out=o, in_=i), reads=[zk], writes=[mvk + "st%d" % c])
        P.op("dve", lambda E, o=mv[:], i=stt_[:]: E.bn_aggr(out=o, in_=i), reads=[mvk + "st0", mvk + "st1"], writes=[mvk + "mv"])
        self.act(sd[:, 0:1], mv[:, 1:2], AF.Sqrt, r=[mvk + "mv"], w=[mvk + "sd"], bias=self.c_eps[:, 0:1])
        P.op("dve", lambda E, o=sd[:, 1:2], i=sd[:, 0:1]: E.reciprocal(out=o, in_=i), reads=[mvk + "sd"], writes=[mvk + "rstd"])
        self.ts("dve", sd[:, 2:3], mv[:, 0:1], sd[:, 1:2], ALU.mult, -1.0, ALU.mult, r=[mvk + "mv", mvk + "rstd"], w=[mvk + "nmr"])
        self.act(xh, z[:], AF.Identity, r=[zk, mvk + "rstd", mvk + "nmr"], w=[xhk], scale=sd[:, 1:2], bias=sd[:, 2:3])

    def load_row(self, q, dst, src_row, key):
        self.dma(q, dst, src_row.partition_broadcast(128)[:, 0, :], w=[key])

    def st_ln1(self, l):
        nc, P = self.nc, self.P
        with ExitStack() as st:
            A = lambda n, s, dt=F32: st.enter_context(nc.sbuf_tensor(self.uid(n), list(s), dt))
            g1p = A("g1p", [128, D]); lg_ = A("lg_", [128, D]); lb_ = A("lb_", [128, D]); s2p = A("s2p", [128, D]); sh2 = A("sh2", [128, D])
            wr = A("wr", [128, 8, NE]); br = A("br", [128, NE])
            csel = A("csel", [128, NE], BF16)
            modrow = lambda j0: self.MODT[l:l + 1, j0:j0 + 8, :].rearrange("a j p -> a (j p)")
            self.load_row("sp", g1p[:], modrow(16), "g1p")
            self.load_row("sp", sh2[:], modrow(24), "sh2")
            self.load_row("sp", s2p[:], modrow(32), "s2p")
            self.load_row("sp", lg_[:], self.ln1_g[l:l + 1, :], "lg_")
            self.load_row("sp", lb_[:], self.ln1_b[l:l + 1, :], "lb_")
            self.load_row("sp", br[:], self.b_router[l:l + 1, :], "br")
            self.dma("sp", wr[:], self.w_router[l].rearrange("(k p) e -> p k e", p=128), w=["wr"])
            self.ts("dve", g1p[:], g1p[:], 1.0, ALU.add, r=["g1p"], w=["g1p"])
            self.ts("dve", s2p[:], s2p[:], 1.0, ALU.add, r=["s2p"], w=["s2p"])
            self.ms("dve", csel[:], 0.0, w=["csel"])
            iotaC = self.c_cst[:, 1024:1056]
            T = []
            for i in range(2):
                d = {n: A("%s%d" % (n, i), [128, D]) for n in ("x", "y", "z", "xh", "x1", "h2")}
                d["h2T"] = A("h2T%d" % i, [128, 8, 128])
                d["st"] = A("st%d" % i, [128, 12]); d["mv"] = A("mv%d" % i, [128, 2]); d["sd"] = A("sd%d" % i, [128, 3])
                d["lgt"] = A("lgt%d" % i, [128, NE]); d["top"] = A("top%d" % i, [128, 8]); d["sel"] = A("sel%d" % i, [128, NE], BF16)
                d["SL"] = A("SL%d" % i, [128, NE]); d["oh"] = A("oh%d" % i, [128, NE]); d["sf"] = A("sf%d" % i, [128, 4])
                d["e4"] = A("e4%d" % i, [128, 4]); d["nt"] = A("nt%d" % i, [128, 2]); d["ovf"] = A("ovf%d" % i, [128, NE])
                T.append(d)
            for t in range(NT):
                i = t % 2; d = T[i]
                kx = lambda nm: "%s%d" % (nm, i)
                rows = slice(t * 128, (t + 1) * 128)
                self.dma("sp", d["x"][:], self.XS[rows, :], w=[kx("x")])
                self.dma("act", d["y"][:], self.YS[rows, :], w=[kx("y")])
                self.tt("dve", d["z"][:], d["y"][:], g1p[:], ALU.mult, r=[kx("y"), "g1p"], w=[kx("z")])
                self.stt(d["z"][:], d["x"][:], ALPHA, d["z"][:], ALU.mult, ALU.add, r=[kx("x"), kx("z")], w=[kx("z")])
                self.layer_norm(d["z"], kx("z"), d["xh"][:], kx("xh"), d, kx("m"))
                self.tt("pool", d["x1"][:], d["xh"][:], lg_[:], ALU.mult, r=[kx("xh"), "lg_"], w=[kx("x1")])
                self.tt("pool", d["x1"][:], d["x1"][:], lb_[:], ALU.add, r=[kx("x1"), "lb_"], w=[kx("x1")])
                self.dma("sp", self.XS[rows, :], d["x1"][:], r=[kx("x1")], w=["XSw"])
                self.tt("dve", d["h2"][:], d["x1"][:], s2p[:], ALU.mult, r=[kx("x1"), "s2p"], w=[kx("h2")])
                self.tt("pool", d["h2"][:], d["h2"][:], sh2[:], ALU.add, r=[kx("h2"), "sh2"], w=[kx("h2")])
                if self.nst < 6:
                    continue
                for hh in range(2):
                    pb = self.ps[hh]; pk = "ps%d" % hh
                    for k4 in range(4):
                        k = hh * 4 + k4
                        self.tr(pb[:, k4 * 128:(k4 + 1) * 128], d["h2"][:, k * 128:(k + 1) * 128], self.c_ident, r=[kx("h2")], w=[pk])
                    self.act(d["h2T"][:, hh * 4:(hh + 1) * 4, :].rearrange("p k t -> p (k t)"), pb[:], AF.Identity, r=[pk], w=[kx("h2T") + "_%d" % hh])
                lps = self.ps[2][:, 0:NE]
                for k in range(8):
                    self.mm(lps, d["h2T"][:, k, :], wr[:, k, :], k == 0, k == 7, r=[kx("h2T") + "_%d" % (k // 4), "wr"], w=["ps2"])
                self.tt("dve", d["lgt"][:], lps, br[:], ALU.add, r=["ps2", "br"], w=[kx("lgt")])
                P.op("dve", lambda E, o=d["top"][:], i_=d["lgt"][:]: E.max(out=o, in_=i_), reads=[kx("lgt")], writes=[kx("top")])
                self.ts("dve", d["sel"][:], d["lgt"][:], d["top"][:, 3:4], ALU.is_ge, r=[kx("lgt"), kx("top")], w=[kx("sel")])
                pps = self.ps[3][:, 0:NE]
                self.mm(pps, self.c_strb[:], d["sel"][:], True, False, r=[kx("sel")], w=["ps3"])
                self.mm(pps, self.c_onesb[:], csel[:], False, True, r=["csel"], w=["ps3"])
                self.tt("pool", csel[:], csel[:], d["sel"][:], ALU.add, r=["csel", kx("sel")], w=["csel"])
                self.ts("dve", d["ovf"][:], pps, float(CAP), ALU.is_ge, 1.0e7, ALU.mult, r=["ps3"], w=[kx("ovf")])
                self.tt("dve", d["SL"][:], pps, iotaC, ALU.add, r=["ps3"], w=[kx("SL")])
                self.tt("dve", d["SL"][:], d["SL"][:], d["ovf"][:], ALU.add, r=[kx("SL"), kx("ovf")], w=[kx("SL")])
                for k in range(4):
                    self.ts("dve", d["oh"][:], d["lgt"][:], d["top"][:, k:k + 1], ALU.is_equal, r=[kx("lgt"), kx("top")], w=[kx("oh")])
                    self.tt("dve", d["oh"][:], d["oh"][:], d["SL"][:], ALU.mult, r=[kx("oh"), kx("SL")], w=[kx("oh")])
                    P.op("dve", lambda E, o=d["sf"][:, k:k + 1], i_=d["oh"][:]: E.tensor_reduce(out=o, in_=i_, axis=mybir.AxisListType.X, op=ALU.add),
                         reads=[kx("oh")], writes=[kx("sf")])
                self.cp("dve", self.slot[:, t, :], d["sf"][:], r=[kx("sf")], w=["slot%d" % t])
                self.ts("dve", d["nt"][:, 0:1], d["top"][:, 0:1], -1.0, ALU.mult, r=[kx("top")], w=[kx("nt")])
                self.act(d["e4"][:], d["top"][:, 0:4], AF.Exp, r=[kx("top"), kx("nt")], w=[kx("e4")], bias=d["nt"][:, 0:1])
                P.op("dve", lambda E, o=d["nt"][:, 1:2], i_=d["e4"][:]: E.tensor_reduce(out=o, in_=i_, axis=mybir.AxisListType.X, op=ALU.add),
                     reads=[kx("e4")], writes=[kx("nt") + "s"])
                P.op("dve", lambda E, o=d["nt"][:, 1:2]: E.reciprocal(out=o, in_=o), reads=[kx("nt") + "s"], writes=[kx("nt") + "s"])
                self.ts("dve", self.wk[:, t, :], d["e4"][:], d["nt"][:, 1:2], ALU.mult, r=[kx("e4"), kx("nt") + "s"], w=["wk%d" % t])
                for k in range(4):
                    P.dma("pool", lambda E, o=self.XG, ix=self.slot[:, t, k:k + 1], i_=d["h2"][:]: E.indirect_dma_start(
                        out=o, out_offset=bass.IndirectOffsetOnAxis(ap=ix, axis=0), in_=i_, in_offset=None,
                        bounds_check=self.bc_reg(E), oob_is_err=False), reads=[kx("h2"), "slot%d" % t], writes=["XG"])
            self.end_stage()

    def st_experts(self, l):
        nc, P = self.nc, self.P
        NPB = CAP // 512
        with ExitStack() as st:
            A = lambda n, s, dt=F32: st.enter_context(nc.sbuf_tensor(self.uid(n), list(s), dt))
            W = [A("W%d" % i, [128, 8, D], BF16) for i in range(6)]
            bcol = [A("bcol%d" % i, [128, 16]) for i in range(2)]
            bd = [A("bd%d" % i, [128, D]) for i in range(2)]
            X = [A("X%d" % i, [128, 4, D], BF16) for i in range(2)]
            XT = [A("XT%d" % i, [128, 8, 512], BF16) for i in range(2)]
            AT = [A("AT%d" % i, [128, 8, 512], BF16) for i in range(2)]
            tm = [{n: A("%s%d" % (n, i), [128, 512]) for n in ("gsb", "ssb", "u1")} for i in range(2)]
            Ysb = [A("Ysb%d" % i, [128, D]) for i in range(2)]

            def load_w(e):
                s0 = (e % 2) * 3
                for i, src in enumerate((self.w_gate, self.w_up, self.w_down)):
                    for k2 in range(2):
                        self.dma("pool", W[s0 + i][:, k2 * 4:(k2 + 1) * 4, :], src[l, e, k2 * 512:(k2 + 1) * 512, :].rearrange("(k p) f -> p k f", p=128),
                                 w=["W%d_%d" % (s0 + i, k2)])
                self.dma("sp", bcol[e % 2][:], self.bgu[l, e], w=["bcol%d" % (e % 2)])
                self.load_row("sp", bd[e % 2][:], self.b_down[l, e:e + 1, :], "bd%d" % (e % 2))

            load_w(0)
            npb = 0; nm_ = 0; ny = 0
            for e in range(NE):
                if e + 1 < NE:
                    load_w(e + 1)
                s0 = (e % 2) * 3
                Wg, Wu, Wd = W[s0], W[s0 + 1], W[s0 + 2]
                wkeys = lambda i: ["W%d_0" % (s0 + i), "W%d_1" % (s0 + i)]
                bc = bcol[e % 2]; bck = "bcol%d" % (e % 2)
                for pb in range(NPB):
                    xi = npb % 2; npb += 1
                    r0 = e * CAP + pb * 512
                    self.dma("pool", X[xi][:], self.XG[r0:r0 + 512, :].rearrange("(j p) d -> p j d", p=128), w=["X%d" % xi])
                    for k in range(8):
                        tp = self.ps[k % 2][:, 0:256].bitcast(BF16); tpk = "ps%d" % (k % 2)
                        for j in range(4):
                            self.tr(tp[:, j * 128:(j + 1) * 128], X[xi][:, j, k * 128:(k + 1) * 128], self.c_identb[:], r=["X%d" % xi], w=[tpk])
                        if k % 2 == 0:
                            self.act(XT[xi][:, k, :], tp, AF.Identity, r=[tpk], w=["XT%d_%d" % (xi, k)])
                        else:
                            self.cp("dve", XT[xi][:, k, :], tp, r=[tpk], w=["XT%d_%d" % (xi, k)])
                    xtk = ["XT%d_%d" % (xi, k) for k in range(8)]
                    def tail(m, ti):
                        t_ = tm[ti]
                        self.tt("pool", t_["ssb"][:], t_["gsb"][:], t_["ssb"][:], ALU.mult, r=["gsb%d" % ti, "ssb%d" % ti], w=["ssb%d" % ti])
                        self.stt(AT[xi][:, m, :], t_["u1"][:], 1.0, t_["ssb"][:], ALU.add, ALU.mult, r=["u1%d" % ti, "ssb%d" % ti], w=["AT%d_%d" % (xi, m)])
                    for m in range(8):
                        ti = nm_ % 2; nm_ += 1
                        t_ = tm[ti]
                        gp = self.ps[2 + ti]; gpk = "ps%d" % (2 + ti)
                        up = self.ps[4 + ti]; upk = "ps%d" % (4 + ti)
                        for k in range(8):
                            self.mm(gp[:], Wg[:, k, m * 128:(m + 1) * 128], XT[xi][:, k, :], k == 0, k == 7, r=wkeys(0) + [xtk[k]], w=[gpk])
                        for k in range(8):
                            self.mm(up[:], Wu[:, k, m * 128:(m + 1) * 128], XT[xi][:, k, :], k == 0, k == 7, r=wkeys(1) + [xtk[k]], w=[upk])
                        self.ts("dve", t_["gsb"][:], gp[:], bc[:, m:m + 1], ALU.add, 7.0, ALU.min, r=[gpk, bck], w=["gsb%d" % ti])
                        self.act(t_["u1"][:], up[:], AF.Identity, r=[upk, bck], w=["u1%d" % ti], bias=bc[:, 8 + m:9 + m])
                        self.act(t_["ssb"][:], t_["gsb"][:], AF.Sigmoid, r=["gsb%d" % ti], w=["ssb%d" % ti], scale=1.702)
                        self.ts("dve", t_["u1"][:], t_["u1"][:], 7.0, ALU.min, -7.0, ALU.max, r=["u1%d" % ti], w=["u1%d" % ti])
                        if m > 0:
                            tail(m - 1, 1 - ti)
                    tail(7, ti)
                    atk = ["AT%d_%d" % (xi, k) for k in range(8)]
                    for j in range(4):
                        yi = ny % 2; ny += 1
                        for hh in range(2):
                            yp = self.ps[6 + hh]; ypk = "ps%d" % (6 + hh)
                            for k in range(8):
                                self.mm(yp[:], AT[xi][:, k, j * 128:(j + 1) * 128], Wd[:, k, hh * 512:(hh + 1) * 512], k == 0, k == 7, r=wkeys(2) + [atk[k]], w=[ypk])
                            self.tt("dve", Ysb[yi][:, hh * 512:(hh + 1) * 512], yp[:], bd[e % 2][:, hh * 512:(hh + 1) * 512], ALU.add,
                                    r=[ypk, "bd%d" % (e % 2)], w=["Ysb%d_%d" % (yi, hh)])
                        self.dma("sp", self.YG[r0 + j * 128:r0 + (j + 1) * 128, :], Ysb[yi][:], r=["Ysb%d_0" % yi, "Ysb%d_1" % yi], w=["YG"])
            self.end_stage()

    def st_ln2(self, l):
        nc, P = self.nc, self.P
        last = (l == self.layers - 1)
        with ExitStack() as st:
            A = lambda n, s, dt=F32: st.enter_context(nc.sbuf_tensor(self.uid(n), list(s), dt))
            g2p = A("g2p", [128, D]); lg_ = A("lg2", [128, D]); lb_ = A("lb2", [128, D])
            self.load_row("sp", g2p[:], self.MODT[l:l + 1, 40:48, :].rearrange("a j p -> a (j p)"), "g2p")
            self.load_row("sp", lg_[:], self.ln2_g[l:l + 1, :], "lg2")
            self.load_row("sp", lb_[:], self.ln2_b[l:l + 1, :], "lb2")
            self.ts("dve", g2p[:], g2p[:], 1.0, ALU.add, r=["g2p"], w=["g2p"])
            T = []
            for i in range(2):
                d = {n: A("%s%d" % (n, i), [128, D]) for n in ("x", "Y0", "Y1", "Y2", "Y3", "z", "xh")}
                d["st"] = A("st%d" % i, [128, 12]); d["mv"] = A("mv%d" % i, [128, 2]); d["sd"] = A("sd%d" % i, [128, 3])
                T.append(d)
            for t in range(NT):
                i = t % 2; d = T[i]
                kx = lambda nm: "%s%d" % (nm, i)
                rows = slice(t * 128, (t + 1) * 128)
                self.dma("sp", d["x"][:], self.XS[rows, :], w=[kx("x")])
                for k in range(4):
                    P.dma("pool", lambda E, o=d["Y%d" % k][:], ix=self.slot[:, t, k:k + 1], i_=self.YG: E.indirect_dma_start(
                        out=o, out_offset=None, in_=i_, in_offset=bass.IndirectOffsetOnAxis(ap=ix, axis=0),
                        bounds_check=self.bc_reg(E), oob_is_err=False), reads=[], writes=[kx("Y%d" % k)])
                self.ts("dve", d["z"][:], d["Y0"][:], self.wk[:, t, 0:1], ALU.mult, r=[kx("Y0")], w=[kx("z")])
                for k in range(1, 4):
                    self.stt(d["z"][:], d["Y%d" % k][:], self.wk[:, t, k:k + 1], d["z"][:], ALU.mult, ALU.add, r=[kx("Y%d" % k), kx("z")], w=[kx("z")])
                self.tt("pool", d["z"][:], d["z"][:], g2p[:], ALU.mult, r=[kx("z"), "g2p"], w=[kx("z")])
                self.stt(d["z"][:], d["x"][:], ALPHA, d["z"][:], ALU.mult, ALU.add, r=[kx("x"), kx("z")], w=[kx("z")])
                self.layer_norm(d["z"], kx("z"), d["xh"][:], kx("xh"), d, kx("m"))
                self.tt("pool", d["xh"][:], d["xh"][:], lg_[:], ALU.mult, r=[kx("xh"), "lg2"], w=[kx("xh")])
                self.tt("pool", d["xh"][:], d["xh"][:], lb_[:], ALU.add, r=[kx("xh"), "lb2"], w=[kx("xh")])
                dst = self.out if (last and self.upto == "all" and self.layers == DEPTH) else self.XS
                self.dma("sp", dst[rows, :], d["xh"][:], r=[kx("xh")], w=["XSw"])
            self.end_stage()


def make_consts():
    c = np.zeros((128, NCST), np.float32)
    c[:, 0:128] = np.eye(128, dtype=np.float32)
    c[:, 128:256] = 1.0
    s = np.arange(128)[:, None]; t = np.arange(128)[None, :]
    c[:, 256:384] = (s <= t)
    c[:, 384:512] = (s < t)
    m = np.ones(512, np.float32); m[0::64] = 0.0
    c[:, 512:1024] = m[None, :]
    c[:, 1024:1056] = (np.arange(NE, dtype=np.float32) * CAP)[None, :]
    u = ((s <= t) & ((s // 64) == (t // 64))).astype(np.uint32)
    return c, u


def host_inputs(inp, b, layers=DEPTH, experts=True):
    f = lambda a: np.ascontiguousarray(a, dtype=np.float32)
    c, u = make_consts()
    L = layers
    m = {
        "x": f(inp["x"][b]),
        "cT": f(inp["c"][b].reshape(8, 128).T),
        "w_in": inp["w_in"][:L],
        "ffb": f(inp["fox_f_bias"].T),
        "lbl": f(inp["hg_lb_logits"].reshape(DEPTH, 8, 128).transpose(2, 0, 1)),
        "normw": f(inp["hg_norm_w"].T),
        "w_branch": inp["w_branch"][:L], "w_out": inp["w_out"][:L], "ada_w": inp["ada_w"][:L],
        "adaB": f(inp["ada_b"].reshape(DEPTH, 48, 128).transpose(2, 0, 1)),
        "ln1_g": f(inp["ln1_g"]), "ln1_b": f(inp["ln1_b"]), "ln2_g": f(inp["ln2_g"]), "ln2_b": f(inp["ln2_b"]),
        "w_router": f(inp["w_router"]), "b_router": f(inp["b_router"]),
        "bgu": f(np.concatenate([inp["b_gate"].reshape(DEPTH, NE, 8, 128).transpose(0, 1, 3, 2),
                                 inp["b_up"].reshape(DEPTH, NE, 8, 128).transpose(0, 1, 3, 2)], axis=3)),
        "b_down": f(inp["b_down"]),
        "cst": c, "cstu": u,
    }
    if experts:
        m["w_gate"] = inp["w_gate"][:L]; m["w_up"] = inp["w_up"][:L]; m["w_down"] = inp["w_down"][:L]
    return m


_CACHE = {}


def kernel(**inputs):
    inp = {k: np.asarray(v) for k, v in inputs.items()}
    if "nc" not in _CACHE:
        _CACHE["nc"] = K().nc
    nc = _CACHE["nc"]
    shared = host_inputs(inp, 0)
    in_maps = []
    for b in range(8):
        m = dict(shared)
        m["x"] = np.ascontiguousarray(inp["x"][b], dtype=np.float32)
        m["cT"] = np.ascontiguousarray(inp["c"][b].reshape(8, 128).T, dtype=np.float32)
        in_maps.append(m)
    res = run_bass_kernel_spmd(nc, in_maps, core_ids=list(range(8)))
    return np.stack([r["out"] for r in res.results], axis=0).astype(np.float32)
```

```python
from contextlib import ExitStack
import numpy as np
import concourse.bass as bass
import concourse.mybir as mybir
from concourse.bass_utils import run_bass_kernel_spmd

F32 = mybir.dt.float32
F32R = mybir.dt.float32r
BF16 = mybir.dt.bfloat16
I32 = mybir.dt.int32
U32 = mybir.dt.uint32
AF = mybir.ActivationFunctionType
ALU = mybir.AluOpType

S = 4096
D = 1024
DEPTH = 4
NT = S // 128
HG_H = 8
FX_H = 16
NE = 32
PIN = 9232
ALPHA = float((2 * DEPTH) ** 0.25)
LN_EPS = 1e-5
RMS_EPS = 1e-6
CT = 7
CAP = CT * 128
NCST = 1056
OFF = dict(hq=0, hf=1024, hi=2048, hg=3072, aq=4096, ak=5120, av=6144, af=7168, ga=7184, gb=8208)

ENGS = ("pe", "act", "dve", "pool", "sp")
NDSEM = 8


class Prog:
    def __init__(self, nc, stack):
        self.nc = nc
        self.sem = {n: stack.enter_context(nc.semaphore("s_" + n)) for n in ENGS}
        self.dsem = {}
        for q in ("sp", "act", "pool"):
            for i in range(NDSEM):
                self.dsem[(q, i)] = stack.enter_context(nc.semaphore("d_%s%d" % (q, i)))
        self.cnt = {k: 0 for k in list(self.sem) + list(self.dsem)}
        self.seen = {n: {} for n in ENGS}
        self.last_w = {}
        self.readers = {}
        self.q = {n: [] for n in ENGS}
        self.dma_i = {"sp": 0, "act": 0, "pool": 0}
        self.dma_pending = {}
        self.n_inst = 0

    def all_sems(self):
        return list(self.sem.values()) + list(self.dsem.values())

    def _deps(self, reads, writes):
        deps = []
        for k in reads:
            if k in self.last_w:
                deps.append(self.last_w[k])
        for k in writes:
            if k in self.last_w:
                deps.append(self.last_w[k])
            deps.extend(self.readers.get(k, ()))
        return deps

    def _commit(self, stamp, reads, writes):
        for k in reads:
            self.readers.setdefault(k, []).append(stamp)
        for k in writes:
            self.last_w[k] = stamp
            self.readers[k] = []

    def _waits(self, eng, deps):
        need = {}
        for (sk, v) in deps:
            if sk == "pe" and eng == "pe":
                continue
            if self.seen[eng].get(sk, 0) >= v:
                continue
            if need.get(sk, 0) < v:
                need[sk] = v
        for sk, v in need.items():
            self.seen[eng][sk] = v
        return list(need.items())

    def _semh(self, sk):
        return self.sem[sk] if isinstance(sk, str) else self.dsem[sk]

    def op(self, eng, fn, reads=(), writes=()):
        deps = self._deps(reads, writes)
        waits = self._waits(eng, deps)
        self.cnt[eng] += 1
        stamp = (eng, self.cnt[eng])
        self._commit(stamp, reads, writes)
        semh = self.sem[eng]
        wl = [(self._semh(sk), v) for sk, v in waits]

        def run(E, fn=fn, wl=wl, semh=semh):
            for h, v in wl:
                E.wait_ge(h, v)
            fn(E).then_inc(semh, 1)
        self.q[eng].append(run)
        self.n_inst += 1
        return stamp

    def dma(self, q, fn, reads=(), writes=()):
        i = self.dma_i[q] % NDSEM
        self.dma_i[q] += 1
        sk = (q, i)
        deps = self._deps(reads, writes)
        if sk in self.dma_pending:
            deps.append(self.dma_pending[sk])
        waits = self._waits(q, deps)
        self.cnt[sk] += 16
        stamp = (sk, self.cnt[sk])
        self.dma_pending[sk] = stamp
        self._commit(stamp, reads, writes)
        semh = self.dsem[sk]
        wl = [(self._semh(s), v) for s, v in waits]

        def run(E, fn=fn, wl=wl, semh=semh):
            for h, v in wl:
                E.wait_ge(h, v)
            fn(E).then_inc(semh, 16)
        self.q[q].append(run)
        self.n_inst += 1
        return stamp

    def barrier(self):
        for eng in ENGS:
            wl = []
            for sk, v in self.cnt.items():
                if v == 0 or sk == eng:
                    continue
                if self.seen[eng].get(sk, 0) >= v:
                    continue
                self.seen[eng][sk] = v
                wl.append((self._semh(sk), v))
            if wl:
                def run(E, wl=wl):
                    for h, v in wl:
                        E.wait_ge(h, v)
                self.q[eng].append(run)
        self.last_w.clear()
        self.readers.clear()

    def flush(self, block):
        for eng, dec in (("sp", block.sync), ("pe", block.tensor), ("act", block.scalar),
                         ("dve", block.vector), ("pool", block.gpsimd)):
            lst = self.q[eng]
            if not lst:
                continue

            def body(E, lst=lst):
                for r in lst:
                    r(E)
            dec(body)
            self.q[eng] = []


STAGES = ["proj", "hgrn", "fox", "merge", "ln1", "experts", "ln2"]


class K:
    def __init__(self, layers=DEPTH, upto="all", dbg=()):
        self.layers = L = layers
        self.upto = upto
        self.nst = len(STAGES) if upto == "all" else STAGES.index(upto) + 1
        nc = self.nc = bass.Bass("TRN2", target_bir_lowering=False)
        IN = lambda n, s, dt=F32: nc.dram_tensor(n, list(s), dt, kind="ExternalInput").ap()
        SC = lambda n, s, dt=F32: nc.dram_tensor(n, list(s), dt, kind=("ExternalOutput" if n in dbg else "Internal")).ap()
        self.x_in = IN("x", [S, D])
        self.cT = IN("cT", [128, 8])
        self.w_in = IN("w_in", [L, D, PIN])
        self.ffb = IN("ffb", [16, DEPTH])
        self.lbl = IN("lbl", [128, DEPTH, 8])
        self.normw = IN("normw", [128, DEPTH])
        self.w_branch = IN("w_branch", [L, 2 * D, D])
        self.w_out = IN("w_out", [L, D, D])
        self.ada_w = IN("ada_w", [L, D, 6 * D])
        self.adaB = IN("adaB", [128, DEPTH, 48])
        self.ln1_g = IN("ln1_g", [DEPTH, D]); self.ln1_b = IN("ln1_b", [DEPTH, D])
        self.ln2_g = IN("ln2_g", [DEPTH, D]); self.ln2_b = IN("ln2_b", [DEPTH, D])
        self.w_router = IN("w_router", [DEPTH, D, NE])
        self.b_router = IN("b_router", [DEPTH, NE])
        if self.nst >= 6:
            self.w_gate = IN("w_gate", [L, NE, D, D])
            self.w_up = IN("w_up", [L, NE, D, D])
            self.w_down = IN("w_down", [L, NE, D, D])
        self.bgu = IN("bgu", [DEPTH, NE, 128, 16])
        self.b_down = IN("b_down", [DEPTH, NE, D])
        self.cst = IN("cst", [128, NCST])
        self.cstu = IN("cstu", [128, 128], U32)
        self.out = nc.dram_tensor("out", [S, D], F32, kind="ExternalOutput").ap()
        self.XS = SC("XS", [S, D])
        self.MODT = SC("MODT", [DEPTH, 48, 128])
        self.PT = {n: SC("PT_" + n, [D, S]) for n in ("hq", "hf", "hg", "aq", "ak", "ga", "gb")}
        self.PT["af"] = SC("PT_af", [16, S])
        self.PR = {n: SC("PR_" + n, [S, D]) for n in ("hi", "av")}
        self.OAT = SC("OAT", [D, S])
        self.OBT = SC("OBT", [D, S])
        self.YS = SC("YS", [S, D])
        self.XG = SC("XG", [NE * CAP, D])
        self.YG = SC("YG", [NE * CAP, D])
        self.build()

    def dma(self, q, out, in_, r=(), w=(), **kw):
        self.P.dma(q, lambda E, o=out, i=in_, kw=kw: E.dma_start(out=o, in_=i, **kw), reads=r, writes=w)

    def act(self, out, in_, func, r=(), w=(), **kw):
        self.P.op("act", lambda E, o=out, i=in_, f=func, kw=kw: E.activation(out=o, in_=i, func=f, **kw), reads=r, writes=w)

    def mm(self, out, lhsT, rhs, start, stop, r=(), w=()):
        self.P.op("pe", lambda E, o=out, a=lhsT, b=rhs, s=start, t=stop: E.matmul(o, a, b, start=s, stop=t), reads=r, writes=w)

    def tr(self, out, in_, ident, r=(), w=()):
        self.P.op("pe", lambda E, o=out, i=in_, d=ident: E.transpose(o, i, d), reads=r, writes=w)

    def tt(self, eng, out, in0, in1, op, r=(), w=()):
        self.P.op(eng, lambda E, o=out, a=in0, b=in1, p=op: E.tensor_tensor(out=o, in0=a, in1=b, op=p), reads=r, writes=w)

    def ts(self, eng, out, in0, s1, op0, s2=None, op1=None, r=(), w=()):
        if op1 is None:
            self.P.op(eng, lambda E, o=out, a=in0, s1=s1, p0=op0: E.tensor_scalar(out=o, in0=a, scalar1=s1, scalar2=None, op0=p0), reads=r, writes=w)
        else:
            self.P.op(eng, lambda E, o=out, a=in0, s1=s1, s2=s2, p0=op0, p1=op1: E.tensor_scalar(out=o, in0=a, scalar1=s1, scalar2=s2, op0=p0, op1=p1), reads=r, writes=w)

    def stt(self, out, in0, scalar, in1, op0, op1, r=(), w=()):
        self.P.op("dve", lambda E, o=out, a=in0, s=scalar, b=in1, p0=op0, p1=op1: E.scalar_tensor_tensor(out=o, in0=a, scalar=s, in1=b, op0=p0, op1=p1), reads=r, writes=w)

    def cp(self, eng, out, in_, r=(), w=()):
        self.P.op(eng, lambda E, o=out, i=in_: E.tensor_copy(out=o, in_=i), reads=r, writes=w)

    def ms(self, eng, out, val, w=()):
        self.P.op(eng, lambda E, o=out, v=val: E.memset(o, v), writes=w)

    def build(self):
        nc = self.nc
        with ExitStack() as st:
            P = self.P = Prog(nc, st)
            with nc.Block() as b0:
                @b0.sync
                def _(E):
                    for h in P.all_sems():
                        E.sem_clear(h)
            self.ps = [st.enter_context(nc.psum_tensor("psb%d" % i, [128, 512], F32)) for i in range(8)]
            A = lambda n, s, dt=F32: st.enter_context(nc.sbuf_tensor(self.uid(n), list(s), dt))
            self.c_cst = A("c_cst", [128, NCST])
            self.c_ident = self.c_cst[:, 0:128]
            self.c_ones = self.c_cst[:, 128:256]
            self.c_identb = A("c_identb", [128, 128], BF16)
            self.c_onesb = A("c_onesb", [128, 128], BF16)
            self.c_trib = A("c_trib", [128, 128], BF16)
            self.c_strb = A("c_strb", [128, 128], BF16)
            self.c_onesr = A("c_onesr", [128, 128])
            self.c_bdc = A("c_bdc", [128, 128], U32)
            self.c_eps = A("c_eps", [128, 2])
            self.modc = A("modc", [128, DEPTH, 48])
            self.sc1p = A("sc1p", [128, DEPTH, 8])
            self.sc2p = A("sc2p", [128, DEPTH, 8])
            self.lbc = A("lbc", [128, DEPTH, 8])
            self.oml = A("oml", [128, DEPTH, 8])
            self.noml = A("noml", [128, DEPTH, 8])
            self.nw = A("nw", [128, DEPTH])
            self.fb = A("fb", [16, DEPTH])
            self.slot = A("slot", [128, NT, 4], I32)
            self.wk = A("wk", [128, NT, 4])
            with nc.Block() as blk:
                self.blk = blk
                self.stage0()
                fns = [self.st_proj, self.st_hgrn, self.st_fox, self.st_merge, self.st_ln1, self.st_experts, self.st_ln2]
                for l in range(self.layers):
                    for f in fns[:self.nst]:
                        f(l)
                P.barrier()
                P.flush(blk)

    def bc_reg(self, E):
        if getattr(self, "_bc", None) is None:
            self._bc = E.to_reg(NE * CAP - 1)
        return self._bc

    def uid(self, n):
        self._uid = getattr(self, "_uid", 0) + 1
        return "%s_u%d" % (n, self._uid)

    def end_stage(self):
        self.P.barrier()
        self.P.flush(self.blk)

    def stage0(self):
        nc, P = self.nc, self.P
        with ExitStack() as st:
            A = lambda n, s, dt=F32: st.enter_context(nc.sbuf_tensor(self.uid(n), list(s), dt))
            self.dma("sp", self.c_cst[:], self.cst, w=["cst"])
            self.dma("sp", self.c_bdc[:], self.cstu, w=["bdc"])
            self.dma("sp", self.nw[:], self.normw, w=["nw"])
            self.dma("sp", self.fb[:], self.ffb, w=["fb"])
            self.cp("dve", self.c_identb[:], self.c_ident, r=["cst"], w=["identb"])
            self.cp("dve", self.c_onesb[:], self.c_ones, r=["cst"], w=["onesb"])
            self.cp("dve", self.c_trib[:], self.c_cst[:, 256:384], r=["cst"], w=["trib"])
            self.cp("dve", self.c_strb[:], self.c_cst[:, 384:512], r=["cst"], w=["strb"])
            self.act(self.c_onesr[:].bitcast(F32R), self.c_ones, AF.Identity, r=["cst"], w=["onesr"])
            self.ms("dve", self.c_eps[:, 0:1], LN_EPS, w=["eps"])
            self.ms("dve", self.c_eps[:, 1:2], RMS_EPS, w=["eps"])
            for i in range(8):
                self.dma("sp" if i % 2 == 0 else "act", self.XS[i * 512:(i + 1) * 512, :], self.x_in[i * 512:(i + 1) * 512, :], w=["XS"])
            lg = A("lg", [128, DEPTH, 8]); ex = A("ex", [128, DEPTH, 8]); mx = A("mx", [128, 8]); sm = A("sm", [128, 8])
            self.dma("sp", lg[:], self.lbl, w=["lg"])
            self.tt("dve", mx[:], lg[:, 0, :], lg[:, 1, :], ALU.max, r=["lg"], w=["mx"])
            for l in (2, 3):
                self.tt("dve", mx[:], mx[:], lg[:, l, :], ALU.max, r=["lg", "mx"], w=["mx"])
            for l in range(DEPTH):
                self.tt("dve", lg[:, l, :], lg[:, l, :], mx[:], ALU.subtract, r=["lg", "mx"], w=["lg"])
            self.act(ex[:], lg[:], AF.Exp, r=["lg"], w=["ex"])
            self.tt("dve", sm[:], ex[:, 0, :], ex[:, 1, :], ALU.add, r=["ex"], w=["sm"])
            for l in (2, 3):
                self.tt("dve", sm[:], sm[:], ex[:, l, :], ALU.add, r=["ex", "sm"], w=["sm"])
            P.op("dve", lambda E: E.reciprocal(out=sm[:], in_=sm[:]), reads=["sm"], writes=["sm"])
            for l in range(DEPTH):
                self.tt("dve", ex[:, l, :], ex[:, l, :], sm[:], ALU.mult, r=["ex", "sm"], w=["ex"])
            self.ms("dve", self.lbc[:, 0, :], 0.0, w=["lbc"])
            self.cp("dve", self.lbc[:, 1, :], ex[:, 1, :], r=["ex"], w=["lbc"])
            for l in (2, 3):
                self.tt("dve", self.lbc[:, l, :], self.lbc[:, l - 1, :], ex[:, l, :], ALU.add, r=["ex", "lbc"], w=["lbc"])
            self.ts("dve", self.lbc[:], self.lbc[:], 0.0, ALU.max, 1.0, ALU.min, r=["lbc"], w=["lbc"])
            self.ts("dve", self.oml[:], self.lbc[:], -1.0, ALU.mult, 1.0, ALU.add, r=["lbc"], w=["oml"])
            self.ts("dve", self.noml[:], self.lbc[:], 1.0, ALU.mult, -1.0, ALU.add, r=["lbc"], w=["noml"])
            ct = A("ct", [128, 8]); ca = A("ca", [128, 8])
            self.dma("sp", ct[:], self.cT, w=["ct"])
            self.act(ca[:], ct[:], AF.Silu, r=["ct"], w=["ca"])
            self.dma("sp", self.modc[:], self.adaB, w=["modc_b"])
            wbuf = [A("adaw%d" % i, [128, 8, 1024]) for i in range(2)]
            mps = self.ps[0]
            n = 0
            for l in range(self.layers):
                for g in range(6):
                    wb = wbuf[n % 2]; key = "adaw%d" % (n % 2)
                    src = self.ada_w[l, :, g * 1024:(g + 1) * 1024].rearrange("(k p) f -> p k f", p=128)
                    self.dma("sp" if n % 2 == 0 else "act", wb[:], src, w=[key])
                    for m in range(8):
                        col = l * 48 + g * 8 + m
                        for k in range(8):
                            self.mm(mps[:, col:col + 1], wb[:, k, m * 128:(m + 1) * 128], ca[:, k:k + 1], k == 0, k == 7, r=[key, "ca"], w=["mps"])
                    n += 1
            L = self.layers
            self.tt("dve", self.modc[:, 0:L, :], self.modc[:, 0:L, :], mps[:, 0:L * 48].rearrange("p (l j) -> p l j", j=48), ALU.add,
                    r=["mps", "modc_b"], w=["modc"])
            self.ts("dve", self.sc1p[:, 0:L, :], self.modc[:, 0:L, 8:16], 1.0, ALU.add, r=["modc"], w=["sc1p"])
            self.ts("dve", self.sc2p[:, 0:L, :], self.modc[:, 0:L, 32:40], 1.0, ALU.add, r=["modc"], w=["sc2p"])
            mt = A("mt", [48, 128])
            for l in range(L):
                tp = self.ps[1]
                self.tr(tp[0:48, 0:128], self.modc[:, l, :], self.c_ident, r=["modc", "cst"], w=["tp"])
                self.act(mt[:], tp[0:48, 0:128], AF.Identity, r=["tp"], w=["mt"])
                self.dma("sp", self.MODT[l], mt[:], r=["mt"], w=["MODT"])
            self.end_stage()

    def st_proj(self, l):
        nc, P = self.nc, self.P
        HALF = S // 2
        NG = HALF // 512
        with ExitStack() as st:
            A = lambda n, s, dt=F32: st.enter_context(nc.sbuf_tensor(self.uid(n), list(s), dt))
            hT = A("hT", [128, 8, HALF])
            xb = [A("xb%d" % i, [128, 4, D]) for i in range(2)]
            wb = [A("wb%d" % i, [128, 8, 512]) for i in range(2)]
            sg = [A("sg%d" % i, [128, HALF]) for i in range(2)]
            so = [A("so%d" % i, [128, 512]) for i in range(2)]
            cnt = dict(w=0, sg=0, so=0, ps=0)

            def nxt(k, n):
                v = cnt[k] % n; cnt[k] += 1
                return v

            def load_w(c0, ncol):
                i = nxt("w", 2)
                src = self.w_in[l, :, c0:c0 + ncol].rearrange("(k p) f -> p k f", p=128)
                self.dma("pool", wb[i][:, :, 0:ncol].bitcast(F32R), src, w=["wb%d" % i])
                return wb[i], "wb%d" % i

            for half in range(2):
                t0 = half * HALF
                for g in range(NG):
                    xi = g % 2
                    src = self.XS[t0 + g * 512:t0 + (g + 1) * 512, :].rearrange("(j p) d -> p j d", p=128)
                    self.dma("sp", xb[xi][:], src, w=["xb%d" % xi])
                    for k in range(8):
                        pi = nxt("ps", 4); pb = self.ps[pi]; pk = "ps%d" % pi
                        for j in range(4):
                            self.tr(pb[:, j * 128:(j + 1) * 128], xb[xi][:, j, k * 128:(k + 1) * 128], self.c_ident, r=["xb%d" % xi], w=[pk])
                        self.act(hT[:, k, g * 512:(g + 1) * 512].bitcast(F32R), pb[:], AF.Identity, r=[pk], w=["hT%d" % g],
                                 scale=self.sc1p[:, l, k:k + 1], bias=self.modc[:, l, k:k + 1])
                fm = [("hq", AF.Silu), ("hf", AF.Sigmoid), ("hg", AF.Silu), ("aq", AF.Identity), ("ak", AF.Identity),
                      ("ga", AF.Sigmoid), ("gb", AF.Sigmoid)]
                for name, fn in fm:
                    for c4 in range(2):
                        w, wkey = load_w(OFF[name] + c4 * 512, 512)
                        for cc in range(4):
                            si = nxt("sg", 2)
                            for g in range(NG):
                                pi = nxt("ps", 4); pb = self.ps[pi]; pk = "ps%d" % pi
                                for k in range(8):
                                    self.mm(pb[:], w[:, k, cc * 128:(cc + 1) * 128].bitcast(F32R), hT[:, k, g * 512:(g + 1) * 512].bitcast(F32R),
                                            k == 0, k == 7, r=[wkey, "hT%d" % g], w=[pk])
                                self.act(sg[si][:, g * 512:(g + 1) * 512], pb[:], fn, r=[pk], w=["sg%d_%d" % (si, g)])
                            r0 = c4 * 512 + cc * 128
                            self.dma("sp", self.PT[name][r0:r0 + 128, t0:t0 + HALF], sg[si][:], r=["sg%d_%d" % (si, g) for g in range(NG)], w=["PT_" + name])
                w, wkey = load_w(OFF["af"], 16)
                si = nxt("sg", 2)
                for g in range(NG):
                    pi = nxt("ps", 4); pb = self.ps[pi]; pk = "ps%d" % pi
                    for k in range(8):
                        self.mm(pb[0:16, :], w[:, k, 0:16].bitcast(F32R), hT[:, k, g * 512:(g + 1) * 512].bitcast(F32R), k == 0, k == 7,
                                r=[wkey, "hT%d" % g], w=[pk])
                    self.act(sg[si][0:16, g * 512:(g + 1) * 512], pb[0:16, :], AF.Identity, r=[pk], w=["sg%d_%d" % (si, g)])
                self.dma("sp", self.PT["af"][:, t0:t0 + HALF], sg[si][0:16, :], r=["sg%d_%d" % (si, g) for g in range(NG)], w=["PT_af"])
                for name in ("hi", "av"):
                    for c4 in range(2):
                        w, wkey = load_w(OFF[name] + c4 * 512, 512)
                        for tt_ in range(HALF // 128):
                            g = tt_ // 4
                            pi = nxt("ps", 4); pb = self.ps[pi]; pk = "ps%d" % pi
                            for k in range(8):
                                self.mm(pb[:], hT[:, k, tt_ * 128:(tt_ + 1) * 128].bitcast(F32R), w[:, k, :].bitcast(F32R), k == 0, k == 7,
                                        r=[wkey, "hT%d" % g], w=[pk])
                            oi = nxt("so", 2)
                            self.cp("dve", so[oi][:], pb[:], r=[pk], w=["so%d" % oi])
                            r0 = t0 + tt_ * 128
                            self.dma("sp", self.PR[name][r0:r0 + 128, c4 * 512:(c4 + 1) * 512], so[oi][:], r=["so%d" % oi], w=["PR_" + name])
            self.end_stage()

    def st_hgrn(self, l):
        nc, P = self.nc, self.P
        NCH = 2
        with ExitStack() as st:
            A = lambda n, s, dt=F32: st.enter_context(nc.sbuf_tensor(self.uid(n), list(s), dt))
            Sst = [A("S%d" % h, [128, 128]) for h in range(HG_H)]
            Sbf = [A("Sb%d" % h, [128, 128], BF16) for h in range(HG_H)]
            ATm = [[A("ATm%d_%d" % (c, i), [128, 128], BF16) for i in range(2)] for c in range(NCH)]
            khT = [[A("khT%d_%d" % (c, i), [128, 128], BF16) for i in range(2)] for c in range(NCH)]
            ptn = ["SQ", "SF", "kk", "gg", "BB", "D1", "D4", "E1", "E2", "E3", "E4"]
            PTs = [{n: A("%s_%d" % (n, i), [128, 512]) for n in ptn} for i in range(2)]
            US = []
            for i in range(2 * NCH):
                d = {n: A("%s_%d" % (n, i), [128, 512], BF16) for n in ("qt", "kt", "qh", "kh")}
                d["dec"] = A("dec_%d" % i, [128, 8])
                for n in ("I", "IA", "IB"):
                    d[n] = A("%s_%d" % (n, i), [128, 4, 128], BF16)
                d["SG"] = A("SG_%d" % i, [128, 512]); d["Ob"] = A("Ob_%d" % i, [128, 512])
                US.append(d)
            EP = [{n: A("%s_%d" % (n, i), [128, 512]) for n in ("sq", "rs", "on")} for i in range(2)]
            for h in range(HG_H):
                self.ms("dve", Sst[h][:], 0.0, w=["S%d" % h])
                self.ms("pool", Sbf[h][:], 0.0, w=["Sb%d" % h])
            for c in range(NCH):
                for i in range(2):
                    self.ms("pool", ATm[c][i][:], 0.0, w=["ATm%d_%d" % (c, i)])
            for i in range(2 * NCH):
                self.ms("pool", US[i]["IA"][:], 0.0, w=["IA_%d" % i])
                self.ms("pool", US[i]["IB"][:], 0.0, w=["IB_%d" % i])
            scanmask = self.c_cst[:, 512:1024]
            NU = (S // 512) * HG_H

            def unit(n):
                bi, h = divmod(n, HG_H)
                return bi, h, slice(bi * 512, (bi + 1) * 512), slice(h * 128, (h + 1) * 128)

            def prologue(n):
                bi, h, cols, rows = unit(n)
                t = PTs[n % 2]; u = US[n % (2 * NCH)]
                tk = lambda nm: "%s_%d" % (nm, n % 2)
                uk = lambda nm: "%s_%d" % (nm, n % (2 * NCH))
                self.dma("sp", t["SQ"][:], self.PT["hq"][rows, cols], w=[tk("SQ")])
                self.dma("sp", t["SF"][:], self.PT["hf"][rows, cols], w=[tk("SF")])
                self.dma("sp", u["SG"][:], self.PT["hg"][rows, cols], w=[uk("SG")])
                isrc = self.PR["hi"][cols, rows].rearrange("(j p) v -> p j v", p=128)
                self.dma("pool", u["I"][:], isrc, w=[uk("I")])
                self.dma("pool", u["IA"][0:64, :, :], isrc[0:64], w=[uk("IA")])
                self.dma("pool", u["IB"][64:128, :, :], isrc[64:128], w=[uk("IB")])
                oml = self.oml[:, l, h:h + 1]; noml = self.noml[:, l, h:h + 1]; lb = self.lbc[:, l, h:h + 1]
                self.ts("dve", t["kk"][:], t["SF"][:], noml, ALU.mult, oml, ALU.add, r=[tk("SF")], w=[tk("kk")])
                self.act(t["gg"][:], t["SF"][:], AF.Ln, r=[tk("SF")], w=[tk("gg")], scale=oml, bias=lb)
                P.op("dve", lambda E, o=t["BB"][:], m=scanmask, g=t["gg"][:]: E.tensor_tensor_scan(out=o, data0=m, data1=g, initial=0.0, op0=ALU.mult, op1=ALU.add),
                     reads=[tk("gg")], writes=[tk("BB")])
                B3 = t["BB"][:].rearrange("p (c t) -> p c t", t=64)
                self.tt("pool", t["D1"][:].rearrange("p (c t) -> p c t", t=64), B3, B3[:, :, 31:32].to_broadcast([128, 8, 64]), ALU.subtract, r=[tk("BB")], w=[tk("D1")])
                self.tt("pool", t["D4"][:].rearrange("p (c t) -> p c t", t=64), B3, B3[:, :, 63:64].to_broadcast([128, 8, 64]), ALU.subtract, r=[tk("BB")], w=[tk("D4")])
                self.act(t["E1"][:], t["D1"][:], AF.Exp, r=[tk("D1")], w=[tk("E1")])
                self.act(t["E2"][:], t["D1"][:], AF.Exp, r=[tk("D1")], w=[tk("E2")], scale=-1.0)
                self.act(t["E3"][:], t["BB"][:], AF.Exp, r=[tk("BB")], w=[tk("E3")])
                self.act(t["E4"][:], t["D4"][:], AF.Exp, r=[tk("D4")], w=[tk("E4")], scale=-1.0)
                self.tt("dve", u["qt"][:], t["SQ"][:], t["E1"][:], ALU.mult, r=[tk("SQ"), tk("E1")], w=[uk("qt")])
                self.tt("pool", u["kt"][:], t["kk"][:], t["E2"][:], ALU.mult, r=[tk("kk"), tk("E2")], w=[uk("kt")])
                self.tt("dve", u["qh"][:], t["SQ"][:], t["E3"][:], ALU.mult, r=[tk("SQ"), tk("E3")], w=[uk("qh")])
                self.tt("pool", u["kh"][:], t["kk"][:], t["E4"][:], ALU.mult, r=[tk("kk"), tk("E4")], w=[uk("kh")])
                self.cp("dve", u["dec"][:], t["E3"][:].rearrange("p (c t) -> p c t", t=64)[:, :, 63], r=[tk("E3")], w=[uk("dec")])

            def tile_steps(n, c, j):
                bi, h, cols, rows = unit(n)
                u = US[n % (2 * NCH)]
                uk = lambda nm: "%s_%d" % (nm, n % (2 * NCH))
                jp = j % 2
                c0 = j * 128
                cb = 3 * c
                at_ps = self.ps[cb][:, 0:128]; atk = "ps%d" % cb
                kh_ps = self.ps[cb + 1][:, 0:64].bitcast(BF16); khk = "ps%d" % (cb + 1)
                su = self.ps[cb + 1][:, 128:256]; suk = khk
                o_ps = self.ps[cb + 2][:, 0:128]; ok = "ps%d" % (cb + 2)
                AT = ATm[c][jp]; ATk = "ATm%d_%d" % (c, jp)
                KT = khT[c][jp]; KTk = "khT%d_%d" % (c, jp)
                Sk, Sbk = "S%d" % h, "Sb%d" % h

                def s1():
                    self.mm(at_ps, u["kt"][:, c0:c0 + 128], u["qt"][:, c0:c0 + 128], True, True, r=[uk("kt"), uk("qt")], w=[atk])
                    self.tr(kh_ps, u["kh"][:, c0:c0 + 128], self.c_identb[:], r=[uk("kh")], w=[khk])

                def s2():
                    P.op("dve", lambda E, o=AT[:], m=self.c_bdc[:], d=at_ps: E.copy_predicated(out=o, mask=m, data=d), reads=[atk], writes=[ATk])
                    self.act(KT[:], kh_ps, AF.Identity, r=[khk], w=[KTk])

                def s3():
                    self.mm(o_ps, u["I"][:, j, :], AT[:], True, False, r=[uk("I"), ATk], w=[ok])
                    self.mm(o_ps[:, 0:64], Sbf[h][:], u["qh"][:, c0:c0 + 64], False, False, r=[Sbk, uk("qh")], w=[ok])
                    self.mm(su, KT[:], u["IA"][:, j, :], True, True, r=[KTk, uk("IA")], w=[suk])

                def s4():
                    self.stt(Sst[h][:], Sst[h][:], u["dec"][:, 2 * j:2 * j + 1], su, ALU.mult, ALU.add, r=[suk, uk("dec"), Sk], w=[Sk])
                    self.act(Sbf[h][:], Sst[h][:], AF.Identity, r=[Sk], w=[Sbk])

                def s5():
                    self.mm(o_ps[:, 64:128], Sbf[h][:], u["qh"][:, c0 + 64:c0 + 128], False, True, r=[Sbk, uk("qh")], w=[ok])
                    self.mm(su, KT[:], u["IB"][:, j, :], True, True, r=[KTk, uk("IB")], w=[suk])

                def s6():
                    self.act(u["Ob"][:, c0:c0 + 128], o_ps, AF.Identity, r=[ok], w=[uk("Ob") + "_%d" % j])
                    self.stt(Sst[h][:], Sst[h][:], u["dec"][:, 2 * j + 1:2 * j + 2], su, ALU.mult, ALU.add, r=[suk, uk("dec"), Sk], w=[Sk])
                    self.act(Sbf[h][:], Sst[h][:], AF.Identity, r=[Sk], w=[Sbk])
                return [s1, s2, s3, s4, s5, s6]

            def epilogue_steps(n, c):
                bi, h, cols, rows = unit(n)
                u = US[n % (2 * NCH)]; e = EP[c % 2]
                uk = lambda nm: "%s_%d" % (nm, n % (2 * NCH))
                ek = lambda nm: "%s_%d" % (nm, c % 2)
                obk = [uk("Ob") + "_%d" % j for j in range(4)]

                def e1():
                    self.act(e["sq"][:].bitcast(F32R), u["Ob"][:], AF.Square, r=obk, w=[ek("sq")])

                def e2():
                    self.mm(self.ps[6 + c % 2][:], self.c_onesr[:].bitcast(F32R), e["sq"][:].bitcast(F32R), True, True, r=[ek("sq")], w=["ps%d" % (6 + c % 2)])

                def e3():
                    self.act(e["rs"][:], self.ps[6 + c % 2][:], AF.Ln, r=["ps%d" % (6 + c % 2)], w=[ek("rs")], scale=1.0 / 128.0, bias=self.c_eps[:, 1:2])
                    self.act(e["rs"][:], e["rs"][:], AF.Exp, r=[ek("rs")], w=[ek("rs")], scale=-0.5)

                def e4():
                    self.tt("dve", e["on"][:], u["Ob"][:], e["rs"][:], ALU.mult, r=obk + [ek("rs")], w=[ek("on")])
                    self.stt(e["on"][:], e["on"][:], self.nw[:, l:l + 1], u["SG"][:], ALU.mult, ALU.mult, r=[ek("on"), uk("SG")], w=[ek("on")])
                    self.dma("sp", self.OAT[rows, cols], e["on"][:], r=[ek("on")], w=["OAT"])
                return [e1, e2, e3, e4]

            for n in range(min(NCH, NU)):
                prologue(n)
            for g0 in range(0, NU, NCH):
                grp = list(range(g0, min(g0 + NCH, NU)))
                for j in range(4):
                    steps = [tile_steps(n, c, j) for c, n in enumerate(grp)]
                    for si in range(6):
                        for stp in steps:
                            stp[si]()
                    nxt = g0 + NCH + j
                    if j < NCH and nxt < NU:
                        prologue(nxt)
                steps = [epilogue_steps(n, c) for c, n in enumerate(grp)]
                for si in range(4):
                    for stp in steps:
                        stp[si]()
            self.end_stage()

    def st_fox(self, l):
        nc, P = self.nc, self.P
        with ExitStack() as st:
            A = lambda n, s, dt=F32: st.enter_context(nc.sbuf_tensor(self.uid(n), list(s), dt))
            ft = A("ft", [16, S]); Fc = A("Fc", [16, S])
            Ftok = A("Ftok", [128, NT, 16]); FrefB = A("FrefB", [128, 16, 8]); rbd = A("rbd", [16, 16, 8])
            vaug = [A("vaug%d" % i, [128, NT, 128], BF16) for i in range(2)]
            QA = [A("QA%d" % i, [128, S], BF16) for i in range(2)]
            QB = [A("QB%d" % i, [128, S], BF16) for i in range(2)]
            KP = [A("KP%d" % i, [128, S], BF16) for i in range(2)]
            e64 = A("e64", [128, 128]); e64r = A("e64r", [128, 128])
            bias = [A("bias%d" % i, [128, NT, 8]) for i in range(2)]
            Pb = [A("Pb%d" % i, [128, 512], BF16) for i in range(4)]
            Osb = [A("Osb%d" % i, [64, 512]) for i in range(2)]
            rc = [A("rc%d" % i, [128, 512]) for i in range(2)]
            zz = A("zz", [128, 512])
            ob = [A("ob%d" % i, [64, 512]) for i in range(2)]
            for i in range(2):
                self.ms("pool", vaug[i][:], 0.0, w=["vaug%d" % i])
                self.ms("pool", vaug[i][:, :, 64:65], 1.0, w=["vaug%d" % i])
                self.ms("pool", QA[i][64:128, :], 0.0, w=["QA%d" % i])
                self.ms("pool", QB[i][0:64, :], 0.0, w=["QB%d" % i])
            self.ms("dve", zz[:], 0.0, w=["zz"])
            self.ms("dve", e64[:], 0.0, w=["e64"])
            self.ms("dve", e64[64:65, :], 1.0, w=["e64"])
            self.act(e64r[:].bitcast(F32R), e64[:], AF.Identity, r=["e64"], w=["e64r"])
            for i in range(2):
                self.act(rc[i][:].bitcast(F32R), zz[:], AF.Identity, r=["zz"], w=["rc%d" % i])
            self.dma("sp", ft[:], self.PT["af"], w=["ft"])
            self.act(ft[:], ft[:], AF.Sigmoid, r=["ft"], w=["ft"], bias=self.fb[:, l:l + 1])
            self.act(ft[:], ft[:], AF.Ln, r=["ft"], w=["ft"])
            P.op("dve", lambda E: E.tensor_tensor_scan(out=Fc[:], data0=self.c_ones[0:16, 0:1].to_broadcast([16, S]), data1=ft[:], initial=0.0, op0=ALU.mult, op1=ALU.add),
                 reads=["ft"], writes=["Fc"])
            for i in range(NT):
                self.tr(self.ps[0][:, i * 16:(i + 1) * 16], Fc[0:16, i * 128:(i + 1) * 128], self.c_ident[0:16, 0:16], r=["Fc"], w=["ps0"])
            self.act(Ftok[:].rearrange("p i h -> p (i h)"), self.ps[0][:], AF.Identity, r=["ps0"], w=["Ftok"])
            fref = Fc[:].rearrange("h (j t) -> h j t", t=512)[:, :, 256]
            self.tt("dve", rbd[:], self.c_ident[0:16, 0:16].unsqueeze(2).to_broadcast([16, 16, 8]), fref.unsqueeze(1).to_broadcast([16, 16, 8]), ALU.mult,
                    r=["Fc"], w=["rbd"])
            self.mm(self.ps[1][:, 0:128], self.c_ones[0:16, :], rbd[:].rearrange("h a j -> h (a j)"), True, True, r=["rbd"], w=["ps1"])
            self.act(FrefB[:].rearrange("p a j -> p (a j)"), self.ps[1][:, 0:128], AF.Identity, r=["ps1"], w=["FrefB"])
            LA = 2
            its = []
            for h in range(FX_H):
                for j in range(S // 512):
                    n_i = 4 * j + 4
                    for i in range(n_i):
                        its.append((h, j, i, n_i))
            hstate = {}

            def head_loads(h):
                par = h % 2; pp = (h // 2) % 2
                rows = slice(h * 64, (h + 1) * 64)
                if h % 2 == 0:
                    self.dma("pool", KP[pp][:], self.PT["ak"][h * 64:(h + 2) * 64, :], w=["KP%d" % pp])
                    self.dma("pool", QA[pp][0:64, :], self.PT["aq"][rows, :], w=["QA%d" % pp])
                    qsrc = QA[pp]; qkey = "QA%d" % pp
                else:
                    self.dma("pool", QB[pp][64:128, :], self.PT["aq"][rows, :], w=["QB%d" % pp])
                    qsrc = QB[pp]; qkey = "QB%d" % pp
                self.dma("pool", vaug[par][:, :, 0:64], self.PR["av"][:, rows].rearrange("(i p) v -> p i v", p=128), w=["vaug%d" % par])
                self.tt("dve", bias[par][:], FrefB[:, h, :].unsqueeze(1).to_broadcast([128, NT, 8]), Ftok[:, :, h:h + 1].to_broadcast([128, NT, 8]), ALU.subtract,
                        r=["FrefB", "Ftok"], w=["bias%d" % par])
                hstate[h] = (par, pp, qsrc, qkey, rows)

            def emit_S(n):
                h, j, i, n_i = its[n]
                if h not in hstate:
                    head_loads(h)
                par, pp, qsrc, qkey, rows = hstate[h]
                cs = max(i - 4 * j, 0) * 128
                si = n % 4
                self.mm(self.ps[si][:, cs:512], KP[pp][:, i * 128:(i + 1) * 128], qsrc[:, j * 512 + cs:(j + 1) * 512], True, True,
                        r=["KP%d" % pp, qkey], w=["ps%d" % si])

            deferred = []

            def run_deferred(force=False):
                for d_ in list(deferred):
                    d_[0] -= 1
                    if d_[0] <= 0 or force:
                        d_[1]()
                        deferred.remove(d_)

            nblk = 0
            for n in range(min(LA, len(its))):
                emit_S(n)
            for n in range(len(its)):
                h, j, i, n_i = its[n]
                par, pp, qsrc, qkey, rows = hstate[h]
                if i == 0 and j == 0 and h + 1 < FX_H and (h + 1) not in hstate:
                    head_loads(h + 1)
                if n + LA < len(its):
                    emit_S(n + LA)
                r_ = i - 4 * j
                cs = max(r_, 0) * 128
                si = n % 4; skey = "ps%d" % si
                if i == 0:
                    oi = nblk % 2; nblk += 1
                o_ps = self.ps[4 + oi]; okey = "ps%d" % (4 + oi)
                pi = n % 4; pkey = "Pb%d" % pi
                pt = Pb[pi][:, cs:512]
                self.act(pt, self.ps[si][:, cs:512], AF.Exp, r=[skey, "bias%d" % par], w=[pkey], scale=0.125, bias=bias[par][:, i, j:j + 1])
                if r_ >= 0:
                    self.tt("pool", Pb[pi][:, cs:cs + 128], Pb[pi][:, cs:cs + 128], self.c_trib[:], ALU.mult, r=[pkey], w=[pkey])
                self.mm(o_ps[:, cs:512], vaug[par][:, i, :], pt, i == 0, i == n_i - 1, r=["vaug%d" % par, pkey], w=[okey])
                run_deferred()
                if i == n_i - 1:
                    self.act(Osb[oi][:], o_ps[0:64, :], AF.Identity, r=[okey], w=["Osb%d" % oi])

                    def _rcp(E, o=rc[oi][64:65, :].bitcast(F32R), i_=o_ps[64:65, :]):
                        with nc.allow_low_precision(reason="fp32r operand for the broadcast matmul"):
                            return E.reciprocal(out=o, in_=i_)
                    P.op("dve", _rcp, reads=[okey], writes=["rc%d" % oi])

                    def part_b(oi=oi, rows=rows, j=j):
                        self.mm(self.ps[6][:], e64r[:].bitcast(F32R), rc[oi][:].bitcast(F32R), True, True, r=["rc%d" % oi, "e64r"], w=["ps6"])
                        self.tt("dve", ob[oi][:], Osb[oi][:], self.ps[6][0:64, :], ALU.mult, r=["Osb%d" % oi, "ps6"], w=["ob%d" % oi])
                        self.dma("sp", self.OBT[rows, j * 512:(j + 1) * 512], ob[oi][:], r=["ob%d" % oi], w=["OBT"])
                    deferred.append([3, part_b])
            run_deferred(force=True)
            self.end_stage()

    def st_merge(self, l):
        nc, P = self.nc, self.P
        with ExitStack() as st:
            A = lambda n, s, dt=F32: st.enter_context(nc.sbuf_tensor(self.uid(n), list(s), dt))
            wbr = A("wbr", [128, 16, D], BF16)
            wo = A("wo", [128, 8, D], BF16)
            inb = []
            for i in range(2):
                inb.append({n: A("%s%d" % (n, i), [128, 8, 512], BF16) for n in ("oa", "ob", "ga", "gb")})
            mg = [A("mg%d" % i, [128, 8, 512], BF16) for i in range(2)]
            t1 = [A("t1_%d" % i, [128, 512]) for i in range(2)]
            t2 = [A("t2_%d" % i, [128, 512]) for i in range(2)]
            ysb = [A("ysb%d" % i, [128, D]) for i in range(2)]
            for k2 in range(4):
                self.dma("pool", wbr[:, k2 * 4:(k2 + 1) * 4, :], self.w_branch[l, k2 * 512:(k2 + 1) * 512, :].rearrange("(k p) d -> p k d", p=128), w=["wbr"])
            for k2 in range(2):
                self.dma("pool", wo[:, k2 * 4:(k2 + 1) * 4, :], self.w_out[l, k2 * 512:(k2 + 1) * 512, :].rearrange("(k p) d -> p k d", p=128), w=["wo"])
            nt_ = 0; ny = 0
            for bi in range(S // 512):
                par = bi % 2
                cols = slice(bi * 512, (bi + 1) * 512)
                ib = inb[par]
                for nm, src in (("oa", self.OAT), ("ob", self.OBT), ("ga", self.PT["ga"]), ("gb", self.PT["gb"])):
                    self.dma("pool", ib[nm][:], src[:, cols].rearrange("(k p) t -> p k t", p=128), w=["%s%d" % (nm, par)])
                for m in range(8):
                    pa = self.ps[(2 * m) % 4]; pak = "ps%d" % ((2 * m) % 4)
                    pb_ = self.ps[(2 * m + 1) % 4]; pbk = "ps%d" % ((2 * m + 1) % 4)
                    for k in range(8):
                        self.mm(pa[:], wbr[:, k, m * 128:(m + 1) * 128], ib["oa"][:, k, :], k == 0, k == 7, r=["wbr", "oa%d" % par], w=[pak])
                    for k in range(8):
                        self.mm(pb_[:], wbr[:, 8 + k, m * 128:(m + 1) * 128], ib["ob"][:, k, :], k == 0, k == 7, r=["wbr", "ob%d" % par], w=[pbk])
                    ti = nt_ % 2; nt_ += 1
                    self.tt("dve", t1[ti][:], pa[:], ib["ga"][:, m, :], ALU.mult, r=[pak, "ga%d" % par], w=["t1_%d" % ti])
                    self.tt("dve", t2[ti][:], pb_[:], ib["gb"][:, m, :], ALU.mult, r=[pbk, "gb%d" % par], w=["t2_%d" % ti])
                    self.tt("pool", mg[par][:, m, :], t1[ti][:], t2[ti][:], ALU.add, r=["t1_%d" % ti, "t2_%d" % ti], w=["mg%d_%d" % (par, m)])
                for r_ in range(4):
                    yi = ny % 2; ny += 1
                    for hh in range(2):
                        yp = self.ps[4 + 2 * yi + hh]; ypk = "ps%d" % (4 + 2 * yi + hh)
                        for k in range(8):
                            self.mm(yp[:], mg[par][:, k, r_ * 128:(r_ + 1) * 128], wo[:, k, hh * 512:(hh + 1) * 512], k == 0, k == 7,
                                    r=["mg%d_%d" % (par, k), "wo"], w=[ypk])
                        self.act(ysb[yi][:, hh * 512:(hh + 1) * 512], yp[:], AF.Identity, r=[ypk], w=["ysb%d_%d" % (yi, hh)])
                    r0 = bi * 512 + r_ * 128
                    self.dma("sp", self.YS[r0:r0 + 128, :], ysb[yi][:], r=["ysb%d_0" % yi, "ysb%d_1" % yi], w=["YS"])
            self.end_stage()

    def layer_norm(self, z, zk, xh, xhk, tmp, mvk):
        P = self.P
        stt_, mv, sd = tmp["st"], tmp["mv"], tmp["sd"]
        for c in range(2):
            P.op("dve", lambda E, o=stt_[:, c * 6:(c + 1) * 6], i=z[:, c * 512:(c + 1) * 512]: E.bn_stats(out=o, in_=i), reads=[zk], writes=[mvk + "st%d" % c])
        P.op("dve", lambda E, o=mv[:], i=stt_[:]: E.bn_aggr(out=o, in_=i), reads=[mvk + "st0", mvk + "st1"], writes=[mvk + "mv"])
        self.act(sd[:, 0:1], mv[:, 1:2], AF.Sqrt, r=[mvk + "mv"], w=[mvk + "sd"], bias=self.c_eps[:, 0:1])
        P.op("dve", lambda E, o=sd[:, 1:2], i=sd[:, 0:1]: E.reciprocal(out=o, in_=i), reads=[mvk + "sd"], writes=[mvk + "rstd"])
        self.ts("dve", sd[:, 2:3], mv[:, 0:1], sd[:, 1:2], ALU.mult, -1.0, ALU.mult, r=[mvk + "mv", mvk + "rstd"], w=[mvk + "nmr"])
        self.act(xh, z[:], AF.Identity, r=[zk, mvk + "rstd", mvk + "nmr"], w=[xhk], scale=sd[:, 1:2], bias=sd[:, 2:3])

    def load_row(self, q, dst, src_row, key):
        self.dma(q, dst, src_row.partition_broadcast(128)[:, 0, :], w=[key])

    def st_ln1(self, l):
        nc, P = self.nc, self.P
        with ExitStack() as st:
            A = lambda n, s, dt=F32: st.enter_context(nc.sbuf_tensor(self.uid(n), list(s), dt))
            g1p = A("g1p", [128, D]); lg_ = A("lg_", [128, D]); lb_ = A("lb_", [128, D]); s2p = A("s2p", [128, D]); sh2 = A("sh2", [128, D])
            wr = A("wr", [128, 8, NE]); br = A("br", [128, NE])
            csel = A("csel", [128, NE], BF16)
            modrow = lambda j0: self.MODT[l:l + 1, j0:j0 + 8, :].rearrange("a j p -> a (j p)")
            self.load_row("sp", g1p[:], modrow(16), "g1p")
            self.load_row("sp", sh2[:], modrow(24), "sh2")
            self.load_row("sp", s2p[:], modrow(32), "s2p")
            self.load_row("sp", lg_[:], self.ln1_g[l:l + 1, :], "lg_")
            self.load_row("sp", lb_[:], self.ln1_b[l:l + 1, :], "lb_")
            self.load_row("sp", br[:], self.b_router[l:l + 1, :], "br")
            self.dma("sp", wr[:], self.w_router[l].rearrange("(k p) e -> p k e", p=128), w=["wr"])
            self.ts("dve", g1p[:], g1p[:], 1.0, ALU.add, r=["g1p"], w=["g1p"])
            self.ts("dve", s2p[:], s2p[:], 1.0, ALU.add, r=["s2p"], w=["s2p"])
            self.ms("dve", csel[:], 0.0, w=["csel"])
            iotaC = self.c_cst[:, 1024:1056]
            T = []
            NB = 4
            for i in range(NB):
                d = {n: A("%s%d" % (n, i), [128, D]) for n in ("x", "y", "z", "xh", "x1", "h2")}
                d["h2T"] = A("h2T%d" % i, [128, 8, 128])
                d["st"] = A("st%d" % i, [128, 12]); d["mv"] = A("mv%d" % i, [128, 2]); d["sd"] = A("sd%d" % i, [128, 3])
                d["lgt"] = A("lgt%d" % i, [128, NE]); d["top"] = A("top%d" % i, [128, 8]); d["sel"] = A("sel%d" % i, [128, NE], BF16)
                d["SL"] = A("SL%d" % i, [128, NE]); d["oh"] = A("oh%d" % i, [128, NE]); d["sf"] = A("sf%d" % i, [128, 4])
                d["e4"] = A("e4%d" % i, [128, 4]); d["nt"] = A("nt%d" % i, [128, 2]); d["ovf"] = A("ovf%d" % i, [128, NE])
                T.append(d)
            def loads1(t):
                i_ = t % NB
                self.dma("sp", T[i_]["x"][:], self.XS[t * 128:(t + 1) * 128, :], w=["x%d" % i_])
                self.dma("sp", T[i_]["y"][:], self.YS[t * 128:(t + 1) * 128, :], w=["y%d" % i_])
            loads1(0); loads1(1)
            for t in range(NT):
                if t + 2 < NT:
                    loads1(t + 2)
                i = t % NB; d = T[i]
                kx = lambda nm: "%s%d" % (nm, i)
                rows = slice(t * 128, (t + 1) * 128)
                self.tt("dve", d["z"][:], d["y"][:], g1p[:], ALU.mult, r=[kx("y"), "g1p"], w=[kx("z")])
                self.stt(d["z"][:], d["x"][:], ALPHA, d["z"][:], ALU.mult, ALU.add, r=[kx("x"), kx("z")], w=[kx("z")])
                self.layer_norm(d["z"], kx("z"), d["xh"][:], kx("xh"), d, kx("m"))
                self.tt("dve", d["x1"][:], d["xh"][:], lg_[:], ALU.mult, r=[kx("xh"), "lg_"], w=[kx("x1")])
                self.tt("pool", d["x1"][:], d["x1"][:], lb_[:], ALU.add, r=[kx("x1"), "lb_"], w=[kx("x1")])
                self.dma("sp", self.XS[rows, :], d["x1"][:], r=[kx("x1")], w=["XSw"])
                self.tt("dve", d["h2"][:], d["x1"][:], s2p[:], ALU.mult, r=[kx("x1"), "s2p"], w=[kx("h2")])
                self.tt("pool", d["h2"][:], d["h2"][:], sh2[:], ALU.add, r=[kx("h2"), "sh2"], w=[kx("h2")])
                if self.nst < 6:
                    continue
                for hh in range(2):
                    pb = self.ps[hh]; pk = "ps%d" % hh
                    for k4 in range(4):
                        k = hh * 4 + k4
                        self.tr(pb[:, k4 * 128:(k4 + 1) * 128], d["h2"][:, k * 128:(k + 1) * 128], self.c_ident, r=[kx("h2")], w=[pk])
                    self.act(d["h2T"][:, hh * 4:(hh + 1) * 4, :].rearrange("p k t -> p (k t)"), pb[:], AF.Identity, r=[pk], w=[kx("h2T") + "_%d" % hh])
                lps = self.ps[2][:, 0:NE]
                for k in range(8):
                    self.mm(lps, d["h2T"][:, k, :], wr[:, k, :], k == 0, k == 7, r=[kx("h2T") + "_%d" % (k // 4), "wr"], w=["ps2"])
                self.tt("dve", d["lgt"][:], lps, br[:], ALU.add, r=["ps2", "br"], w=[kx("lgt")])
                P.op("dve", lambda E, o=d["top"][:], i_=d["lgt"][:]: E.max(out=o, in_=i_), reads=[kx("lgt")], writes=[kx("top")])
                self.ts("dve", d["sel"][:], d["lgt"][:], d["top"][:, 3:4], ALU.is_ge, r=[kx("lgt"), kx("top")], w=[kx("sel")])
                pps = self.ps[3][:, 0:NE]
                self.mm(pps, self.c_strb[:], d["sel"][:], True, False, r=[kx("sel")], w=["ps3"])
                self.mm(pps, self.c_onesb[:], csel[:], False, True, r=["csel"], w=["ps3"])
                self.tt("pool", csel[:], csel[:], d["sel"][:], ALU.add, r=["csel", kx("sel")], w=["csel"])
                self.ts("dve", d["ovf"][:], pps, float(CAP), ALU.is_ge, 1.0e7, ALU.mult, r=["ps3"], w=[kx("ovf")])
                self.tt("dve", d["SL"][:], pps, iotaC, ALU.add, r=["ps3"], w=[kx("SL")])
                self.tt("dve", d["SL"][:], d["SL"][:], d["ovf"][:], ALU.add, r=[kx("SL"), kx("ovf")], w=[kx("SL")])
                for k in range(4):
                    self.ts("dve", d["oh"][:], d["lgt"][:], d["top"][:, k:k + 1], ALU.is_equal, r=[kx("lgt"), kx("top")], w=[kx("oh")])
                    self.tt("dve", d["oh"][:], d["oh"][:], d["SL"][:], ALU.mult, r=[kx("oh"), kx("SL")], w=[kx("oh")])
                    P.op("dve", lambda E, o=d["sf"][:, k:k + 1], i_=d["oh"][:]: E.tensor_reduce(out=o, in_=i_, axis=mybir.AxisListType.X, op=ALU.add),
                         reads=[kx("oh")], writes=[kx("sf")])
                self.cp("dve", self.slot[:, t, :], d["sf"][:], r=[kx("sf")], w=["slot%d" % t])
                self.ts("dve", d["nt"][:, 0:1], d["top"][:, 0:1], -1.0, ALU.mult, r=[kx("top")], w=[kx("nt")])
                self.act(d["e4"][:], d["top"][:, 0:4], AF.Exp, r=[kx("top"), kx("nt")], w=[kx("e4")], bias=d["nt"][:, 0:1])
                P.op("dve", lambda E, o=d["nt"][:, 1:2], i_=d["e4"][:]: E.tensor_reduce(out=o, in_=i_, axis=mybir.AxisListType.X, op=ALU.add),
                     reads=[kx("e4")], writes=[kx("nt") + "s"])
                P.op("dve", lambda E, o=d["nt"][:, 1:2]: E.reciprocal(out=o, in_=o), reads=[kx("nt") + "s"], writes=[kx("nt") + "s"])
                self.ts("dve", self.wk[:, t, :], d["e4"][:], d["nt"][:, 1:2], ALU.mult, r=[kx("e4"), kx("nt") + "s"], w=["wk%d" % t])
                for k in range(4):
                    P.dma("pool", lambda E, o=self.XG, ix=self.slot[:, t, k:k + 1], i_=d["h2"][:]: E.indirect_dma_start(
                        out=o, out_offset=bass.IndirectOffsetOnAxis(ap=ix, axis=0), in_=i_, in_offset=None,
                        bounds_check=self.bc_reg(E), oob_is_err=False), reads=[kx("h2"), "slot%d" % t], writes=["XG"])
            self.end_stage()

    def st_experts(self, l):
        nc, P = self.nc, self.P
        blocks = []
        r = 0
        while r < CAP:
            nr = min(512, CAP - r)
            blocks.append((r, nr))
            r += nr
        with ExitStack() as st:
            A = lambda n, s, dt=F32: st.enter_context(nc.sbuf_tensor(self.uid(n), list(s), dt))
            W = [A("W%d" % i, [128, 8, D], BF16) for i in range(6)]
            bcol = [A("bcol%d" % i, [128, 16]) for i in range(2)]
            bd = [A("bd%d" % i, [128, D]) for i in range(2)]
            X = [A("X%d" % i, [128, 4, D], BF16) for i in range(2)]
            XT = [A("XT%d" % i, [128, 8, 512], BF16) for i in range(2)]
            AT = [A("AT%d" % i, [128, 8, 512], BF16) for i in range(2)]
            tm = [{n: A("%s%d" % (n, i), [128, 512]) for n in ("gsb", "ssb", "u1")} for i in range(2)]
            Ysb = [A("Ysb%d" % i, [128, D]) for i in range(2)]

            def load_w(e):
                s0 = (e % 2) * 3
                for i, src in enumerate((self.w_gate, self.w_up, self.w_down)):
                    for k2 in range(2):
                        self.dma("pool", W[s0 + i][:, k2 * 4:(k2 + 1) * 4, :], src[l, e, k2 * 512:(k2 + 1) * 512, :].rearrange("(k p) f -> p k f", p=128),
                                 w=["W%d_%d" % (s0 + i, k2)])
                self.dma("sp", bcol[e % 2][:], self.bgu[l, e], w=["bcol%d" % (e % 2)])
                self.load_row("sp", bd[e % 2][:], self.b_down[l, e:e + 1, :], "bd%d" % (e % 2))

            units = [(e, r0, nr) for e in range(NE) for (r0, nr) in blocks]
            NU = len(units)
            cnt = dict(m=0, y=0)

            def loadX(n):
                e, r0, nr = units[n]
                xi = n % 2
                g0 = e * CAP + r0
                self.dma("pool", X[xi][:, 0:nr // 128, :], self.XG[g0:g0 + nr, :].rearrange("(j p) d -> p j d", p=128), w=["X%d" % xi])

            def phT(n):
                e, r0, nr = units[n]
                xi = n % 2
                for k in range(8):
                    tp = self.ps[k % 2][:, 0:256].bitcast(BF16); tpk = "ps%d" % (k % 2)
                    for j in range(nr // 128):
                        self.tr(tp[:, j * 128:(j + 1) * 128], X[xi][:, j, k * 128:(k + 1) * 128], self.c_identb[:], r=["X%d" % xi], w=[tpk])
                    if k % 2 == 0:
                        self.act(XT[xi][:, k, 0:nr], tp[:, 0:nr], AF.Identity, r=[tpk], w=["XT%d_%d" % (xi, k)])
                    else:
                        self.cp("dve", XT[xi][:, k, 0:nr], tp[:, 0:nr], r=[tpk], w=["XT%d_%d" % (xi, k)])

            def phGU(n):
                e, r0, nr = units[n]
                xi = n % 2
                s0 = (e % 2) * 3
                Wg, Wu = W[s0], W[s0 + 1]
                wkeys = lambda i: ["W%d_0" % (s0 + i), "W%d_1" % (s0 + i)]
                bc = bcol[e % 2]; bck = "bcol%d" % (e % 2)
                xtk = ["XT%d_%d" % (xi, k) for k in range(8)]

                def tail(m, ti):
                    t_ = tm[ti]
                    self.tt("pool", t_["ssb"][:, 0:nr], t_["gsb"][:, 0:nr], t_["ssb"][:, 0:nr], ALU.mult, r=["gsb%d" % ti, "ssb%d" % ti], w=["ssb%d" % ti])
                    self.stt(AT[xi][:, m, 0:nr], t_["u1"][:, 0:nr], 1.0, t_["ssb"][:, 0:nr], ALU.add, ALU.mult, r=["u1%d" % ti, "ssb%d" % ti], w=["AT%d_%d" % (xi, m)])
                ti = 0
                for m in range(8):
                    ti = cnt["m"] % 2; cnt["m"] += 1
                    t_ = tm[ti]
                    gp = self.ps[2 + ti]; gpk = "ps%d" % (2 + ti)
                    up = self.ps[4 + ti]; upk = "ps%d" % (4 + ti)
                    for k in range(8):
                        self.mm(gp[:, 0:nr], Wg[:, k, m * 128:(m + 1) * 128], XT[xi][:, k, 0:nr], k == 0, k == 7, r=wkeys(0) + [xtk[k]], w=[gpk])
                    for k in range(8):
                        self.mm(up[:, 0:nr], Wu[:, k, m * 128:(m + 1) * 128], XT[xi][:, k, 0:nr], k == 0, k == 7, r=wkeys(1) + [xtk[k]], w=[upk])
                    self.ts("dve", t_["gsb"][:, 0:nr], gp[:, 0:nr], bc[:, m:m + 1], ALU.add, 7.0, ALU.min, r=[gpk, bck], w=["gsb%d" % ti])
                    self.act(t_["u1"][:, 0:nr], up[:, 0:nr], AF.Identity, r=[upk, bck], w=["u1%d" % ti], bias=bc[:, 8 + m:9 + m])
                    self.act(t_["ssb"][:, 0:nr], t_["gsb"][:, 0:nr], AF.Sigmoid, r=["gsb%d" % ti], w=["ssb%d" % ti], scale=1.702)
                    self.ts("dve", t_["u1"][:, 0:nr], t_["u1"][:, 0:nr], 7.0, ALU.min, -7.0, ALU.max, r=["u1%d" % ti], w=["u1%d" % ti])
                    if m > 0:
                        tail(m - 1, 1 - ti)
                tail(7, ti)

            def phY(n):
                e, r0, nr = units[n]
                xi = n % 2
                s0 = (e % 2) * 3
                Wd = W[s0 + 2]
                wk_ = ["W%d_0" % (s0 + 2), "W%d_1" % (s0 + 2)]
                atk = ["AT%d_%d" % (xi, k) for k in range(8)]
                g0 = e * CAP + r0
                for j in range(nr // 128):
                    yi = cnt["y"] % 2; cnt["y"] += 1
                    for hh in range(2):
                        yp = self.ps[6 + hh]; ypk = "ps%d" % (6 + hh)
                        for k in range(8):
                            self.mm(yp[:], AT[xi][:, k, j * 128:(j + 1) * 128], Wd[:, k, hh * 512:(hh + 1) * 512], k == 0, k == 7, r=wk_ + [atk[k]], w=[ypk])
                        self.tt("dve", Ysb[yi][:, hh * 512:(hh + 1) * 512], yp[:], bd[e % 2][:, hh * 512:(hh + 1) * 512], ALU.add,
                                r=[ypk, "bd%d" % (e % 2)], w=["Ysb%d_%d" % (yi, hh)])
                    self.dma("sp", self.YG[g0 + j * 128:g0 + (j + 1) * 128, :], Ysb[yi][:], r=["Ysb%d_0" % yi, "Ysb%d_1" % yi], w=["YG"])

            load_w(0)
            loadX(0)
            if NU > 1:
                loadX(1)
            phT(0)
            for n in range(NU):
                e, r0, nr = units[n]
                if r0 == 0 and e + 1 < NE:
                    load_w(e + 1)
                phGU(n)
                if n + 1 < NU:
                    phT(n + 1)
                if n + 2 < NU:
                    loadX(n + 2)
                phY(n)
            self.end_stage()

    def st_ln2(self, l):
        nc, P = self.nc, self.P
        last = (l == self.layers - 1)
        with ExitStack() as st:
            A = lambda n, s, dt=F32: st.enter_context(nc.sbuf_tensor(self.uid(n), list(s), dt))
            g2p = A("g2p", [128, D]); lg_ = A("lg2", [128, D]); lb_ = A("lb2", [128, D])
            self.load_row("sp", g2p[:], self.MODT[l:l + 1, 40:48, :].rearrange("a j p -> a (j p)"), "g2p")
            self.load_row("sp", lg_[:], self.ln2_g[l:l + 1, :], "lg2")
            self.load_row("sp", lb_[:], self.ln2_b[l:l + 1, :], "lb2")
            self.ts("dve", g2p[:], g2p[:], 1.0, ALU.add, r=["g2p"], w=["g2p"])
            T = []
            NB = 4
            for i in range(NB):
                d = {n: A("%s%d" % (n, i), [128, D]) for n in ("x", "Y0", "Y1", "Y2", "Y3", "z", "xh")}
                d["st"] = A("st%d" % i, [128, 12]); d["mv"] = A("mv%d" % i, [128, 2]); d["sd"] = A("sd%d" % i, [128, 3])
                T.append(d)
            def loads2(t):
                i_ = t % NB
                self.dma("sp", T[i_]["x"][:], self.XS[t * 128:(t + 1) * 128, :], w=["x%d" % i_])
                for k in range(4):
                    P.dma("pool", lambda E, o=T[i_]["Y%d" % k][:], ix=self.slot[:, t, k:k + 1], i_=self.YG: E.indirect_dma_start(
                        out=o, out_offset=None, in_=i_, in_offset=bass.IndirectOffsetOnAxis(ap=ix, axis=0),
                        bounds_check=self.bc_reg(E), oob_is_err=False), reads=[], writes=["Y%d%d" % (k, i_)])
            loads2(0); loads2(1)
            for t in range(NT):
                if t + 2 < NT:
                    loads2(t + 2)
                i = t % NB; d = T[i]
                kx = lambda nm: "%s%d" % (nm, i)
                rows = slice(t * 128, (t + 1) * 128)
                self.ts("dve", d["z"][:], d["Y0"][:], self.wk[:, t, 0:1], ALU.mult, r=[kx("Y0")], w=[kx("z")])
                for k in range(1, 4):
                    self.stt(d["z"][:], d["Y%d" % k][:], self.wk[:, t, k:k + 1], d["z"][:], ALU.mult, ALU.add, r=[kx("Y%d" % k), kx("z")], w=[kx("z")])
                self.tt("pool", d["z"][:], d["z"][:], g2p[:], ALU.mult, r=[kx("z"), "g2p"], w=[kx("z")])
                self.stt(d["z"][:], d["x"][:], ALPHA, d["z"][:], ALU.mult, ALU.add, r=[kx("x"), kx("z")], w=[kx("z")])
                self.layer_norm(d["z"], kx("z"), d["xh"][:], kx("xh"), d, kx("m"))
                self.tt("dve", d["xh"][:], d["xh"][:], lg_[:], ALU.mult, r=[kx("xh"), "lg2"], w=[kx("xh")])
                self.tt("pool", d["xh"][:], d["xh"][:], lb_[:], ALU.add, r=[kx("xh"), "lb2"], w=[kx("xh")])
                dst = self.out if (last and self.upto == "all" and self.layers == DEPTH) else self.XS
                self.dma("sp", dst[rows, :], d["xh"][:], r=[kx("xh")], w=["XSw"])
            self.end_stage()


def make_consts():
    c = np.zeros((128, NCST), np.float32)
    c[:, 0:128] = np.eye(128, dtype=np.float32)
    c[:, 128:256] = 1.0
    s = np.arange(128)[:, None]; t = np.arange(128)[None, :]
    c[:, 256:384] = (s <= t)
    c[:, 384:512] = (s < t)
    m = np.ones(512, np.float32); m[0::64] = 0.0
    c[:, 512:1024] = m[None, :]
    c[:, 1024:1056] = (np.arange(NE, dtype=np.float32) * CAP)[None, :]
    u = ((s <= t) & ((s // 64) == (t // 64))).astype(np.uint32)
    return c, u


def host_inputs(inp, b, layers=DEPTH, experts=True):
    f = lambda a: np.ascontiguousarray(a, dtype=np.float32)
    c, u = make_consts()
    L = layers
    m = {
        "x": f(inp["x"][b]),
        "cT": f(inp["c"][b].reshape(8, 128).T),
        "w_in": inp["w_in"][:L],
        "ffb": f(inp["fox_f_bias"].T),
        "lbl": f(inp["hg_lb_logits"].reshape(DEPTH, 8, 128).transpose(2, 0, 1)),
        "normw": f(inp["hg_norm_w"].T),
        "w_branch": inp["w_branch"][:L], "w_out": inp["w_out"][:L], "ada_w": inp["ada_w"][:L],
        "adaB": f(inp["ada_b"].reshape(DEPTH, 48, 128).transpose(2, 0, 1)),
        "ln1_g": f(inp["ln1_g"]), "ln1_b": f(inp["ln1_b"]), "ln2_g": f(inp["ln2_g"]), "ln2_b": f(inp["ln2_b"]),
        "w_router": f(inp["w_router"]), "b_router": f(inp["b_router"]),
        "bgu": f(np.concatenate([inp["b_gate"].reshape(DEPTH, NE, 8, 128).transpose(0, 1, 3, 2),
                                 inp["b_up"].reshape(DEPTH, NE, 8, 128).transpose(0, 1, 3, 2)], axis=3)),
        "b_down": f(inp["b_down"]),
        "cst": c, "cstu": u,
    }
    if experts:
        m["w_gate"] = inp["w_gate"][:L]; m["w_up"] = inp["w_up"][:L]; m["w_down"] = inp["w_down"][:L]
    return m


_CACHE = {}


def kernel(**inputs):
    inp = {k: np.asarray(v) for k, v in inputs.items()}
    if "nc" not in _CACHE:
        _CACHE["nc"] = K().nc
    nc = _CACHE["nc"]
    shared = host_inputs(inp, 0)
    in_maps = []
    for b in range(8):
        m = dict(shared)
        m["x"] = np.ascontiguousarray(inp["x"][b], dtype=np.float32)
        m["cT"] = np.ascontiguousarray(inp["c"][b].reshape(8, 128).T, dtype=np.float32)
        in_maps.append(m)
    res = run_bass_kernel_spmd(nc, in_maps, core_ids=list(range(8)))
    return np.stack([r["out"] for r in res.results], axis=0).astype(np.float32)
```

```python
from contextlib import ExitStack
import numpy as np
import concourse.bass as bass
import concourse.mybir as mybir
from concourse.bass_utils import run_bass_kernel_spmd

F32 = mybir.dt.float32
F32R = mybir.dt.float32r
BF16 = mybir.dt.bfloat16
I32 = mybir.dt.int32
U32 = mybir.dt.uint32
AF = mybir.ActivationFunctionType
ALU = mybir.AluOpType

S = 4096
D = 1024
DEPTH = 4
NT = S // 128
HG_H = 8
FX_H = 16
NE = 32
PIN = 9232
ALPHA = float((2 * DEPTH) ** 0.25)
LN_EPS = 1e-5
RMS_EPS = 1e-6
CT = 7
CAP = CT * 128
DYN_SKIP = True
NCST = 1056
OFF = dict(hq=0, hf=1024, hi=2048, hg=3072, aq=4096, ak=5120, av=6144, af=7168, ga=7184, gb=8208)

ENGS = ("pe", "act", "dve", "pool", "sp")
NDSEM = 8


class Prog:
    def __init__(self, nc, stack):
        self.nc = nc
        self.sem = {n: stack.enter_context(nc.semaphore("s_" + n)) for n in ENGS}
        self.dsem = {}
        for q in ("sp", "act", "pool"):
            for i in range(NDSEM):
                self.dsem[(q, i)] = stack.enter_context(nc.semaphore("d_%s%d" % (q, i)))
        self.cnt = {k: 0 for k in list(self.sem) + list(self.dsem)}
        self.seen = {n: {} for n in ENGS}
        self.last_w = {}
        self.readers = {}
        self.q = {n: [] for n in ENGS}
        self.dma_i = {"sp": 0, "act": 0, "pool": 0}
        self.dma_pending = {}
        self.n_inst = 0
        self.in_region = False
        self.cregs = {}
        self.creg_owner = {}

    def all_sems(self):
        return list(self.sem.values()) + list(self.dsem.values())

    def _deps(self, reads, writes):
        deps = []
        for k in reads:
            if k in self.last_w:
                deps.append(self.last_w[k])
        for k in writes:
            if k in self.last_w:
                deps.append(self.last_w[k])
            deps.extend(self.readers.get(k, ()))
        return deps

    def _commit(self, stamp, reads, writes):
        for k in reads:
            self.readers.setdefault(k, []).append(stamp)
        for k in writes:
            self.last_w[k] = stamp
            self.readers[k] = []

    def _waits(self, eng, deps):
        need = {}
        for (sk, v) in deps:
            if sk == "pe" and eng == "pe":
                continue
            if self.seen[eng].get(sk, 0) >= v:
                continue
            if need.get(sk, 0) < v:
                need[sk] = v
        for sk, v in need.items():
            self.seen[eng][sk] = v
        return list(need.items())

    def _semh(self, sk):
        return self.sem[sk] if isinstance(sk, str) else self.dsem[sk]

    def op(self, eng, fn, reads=(), writes=()):
        deps = self._deps(reads, writes)
        waits = self._waits(eng, deps)
        self.cnt[eng] += 1
        stamp = (eng, self.cnt[eng])
        self._commit(stamp, reads, writes)
        semh = self.sem[eng]
        wl = [(self._semh(sk), v) for sk, v in waits]

        def run(E, fn=fn, wl=wl, semh=semh):
            for h, v in wl:
                E.wait_ge(h, v)
            fn(E).then_inc(semh, 1)
        if self.in_region:
            self.reg_ops[eng].append(run)
            self._reg_note(eng, eng, self.cnt[eng] - 1, 1, waits)
        else:
            self.q[eng].append(run)
        self.n_inst += 1
        return stamp

    def dma(self, q, fn, reads=(), writes=()):
        i = self.dma_i[q] % NDSEM
        self.dma_i[q] += 1
        sk = (q, i)
        deps = self._deps(reads, writes)
        if sk in self.dma_pending:
            deps.append(self.dma_pending[sk])
        waits = self._waits(q, deps)
        self.cnt[sk] += 16
        stamp = (sk, self.cnt[sk])
        self.dma_pending[sk] = stamp
        self._commit(stamp, reads, writes)
        semh = self.dsem[sk]
        wl = [(self._semh(s), v) for s, v in waits]

        def run(E, fn=fn, wl=wl, semh=semh):
            for h, v in wl:
                E.wait_ge(h, v)
            fn(E).then_inc(semh, 16)
        if self.in_region:
            self.reg_ops[q].append(run)
            self._reg_note(q, sk, self.cnt[sk] - 16, 16, waits)
        else:
            self.q[q].append(run)
        self.n_inst += 1
        return stamp

    def _reg_note(self, eng, sk, before, inc, waits):
        d = self.reg_inc[eng].setdefault(sk, [before, 0])
        d[1] += inc
        for wsk, v in waits:
            if self.reg_wait[eng].get(wsk, 0) < v:
                self.reg_wait[eng][wsk] = v

    def begin_region(self):
        self.reg_ops = {e: [] for e in ENGS}
        self.reg_inc = {e: {} for e in ENGS}
        self.reg_wait = {e: {} for e in ENGS}
        self.in_region = True

    def end_region(self, cond_key, cond_ap, thr):
        self.in_region = False
        for eng in ENGS:
            ops = self.reg_ops[eng]
            if not ops:
                continue
            incs = [(self._semh(sk), b, i) for sk, (b, i) in self.reg_inc[eng].items()]
            waits = [(self._semh(sk), v) for sk, v in self.reg_wait[eng].items()]
            slot = cond_key % 3
            own = self.creg_owner.setdefault(eng, {})
            need_load = own.get(slot) != cond_key
            own[slot] = cond_key

            def run(E, ops=ops, incs=incs, waits=waits, eng=eng, slot=slot, need_load=need_load):
                if (eng, slot) not in self.cregs:
                    self.cregs[(eng, slot)] = E.alloc_register("creg_%s%d" % (eng, slot))
                reg = self.cregs[(eng, slot)]
                if need_load:
                    E.reg_load(reg, cond_ap)
                with E.If_lt(reg, thr + 1):
                    for h, b, i in incs:
                        if b > 0:
                            E.wait_ge(h, b)
                        E.sem_inc(h, i)
                    for h, v in waits:
                        E.wait_ge(h, v)
                with E.Else():
                    for r in ops:
                        r(E)
            self.q[eng].append(run)

    def barrier(self):
        for eng in ENGS:
            wl = []
            for sk, v in self.cnt.items():
                if v == 0 or sk == eng:
                    continue
                if self.seen[eng].get(sk, 0) >= v:
                    continue
                self.seen[eng][sk] = v
                wl.append((self._semh(sk), v))
            if wl:
                def run(E, wl=wl):
                    for h, v in wl:
                        E.wait_ge(h, v)
                self.q[eng].append(run)
        self.last_w.clear()
        self.readers.clear()

    def flush(self, block):
        for eng, dec in (("sp", block.sync), ("pe", block.tensor), ("act", block.scalar),
                         ("dve", block.vector), ("pool", block.gpsimd)):
            lst = self.q[eng]
            if not lst:
                continue

            def body(E, lst=lst):
                for r in lst:
                    r(E)
            dec(body)
            self.q[eng] = []


STAGES = ["proj", "hgrn", "fox", "merge", "ln1", "experts", "ln2"]


class K:
    def __init__(self, layers=DEPTH, upto="all", dbg=()):
        self.layers = L = layers
        self.upto = upto
        self.nst = len(STAGES) if upto == "all" else STAGES.index(upto) + 1
        nc = self.nc = bass.Bass("TRN2", target_bir_lowering=False)
        IN = lambda n, s, dt=F32: nc.dram_tensor(n, list(s), dt, kind="ExternalInput").ap()
        SC = lambda n, s, dt=F32: nc.dram_tensor(n, list(s), dt, kind=("ExternalOutput" if n in dbg else "Internal")).ap()
        self.x_in = IN("x", [S, D])
        self.cT = IN("cT", [128, 8])
        self.w_in = IN("w_in", [L, D, PIN])
        self.ffb = IN("ffb", [16, DEPTH])
        self.lbl = IN("lbl", [128, DEPTH, 8])
        self.normw = IN("normw", [128, DEPTH])
        self.w_branch = IN("w_branch", [L, 2 * D, D])
        self.w_out = IN("w_out", [L, D, D])
        self.ada_w = IN("ada_w", [L, D, 6 * D])
        self.adaB = IN("adaB", [128, DEPTH, 48])
        self.ln1_g = IN("ln1_g", [DEPTH, D]); self.ln1_b = IN("ln1_b", [DEPTH, D])
        self.ln2_g = IN("ln2_g", [DEPTH, D]); self.ln2_b = IN("ln2_b", [DEPTH, D])
        self.w_router = IN("w_router", [DEPTH, D, NE])
        self.b_router = IN("b_router", [DEPTH, NE])
        if self.nst >= 6:
            self.w_gate = IN("w_gate", [L, NE, D, D])
            self.w_up = IN("w_up", [L, NE, D, D])
            self.w_down = IN("w_down", [L, NE, D, D])
        self.bgu = IN("bgu", [DEPTH, NE, 128, 16])
        self.b_down = IN("b_down", [DEPTH, NE, D])
        self.cst = IN("cst", [128, NCST])
        self.cstu = IN("cstu", [128, 128], U32)
        self.out = nc.dram_tensor("out", [S, D], F32, kind="ExternalOutput").ap()
        self.XS = SC("XS", [S, D])
        self.MODT = SC("MODT", [DEPTH, 48, 128])
        self.PT = {n: SC("PT_" + n, [D, S]) for n in ("hq", "hf", "hg", "aq", "ak", "ga", "gb")}
        self.PT["af"] = SC("PT_af", [16, S])
        self.PR = {n: SC("PR_" + n, [S, D]) for n in ("hi", "av")}
        self.OAT = SC("OAT", [D, S])
        self.OBT = SC("OBT", [D, S])
        self.YS = SC("YS", [S, D])
        self.XG = SC("XG", [NE * CAP, D])
        self.YG = SC("YG", [NE * CAP, D])
        self.CNT = SC("CNT", [1, NE], I32)
        self.build()

    def dma(self, q, out, in_, r=(), w=(), **kw):
        self.P.dma(q, lambda E, o=out, i=in_, kw=kw: E.dma_start(out=o, in_=i, **kw), reads=r, writes=w)

    def act(self, out, in_, func, r=(), w=(), **kw):
        self.P.op("act", lambda E, o=out, i=in_, f=func, kw=kw: E.activation(out=o, in_=i, func=f, **kw), reads=r, writes=w)

    def mm(self, out, lhsT, rhs, start, stop, r=(), w=()):
        self.P.op("pe", lambda E, o=out, a=lhsT, b=rhs, s=start, t=stop: E.matmul(o, a, b, start=s, stop=t), reads=r, writes=w)

    def tr(self, out, in_, ident, r=(), w=()):
        self.P.op("pe", lambda E, o=out, i=in_, d=ident: E.transpose(o, i, d), reads=r, writes=w)

    def tt(self, eng, out, in0, in1, op, r=(), w=()):
        self.P.op(eng, lambda E, o=out, a=in0, b=in1, p=op: E.tensor_tensor(out=o, in0=a, in1=b, op=p), reads=r, writes=w)

    def ts(self, eng, out, in0, s1, op0, s2=None, op1=None, r=(), w=()):
        if op1 is None:
            self.P.op(eng, lambda E, o=out, a=in0, s1=s1, p0=op0: E.tensor_scalar(out=o, in0=a, scalar1=s1, scalar2=None, op0=p0), reads=r, writes=w)
        else:
            self.P.op(eng, lambda E, o=out, a=in0, s1=s1, s2=s2, p0=op0, p1=op1: E.tensor_scalar(out=o, in0=a, scalar1=s1, scalar2=s2, op0=p0, op1=p1), reads=r, writes=w)

    def stt(self, out, in0, scalar, in1, op0, op1, r=(), w=()):
        self.P.op("dve", lambda E, o=out, a=in0, s=scalar, b=in1, p0=op0, p1=op1: E.scalar_tensor_tensor(out=o, in0=a, scalar=s, in1=b, op0=p0, op1=p1), reads=r, writes=w)

    def cp(self, eng, out, in_, r=(), w=()):
        self.P.op(eng, lambda E, o=out, i=in_: E.tensor_copy(out=o, in_=i), reads=r, writes=w)

    def ms(self, eng, out, val, w=()):
        self.P.op(eng, lambda E, o=out, v=val: E.memset(o, v), writes=w)

    def build(self):
        nc = self.nc
        with ExitStack() as st:
            P = self.P = Prog(nc, st)
            with nc.Block() as b0:
                @b0.sync
                def _(E):
                    for h in P.all_sems():
                        E.sem_clear(h)
            self.ps = [st.enter_context(nc.psum_tensor("psb%d" % i, [128, 512], F32)) for i in range(8)]
            A = lambda n, s, dt=F32: st.enter_context(nc.sbuf_tensor(self.uid(n), list(s), dt))
            self.c_cst = A("c_cst", [128, NCST])
            self.c_ident = self.c_cst[:, 0:128]
            self.c_ones = self.c_cst[:, 128:256]
            self.c_identb = A("c_identb", [128, 128], BF16)
            self.c_onesb = A("c_onesb", [128, 128], BF16)
            self.c_trib = A("c_trib", [128, 128], BF16)
            self.c_strb = A("c_strb", [128, 128], BF16)
            self.c_onesr = A("c_onesr", [128, 128])
            self.c_bdc = A("c_bdc", [128, 128], U32)
            self.c_eps = A("c_eps", [128, 2])
            self.modc = A("modc", [128, DEPTH, 48])
            self.sc1p = A("sc1p", [128, DEPTH, 8])
            self.sc2p = A("sc2p", [128, DEPTH, 8])
            self.lbc = A("lbc", [128, DEPTH, 8])
            self.oml = A("oml", [128, DEPTH, 8])
            self.noml = A("noml", [128, DEPTH, 8])
            self.nw = A("nw", [128, DEPTH])
            self.fb = A("fb", [16, DEPTH])
            self.slot = A("slot", [128, NT, 4], I32)
            self.wk = A("wk", [128, NT, 4])
            with nc.Block() as blk:
                self.blk = blk
                self.stage0()
                fns = [self.st_proj, self.st_hgrn, self.st_fox, self.st_merge, self.st_ln1, self.st_experts, self.st_ln2]
                for l in range(self.layers):
                    for f in fns[:self.nst]:
                        f(l)
                P.barrier()
                P.flush(blk)

    def bc_reg(self, E):
        if getattr(self, "_bc", None) is None:
            self._bc = E.to_reg(NE * CAP - 1)
        return self._bc

    def uid(self, n):
        self._uid = getattr(self, "_uid", 0) + 1
        return "%s_u%d" % (n, self._uid)

    def end_stage(self):
        self.P.barrier()
        self.P.flush(self.blk)

    def stage0(self):
        nc, P = self.nc, self.P
        with ExitStack() as st:
            A = lambda n, s, dt=F32: st.enter_context(nc.sbuf_tensor(self.uid(n), list(s), dt))
            self.dma("sp", self.c_cst[:], self.cst, w=["cst"])
            self.dma("sp", self.c_bdc[:], self.cstu, w=["bdc"])
            self.dma("sp", self.nw[:], self.normw, w=["nw"])
            self.dma("sp", self.fb[:], self.ffb, w=["fb"])
            self.cp("dve", self.c_identb[:], self.c_ident, r=["cst"], w=["identb"])
            self.cp("dve", self.c_onesb[:], self.c_ones, r=["cst"], w=["onesb"])
            self.cp("dve", self.c_trib[:], self.c_cst[:, 256:384], r=["cst"], w=["trib"])
            self.cp("dve", self.c_strb[:], self.c_cst[:, 384:512], r=["cst"], w=["strb"])
            self.act(self.c_onesr[:].bitcast(F32R), self.c_ones, AF.Identity, r=["cst"], w=["onesr"])
            self.ms("dve", self.c_eps[:, 0:1], LN_EPS, w=["eps"])
            self.ms("dve", self.c_eps[:, 1:2], RMS_EPS, w=["eps"])
            for i in range(8):
                self.dma("sp" if i % 2 == 0 else "act", self.XS[i * 512:(i + 1) * 512, :], self.x_in[i * 512:(i + 1) * 512, :], w=["XS"])
            lg = A("lg", [128, DEPTH, 8]); ex = A("ex", [128, DEPTH, 8]); mx = A("mx", [128, 8]); sm = A("sm", [128, 8])
            self.dma("sp", lg[:], self.lbl, w=["lg"])
            self.tt("dve", mx[:], lg[:, 0, :], lg[:, 1, :], ALU.max, r=["lg"], w=["mx"])
            for l in (2, 3):
                self.tt("dve", mx[:], mx[:], lg[:, l, :], ALU.max, r=["lg", "mx"], w=["mx"])
            for l in range(DEPTH):
                self.tt("dve", lg[:, l, :], lg[:, l, :], mx[:], ALU.subtract, r=["lg", "mx"], w=["lg"])
            self.act(ex[:], lg[:], AF.Exp, r=["lg"], w=["ex"])
            self.tt("dve", sm[:], ex[:, 0, :], ex[:, 1, :], ALU.add, r=["ex"], w=["sm"])
            for l in (2, 3):
                self.tt("dve", sm[:], sm[:], ex[:, l, :], ALU.add, r=["ex", "sm"], w=["sm"])
            P.op("dve", lambda E: E.reciprocal(out=sm[:], in_=sm[:]), reads=["sm"], writes=["sm"])
            for l in range(DEPTH):
                self.tt("dve", ex[:, l, :], ex[:, l, :], sm[:], ALU.mult, r=["ex", "sm"], w=["ex"])
            self.ms("dve", self.lbc[:, 0, :], 0.0, w=["lbc"])
            self.cp("dve", self.lbc[:, 1, :], ex[:, 1, :], r=["ex"], w=["lbc"])
            for l in (2, 3):
                self.tt("dve", self.lbc[:, l, :], self.lbc[:, l - 1, :], ex[:, l, :], ALU.add, r=["ex", "lbc"], w=["lbc"])
            self.ts("dve", self.lbc[:], self.lbc[:], 0.0, ALU.max, 1.0, ALU.min, r=["lbc"], w=["lbc"])
            self.ts("dve", self.oml[:], self.lbc[:], -1.0, ALU.mult, 1.0, ALU.add, r=["lbc"], w=["oml"])
            self.ts("dve", self.noml[:], self.lbc[:], 1.0, ALU.mult, -1.0, ALU.add, r=["lbc"], w=["noml"])
            ct = A("ct", [128, 8]); ca = A("ca", [128, 8])
            self.dma("sp", ct[:], self.cT, w=["ct"])
            self.act(ca[:], ct[:], AF.Silu, r=["ct"], w=["ca"])
            self.dma("sp", self.modc[:], self.adaB, w=["modc_b"])
            wbuf = [A("adaw%d" % i, [128, 8, 1024]) for i in range(2)]
            mps = self.ps[0]
            n = 0
            for l in range(self.layers):
                for g in range(6):
                    wb = wbuf[n % 2]; key = "adaw%d" % (n % 2)
                    src = self.ada_w[l, :, g * 1024:(g + 1) * 1024].rearrange("(k p) f -> p k f", p=128)
                    self.dma("sp" if n % 2 == 0 else "act", wb[:], src, w=[key])
                    for m in range(8):
                        col = l * 48 + g * 8 + m
                        for k in range(8):
                            self.mm(mps[:, col:col + 1], wb[:, k, m * 128:(m + 1) * 128], ca[:, k:k + 1], k == 0, k == 7, r=[key, "ca"], w=["mps"])
                    n += 1
            L = self.layers
            self.tt("dve", self.modc[:, 0:L, :], self.modc[:, 0:L, :], mps[:, 0:L * 48].rearrange("p (l j) -> p l j", j=48), ALU.add,
                    r=["mps", "modc_b"], w=["modc"])
            self.ts("dve", self.sc1p[:, 0:L, :], self.modc[:, 0:L, 8:16], 1.0, ALU.add, r=["modc"], w=["sc1p"])
            self.ts("dve", self.sc2p[:, 0:L, :], self.modc[:, 0:L, 32:40], 1.0, ALU.add, r=["modc"], w=["sc2p"])
            mt = A("mt", [48, 128])
            for l in range(L):
                tp = self.ps[1]
                self.tr(tp[0:48, 0:128], self.modc[:, l, :], self.c_ident, r=["modc", "cst"], w=["tp"])
                self.act(mt[:], tp[0:48, 0:128], AF.Identity, r=["tp"], w=["mt"])
                self.dma("sp", self.MODT[l], mt[:], r=["mt"], w=["MODT"])
            self.end_stage()

    def st_proj(self, l):
        nc, P = self.nc, self.P
        HALF = S // 2
        NG = HALF // 512
        with ExitStack() as st:
            A = lambda n, s, dt=F32: st.enter_context(nc.sbuf_tensor(self.uid(n), list(s), dt))
            hT = A("hT", [128, 8, HALF])
            xb = [A("xb%d" % i, [128, 4, D]) for i in range(2)]
            wb = [A("wb%d" % i, [128, 8, 512]) for i in range(2)]
            sg = [A("sg%d" % i, [128, HALF]) for i in range(2)]
            so = [A("so%d" % i, [128, 512]) for i in range(2)]
            cnt = dict(w=0, sg=0, so=0, ps=0)

            def nxt(k, n):
                v = cnt[k] % n; cnt[k] += 1
                return v

            def load_w(c0, ncol):
                i = nxt("w", 2)
                src = self.w_in[l, :, c0:c0 + ncol].rearrange("(k p) f -> p k f", p=128)
                self.dma("pool", wb[i][:, :, 0:ncol].bitcast(F32R), src, w=["wb%d" % i])
                return wb[i], "wb%d" % i

            for half in range(2):
                t0 = half * HALF
                for g in range(NG):
                    xi = g % 2
                    src = self.XS[t0 + g * 512:t0 + (g + 1) * 512, :].rearrange("(j p) d -> p j d", p=128)
                    self.dma("sp", xb[xi][:], src, w=["xb%d" % xi])
                    for k in range(8):
                        pi = nxt("ps", 4); pb = self.ps[pi]; pk = "ps%d" % pi
                        for j in range(4):
                            self.tr(pb[:, j * 128:(j + 1) * 128], xb[xi][:, j, k * 128:(k + 1) * 128], self.c_ident, r=["xb%d" % xi], w=[pk])
                        self.act(hT[:, k, g * 512:(g + 1) * 512].bitcast(F32R), pb[:], AF.Identity, r=[pk], w=["hT%d" % g],
                                 scale=self.sc1p[:, l, k:k + 1], bias=self.modc[:, l, k:k + 1])
                fm = [("hq", AF.Silu), ("hf", AF.Sigmoid), ("hg", AF.Silu), ("aq", AF.Identity), ("ak", AF.Identity),
                      ("ga", AF.Sigmoid), ("gb", AF.Sigmoid)]
                for name, fn in fm:
                    for c4 in range(2):
                        w, wkey = load_w(OFF[name] + c4 * 512, 512)
                        for cc in range(4):
                            si = nxt("sg", 2)
                            for g in range(NG):
                                pi = nxt("ps", 4); pb = self.ps[pi]; pk = "ps%d" % pi
                                for k in range(8):
                                    self.mm(pb[:], w[:, k, cc * 128:(cc + 1) * 128].bitcast(F32R), hT[:, k, g * 512:(g + 1) * 512].bitcast(F32R),
                                            k == 0, k == 7, r=[wkey, "hT%d" % g], w=[pk])
                                self.act(sg[si][:, g * 512:(g + 1) * 512], pb[:], fn, r=[pk], w=["sg%d_%d" % (si, g)])
                            r0 = c4 * 512 + cc * 128
                            self.dma("sp", self.PT[name][r0:r0 + 128, t0:t0 + HALF], sg[si][:], r=["sg%d_%d" % (si, g) for g in range(NG)], w=["PT_" + name])
                w, wkey = load_w(OFF["af"], 16)
                si = nxt("sg", 2)
                for g in range(NG):
                    pi = nxt("ps", 4); pb = self.ps[pi]; pk = "ps%d" % pi
                    for k in range(8):
                        self.mm(pb[0:16, :], w[:, k, 0:16].bitcast(F32R), hT[:, k, g * 512:(g + 1) * 512].bitcast(F32R), k == 0, k == 7,
                                r=[wkey, "hT%d" % g], w=[pk])
                    self.act(sg[si][0:16, g * 512:(g + 1) * 512], pb[0:16, :], AF.Identity, r=[pk], w=["sg%d_%d" % (si, g)])
                self.dma("sp", self.PT["af"][:, t0:t0 + HALF], sg[si][0:16, :], r=["sg%d_%d" % (si, g) for g in range(NG)], w=["PT_af"])
                for name in ("hi", "av"):
                    for c4 in range(2):
                        w, wkey = load_w(OFF[name] + c4 * 512, 512)
                        for tt_ in range(HALF // 128):
                            g = tt_ // 4
                            pi = nxt("ps", 4); pb = self.ps[pi]; pk = "ps%d" % pi
                            for k in range(8):
                                self.mm(pb[:], hT[:, k, tt_ * 128:(tt_ + 1) * 128].bitcast(F32R), w[:, k, :].bitcast(F32R), k == 0, k == 7,
                                        r=[wkey, "hT%d" % g], w=[pk])
                            oi = nxt("so", 2)
                            self.cp("dve", so[oi][:], pb[:], r=[pk], w=["so%d" % oi])
                            r0 = t0 + tt_ * 128
                            self.dma("sp", self.PR[name][r0:r0 + 128, c4 * 512:(c4 + 1) * 512], so[oi][:], r=["so%d" % oi], w=["PR_" + name])
            self.end_stage()

    def st_hgrn(self, l):
        nc, P = self.nc, self.P
        NCH = 3
        with ExitStack() as st:
            A = lambda n, s, dt=F32: st.enter_context(nc.sbuf_tensor(self.uid(n), list(s), dt))
            Sst = [A("S%d" % h, [128, 128]) for h in range(HG_H)]
            Sbf = [A("Sb%d" % h, [128, 128], BF16) for h in range(HG_H)]
            ATm = [[A("ATm%d_%d" % (c, i), [128, 128], BF16) for i in range(2)] for c in range(NCH)]
            khT = [[A("khT%d_%d" % (c, i), [128, 128], BF16) for i in range(2)] for c in range(NCH)]
            ptn = ["SQ", "SF", "kk", "gg", "BB", "D1", "D4", "E1", "E2", "E3", "E4"]
            PTs = [{n: A("%s_%d" % (n, i), [128, 512]) for n in ptn} for i in range(2)]
            US = []
            for i in range(2 * NCH):
                d = {n: A("%s_%d" % (n, i), [128, 512], BF16) for n in ("qt", "kt", "qh", "kh")}
                d["dec"] = A("dec_%d" % i, [128, 8])
                for n in ("I", "IA", "IB"):
                    d[n] = A("%s_%d" % (n, i), [128, 4, 128], BF16)
                d["SG"] = A("SG_%d" % i, [128, 512]); d["Ob"] = A("Ob_%d" % i, [128, 512])
                US.append(d)
            EP = [{n: A("%s_%d" % (n, i), [128, 512]) for n in ("sq", "rs", "on")} for i in range(2)]
            for h in range(HG_H):
                self.ms("dve", Sst[h][:], 0.0, w=["S%d" % h])
                self.ms("pool", Sbf[h][:], 0.0, w=["Sb%d" % h])
            for c in range(NCH):
                for i in range(2):
                    self.ms("pool", ATm[c][i][:], 0.0, w=["ATm%d_%d" % (c, i)])
            for i in range(2 * NCH):
                self.ms("pool", US[i]["IA"][:], 0.0, w=["IA_%d" % i])
                self.ms("pool", US[i]["IB"][:], 0.0, w=["IB_%d" % i])
            scanmask = self.c_cst[:, 512:1024]
            NU = (S // 512) * HG_H

            def unit(n):
                bi, h = divmod(n, HG_H)
                return bi, h, slice(bi * 512, (bi + 1) * 512), slice(h * 128, (h + 1) * 128)

            def prologue(n):
                bi, h, cols, rows = unit(n)
                t = PTs[n % 2]; u = US[n % (2 * NCH)]
                tk = lambda nm: "%s_%d" % (nm, n % 2)
                uk = lambda nm: "%s_%d" % (nm, n % (2 * NCH))
                self.dma("sp", t["SQ"][:], self.PT["hq"][rows, cols], w=[tk("SQ")])
                self.dma("sp", t["SF"][:], self.PT["hf"][rows, cols], w=[tk("SF")])
                self.dma("sp", u["SG"][:], self.PT["hg"][rows, cols], w=[uk("SG")])
                isrc = self.PR["hi"][cols, rows].rearrange("(j p) v -> p j v", p=128)
                self.dma("pool", u["I"][:], isrc, w=[uk("I")])
                self.dma("pool", u["IA"][0:64, :, :], isrc[0:64], w=[uk("IA")])
                self.dma("pool", u["IB"][64:128, :, :], isrc[64:128], w=[uk("IB")])
                oml = self.oml[:, l, h:h + 1]; noml = self.noml[:, l, h:h + 1]; lb = self.lbc[:, l, h:h + 1]
                self.ts("dve", t["kk"][:], t["SF"][:], noml, ALU.mult, oml, ALU.add, r=[tk("SF")], w=[tk("kk")])
                self.act(t["gg"][:], t["SF"][:], AF.Ln, r=[tk("SF")], w=[tk("gg")], scale=oml, bias=lb)
                P.op("dve", lambda E, o=t["BB"][:], m=scanmask, g=t["gg"][:]: E.tensor_tensor_scan(out=o, data0=m, data1=g, initial=0.0, op0=ALU.mult, op1=ALU.add),
                     reads=[tk("gg")], writes=[tk("BB")])
                B3 = t["BB"][:].rearrange("p (c t) -> p c t", t=64)
                self.tt("pool", t["D1"][:].rearrange("p (c t) -> p c t", t=64), B3, B3[:, :, 31:32].to_broadcast([128, 8, 64]), ALU.subtract, r=[tk("BB")], w=[tk("D1")])
                self.tt("pool", t["D4"][:].rearrange("p (c t) -> p c t", t=64), B3, B3[:, :, 63:64].to_broadcast([128, 8, 64]), ALU.subtract, r=[tk("BB")], w=[tk("D4")])
                self.act(t["E1"][:], t["D1"][:], AF.Exp, r=[tk("D1")], w=[tk("E1")])
                self.act(t["E2"][:], t["D1"][:], AF.Exp, r=[tk("D1")], w=[tk("E2")], scale=-1.0)
                self.act(t["E3"][:], t["BB"][:], AF.Exp, r=[tk("BB")], w=[tk("E3")])
                self.act(t["E4"][:], t["D4"][:], AF.Exp, r=[tk("D4")], w=[tk("E4")], scale=-1.0)
                self.tt("dve", u["qt"][:], t["SQ"][:], t["E1"][:], ALU.mult, r=[tk("SQ"), tk("E1")], w=[uk("qt")])
                self.tt("pool", u["kt"][:], t["kk"][:], t["E2"][:], ALU.mult, r=[tk("kk"), tk("E2")], w=[uk("kt")])
                self.tt("dve", u["qh"][:], t["SQ"][:], t["E3"][:], ALU.mult, r=[tk("SQ"), tk("E3")], w=[uk("qh")])
                self.tt("pool", u["kh"][:], t["kk"][:], t["E4"][:], ALU.mult, r=[tk("kk"), tk("E4")], w=[uk("kh")])
                self.cp("dve", u["dec"][:], t["E3"][:].rearrange("p (c t) -> p c t", t=64)[:, :, 63], r=[tk("E3")], w=[uk("dec")])

            def tile_steps(n, c, j):
                bi, h, cols, rows = unit(n)
                u = US[n % (2 * NCH)]
                uk = lambda nm: "%s_%d" % (nm, n % (2 * NCH))
                jp = j % 2
                c0 = j * 128
                at_ps = self.ps[0][:, c * 128:(c + 1) * 128]; atk = "ps0"
                kh_ps = self.ps[1 + 2 * c][:, 0:64].bitcast(BF16); khk = "ps%d" % (1 + 2 * c)
                su = self.ps[1 + 2 * c][:, 128:256]; suk = khk
                o_ps = self.ps[2 + 2 * c][:, 0:128]; ok = "ps%d" % (2 + 2 * c)
                AT = ATm[c][jp]; ATk = "ATm%d_%d" % (c, jp)
                KT = khT[c][jp]; KTk = "khT%d_%d" % (c, jp)
                Sk, Sbk = "S%d" % h, "Sb%d" % h

                def s1():
                    self.mm(at_ps, u["kt"][:, c0:c0 + 128], u["qt"][:, c0:c0 + 128], True, True, r=[uk("kt"), uk("qt")], w=[atk])
                    self.tr(kh_ps, u["kh"][:, c0:c0 + 128], self.c_identb[:], r=[uk("kh")], w=[khk])

                def s2():
                    P.op("dve", lambda E, o=AT[:], m=self.c_bdc[:], d=at_ps: E.copy_predicated(out=o, mask=m, data=d), reads=[atk], writes=[ATk])
                    self.act(KT[:], kh_ps, AF.Identity, r=[khk], w=[KTk])

                def s3():
                    self.mm(o_ps, u["I"][:, j, :], AT[:], True, False, r=[uk("I"), ATk], w=[ok])
                    self.mm(o_ps[:, 0:64], Sbf[h][:], u["qh"][:, c0:c0 + 64], False, False, r=[Sbk, uk("qh")], w=[ok])
                    self.mm(su, KT[:], u["IA"][:, j, :], True, True, r=[KTk, uk("IA")], w=[suk])

                def s4():
                    self.stt(Sbf[h][:], Sst[h][:], u["dec"][:, 2 * j:2 * j + 1], su, ALU.mult, ALU.add, r=[suk, uk("dec"), Sk], w=[Sbk])
                    self.stt(Sst[h][:], Sst[h][:], u["dec"][:, 2 * j:2 * j + 1], su, ALU.mult, ALU.add, r=[suk, uk("dec"), Sk], w=[Sk])

                def s5():
                    self.mm(o_ps[:, 64:128], Sbf[h][:], u["qh"][:, c0 + 64:c0 + 128], False, True, r=[Sbk, uk("qh")], w=[ok])
                    self.mm(su, KT[:], u["IB"][:, j, :], True, True, r=[KTk, uk("IB")], w=[suk])

                def s6():
                    self.act(u["Ob"][:, c0:c0 + 128], o_ps, AF.Identity, r=[ok], w=[uk("Ob") + "_%d" % j])
                    self.stt(Sbf[h][:], Sst[h][:], u["dec"][:, 2 * j + 1:2 * j + 2], su, ALU.mult, ALU.add, r=[suk, uk("dec"), Sk], w=[Sbk])
                    self.stt(Sst[h][:], Sst[h][:], u["dec"][:, 2 * j + 1:2 * j + 2], su, ALU.mult, ALU.add, r=[suk, uk("dec"), Sk], w=[Sk])
                return [s1, s2, s3, s4, s5, s6]

            def epilogue_steps(n, c):
                bi, h, cols, rows = unit(n)
                u = US[n % (2 * NCH)]; e = EP[c % 2]
                uk = lambda nm: "%s_%d" % (nm, n % (2 * NCH))
                ek = lambda nm: "%s_%d" % (nm, c % 2)
                obk = [uk("Ob") + "_%d" % j for j in range(4)]

                def e1():
                    self.act(e["sq"][:].bitcast(F32R), u["Ob"][:], AF.Square, r=obk, w=[ek("sq")])

                def e2():
                    self.mm(self.ps[7][:], self.c_onesr[:].bitcast(F32R), e["sq"][:].bitcast(F32R), True, True, r=[ek("sq")], w=["ps7"])

                def e3():
                    self.act(e["rs"][:], self.ps[7][:], AF.Ln, r=["ps7"], w=[ek("rs")], scale=1.0 / 128.0, bias=self.c_eps[:, 1:2])
                    self.act(e["rs"][:], e["rs"][:], AF.Exp, r=[ek("rs")], w=[ek("rs")], scale=-0.5)

                def e4():
                    self.tt("dve", e["on"][:], u["Ob"][:], e["rs"][:], ALU.mult, r=obk + [ek("rs")], w=[ek("on")])
                    self.stt(e["on"][:], e["on"][:], self.nw[:, l:l + 1], u["SG"][:], ALU.mult, ALU.mult, r=[ek("on"), uk("SG")], w=[ek("on")])
                    self.dma("sp", self.OAT[rows, cols], e["on"][:], r=[ek("on")], w=["OAT"])
                return [e1, e2, e3, e4]

            for n in range(min(NCH, NU)):
                prologue(n)
            for g0 in range(0, NU, NCH):
                grp = list(range(g0, min(g0 + NCH, NU)))
                for j in range(4):
                    steps = [tile_steps(n, c, j) for c, n in enumerate(grp)]
                    for si in range(6):
                        for stp in steps:
                            stp[si]()
                    nxt = g0 + NCH + j
                    if j < NCH and nxt < NU:
                        prologue(nxt)
                for c, n in enumerate(grp):
                    for stp in epilogue_steps(n, c):
                        stp()
            self.end_stage()

    def st_fox(self, l):
        nc, P = self.nc, self.P
        with ExitStack() as st:
            A = lambda n, s, dt=F32: st.enter_context(nc.sbuf_tensor(self.uid(n), list(s), dt))
            ft = A("ft", [16, S]); Fc = A("Fc", [16, S])
            Ftok = A("Ftok", [128, NT, 16]); FrefB = A("FrefB", [128, 16, 8]); rbd = A("rbd", [16, 16, 8])
            vaug = [A("vaug%d" % i, [128, NT, 128], BF16) for i in range(2)]
            QA = [A("QA%d" % i, [128, S], BF16) for i in range(2)]
            QB = [A("QB%d" % i, [128, S], BF16) for i in range(2)]
            KP = [A("KP%d" % i, [128, S], BF16) for i in range(2)]
            e64 = A("e64", [128, 128]); e64r = A("e64r", [128, 128])
            bias = [A("bias%d" % i, [128, NT, 8]) for i in range(2)]
            Pb = [A("Pb%d" % i, [128, 512], BF16) for i in range(4)]
            Osb = [A("Osb%d" % i, [64, 512]) for i in range(2)]
            rc = [A("rc%d" % i, [128, 512]) for i in range(2)]
            zz = A("zz", [128, 512])
            ob = [A("ob%d" % i, [64, 512]) for i in range(2)]
            for i in range(2):
                self.ms("pool", vaug[i][:], 0.0, w=["vaug%d" % i])
                self.ms("pool", vaug[i][:, :, 64:65], 1.0, w=["vaug%d" % i])
                self.ms("pool", QA[i][64:128, :], 0.0, w=["QA%d" % i])
                self.ms("pool", QB[i][0:64, :], 0.0, w=["QB%d" % i])
            self.ms("dve", zz[:], 0.0, w=["zz"])
            self.ms("dve", e64[:], 0.0, w=["e64"])
            self.ms("dve", e64[64:65, :], 1.0, w=["e64"])
            self.act(e64r[:].bitcast(F32R), e64[:], AF.Identity, r=["e64"], w=["e64r"])
            for i in range(2):
                self.act(rc[i][:].bitcast(F32R), zz[:], AF.Identity, r=["zz"], w=["rc%d" % i])
            self.dma("sp", ft[:], self.PT["af"], w=["ft"])
            self.act(ft[:], ft[:], AF.Sigmoid, r=["ft"], w=["ft"], bias=self.fb[:, l:l + 1])
            self.act(ft[:], ft[:], AF.Ln, r=["ft"], w=["ft"])
            P.op("dve", lambda E: E.tensor_tensor_scan(out=Fc[:], data0=self.c_ones[0:16, 0:1].to_broadcast([16, S]), data1=ft[:], initial=0.0, op0=ALU.mult, op1=ALU.add),
                 reads=["ft"], writes=["Fc"])
            for i in range(NT):
                self.tr(self.ps[0][:, i * 16:(i + 1) * 16], Fc[0:16, i * 128:(i + 1) * 128], self.c_ident[0:16, 0:16], r=["Fc"], w=["ps0"])
            self.act(Ftok[:].rearrange("p i h -> p (i h)"), self.ps[0][:], AF.Identity, r=["ps0"], w=["Ftok"])
            fref = Fc[:].rearrange("h (j t) -> h j t", t=512)[:, :, 256]
            self.tt("dve", rbd[:], self.c_ident[0:16, 0:16].unsqueeze(2).to_broadcast([16, 16, 8]), fref.unsqueeze(1).to_broadcast([16, 16, 8]), ALU.mult,
                    r=["Fc"], w=["rbd"])
            self.mm(self.ps[1][:, 0:128], self.c_ones[0:16, :], rbd[:].rearrange("h a j -> h (a j)"), True, True, r=["rbd"], w=["ps1"])
            self.act(FrefB[:].rearrange("p a j -> p (a j)"), self.ps[1][:, 0:128], AF.Identity, r=["ps1"], w=["FrefB"])
            LA = 2
            its = []
            for h in range(FX_H):
                for j in range(S // 512):
                    n_i = 4 * j + 4
                    for i in range(n_i):
                        its.append((h, j, i, n_i))
            hstate = {}

            def head_loads(h):
                par = h % 2; pp = (h // 2) % 2
                rows = slice(h * 64, (h + 1) * 64)
                if h % 2 == 0:
                    self.dma("pool", KP[pp][:], self.PT["ak"][h * 64:(h + 2) * 64, :], w=["KP%d" % pp])
                    self.dma("pool", QA[pp][0:64, :], self.PT["aq"][rows, :], w=["QA%d" % pp])
                    qsrc = QA[pp]; qkey = "QA%d" % pp
                else:
                    self.dma("pool", QB[pp][64:128, :], self.PT["aq"][rows, :], w=["QB%d" % pp])
                    qsrc = QB[pp]; qkey = "QB%d" % pp
                self.dma("pool", vaug[par][:, :, 0:64], self.PR["av"][:, rows].rearrange("(i p) v -> p i v", p=128), w=["vaug%d" % par])
                self.tt("dve", bias[par][:], FrefB[:, h, :].unsqueeze(1).to_broadcast([128, NT, 8]), Ftok[:, :, h:h + 1].to_broadcast([128, NT, 8]), ALU.subtract,
                        r=["FrefB", "Ftok"], w=["bias%d" % par])
                hstate[h] = (par, pp, qsrc, qkey, rows)

            def emit_S(n):
                h, j, i, n_i = its[n]
                if h not in hstate:
                    head_loads(h)
                par, pp, qsrc, qkey, rows = hstate[h]
                cs = max(i - 4 * j, 0) * 128
                si = n % 4
                self.mm(self.ps[si][:, cs:512], KP[pp][:, i * 128:(i + 1) * 128], qsrc[:, j * 512 + cs:(j + 1) * 512], True, True,
                        r=["KP%d" % pp, qkey], w=["ps%d" % si])

            deferred = []

            def run_deferred(force=False):
                for d_ in list(deferred):
                    d_[0] -= 1
                    if d_[0] <= 0 or force:
                        d_[1]()
                        deferred.remove(d_)

            nblk = 0
            for n in range(min(LA, len(its))):
                emit_S(n)
            for n in range(len(its)):
                h, j, i, n_i = its[n]
                par, pp, qsrc, qkey, rows = hstate[h]
                if i == 0 and j == 0 and h + 1 < FX_H and (h + 1) not in hstate:
                    head_loads(h + 1)
                if n + LA < len(its):
                    emit_S(n + LA)
                r_ = i - 4 * j
                cs = max(r_, 0) * 128
                si = n % 4; skey = "ps%d" % si
                if i == 0:
                    oi = nblk % 2; nblk += 1
                o_ps = self.ps[4 + oi]; okey = "ps%d" % (4 + oi)
                pi = n % 4; pkey = "Pb%d" % pi
                pt = Pb[pi][:, cs:512]
                self.act(pt, self.ps[si][:, cs:512], AF.Exp, r=[skey, "bias%d" % par], w=[pkey], scale=0.125, bias=bias[par][:, i, j:j + 1])
                if r_ >= 0:
                    self.tt("pool", Pb[pi][:, cs:cs + 128], Pb[pi][:, cs:cs + 128], self.c_trib[:], ALU.mult, r=[pkey], w=[pkey])
                self.mm(o_ps[:, cs:512], vaug[par][:, i, :], pt, i == 0, i == n_i - 1, r=["vaug%d" % par, pkey], w=[okey])
                run_deferred()
                if i == n_i - 1:
                    self.cp("dve", Osb[oi][:], o_ps[0:64, :], r=[okey], w=["Osb%d" % oi])

                    def _rcp(E, o=rc[oi][64:65, :].bitcast(F32R), i_=o_ps[64:65, :]):
                        with nc.allow_low_precision(reason="fp32r operand for the broadcast matmul"):
                            return E.reciprocal(out=o, in_=i_)
                    P.op("dve", _rcp, reads=[okey], writes=["rc%d" % oi])

                    def part_b(oi=oi, rows=rows, j=j):
                        self.mm(self.ps[6][:], e64r[:].bitcast(F32R), rc[oi][:].bitcast(F32R), True, True, r=["rc%d" % oi, "e64r"], w=["ps6"])
                        self.tt("dve", ob[oi][:], Osb[oi][:], self.ps[6][0:64, :], ALU.mult, r=["Osb%d" % oi, "ps6"], w=["ob%d" % oi])
                        self.dma("sp", self.OBT[rows, j * 512:(j + 1) * 512], ob[oi][:], r=["ob%d" % oi], w=["OBT"])
                    deferred.append([3, part_b])
            run_deferred(force=True)
            self.end_stage()

    def st_merge(self, l):
        nc, P = self.nc, self.P
        with ExitStack() as st:
            A = lambda n, s, dt=F32: st.enter_context(nc.sbuf_tensor(self.uid(n), list(s), dt))
            wbr = A("wbr", [128, 16, D], BF16)
            wo = A("wo", [128, 8, D], BF16)
            inb = []
            for i in range(2):
                inb.append({n: A("%s%d" % (n, i), [128, 8, 512], BF16) for n in ("oa", "ob", "ga", "gb")})
            mg = [A("mg%d" % i, [128, 8, 512], BF16) for i in range(2)]
            t1 = [A("t1_%d" % i, [128, 512]) for i in range(2)]
            t2 = [A("t2_%d" % i, [128, 512]) for i in range(2)]
            ysb = [A("ysb%d" % i, [128, D]) for i in range(2)]
            for k2 in range(4):
                self.dma("pool", wbr[:, k2 * 4:(k2 + 1) * 4, :], self.w_branch[l, k2 * 512:(k2 + 1) * 512, :].rearrange("(k p) d -> p k d", p=128), w=["wbr"])
            for k2 in range(2):
                self.dma("pool", wo[:, k2 * 4:(k2 + 1) * 4, :], self.w_out[l, k2 * 512:(k2 + 1) * 512, :].rearrange("(k p) d -> p k d", p=128), w=["wo"])
            nt_ = 0; ny = 0
            for bi in range(S // 512):
                par = bi % 2
                cols = slice(bi * 512, (bi + 1) * 512)
                ib = inb[par]
                for nm, src in (("oa", self.OAT), ("ob", self.OBT), ("ga", self.PT["ga"]), ("gb", self.PT["gb"])):
                    self.dma("pool", ib[nm][:], src[:, cols].rearrange("(k p) t -> p k t", p=128), w=["%s%d" % (nm, par)])
                for m in range(8):
                    pa = self.ps[(2 * m) % 4]; pak = "ps%d" % ((2 * m) % 4)
                    pb_ = self.ps[(2 * m + 1) % 4]; pbk = "ps%d" % ((2 * m + 1) % 4)
                    for k in range(8):
                        self.mm(pa[:], wbr[:, k, m * 128:(m + 1) * 128], ib["oa"][:, k, :], k == 0, k == 7, r=["wbr", "oa%d" % par], w=[pak])
                    for k in range(8):
                        self.mm(pb_[:], wbr[:, 8 + k, m * 128:(m + 1) * 128], ib["ob"][:, k, :], k == 0, k == 7, r=["wbr", "ob%d" % par], w=[pbk])
                    ti = nt_ % 2; nt_ += 1
                    self.tt("dve", t1[ti][:], pa[:], ib["ga"][:, m, :], ALU.mult, r=[pak, "ga%d" % par], w=["t1_%d" % ti])
                    self.tt("dve", t2[ti][:], pb_[:], ib["gb"][:, m, :], ALU.mult, r=[pbk, "gb%d" % par], w=["t2_%d" % ti])
                    self.tt("pool", mg[par][:, m, :], t1[ti][:], t2[ti][:], ALU.add, r=["t1_%d" % ti, "t2_%d" % ti], w=["mg%d_%d" % (par, m)])
                for r_ in range(4):
                    yi = ny % 2; ny += 1
                    for hh in range(2):
                        yp = self.ps[4 + 2 * yi + hh]; ypk = "ps%d" % (4 + 2 * yi + hh)
                        for k in range(8):
                            self.mm(yp[:], mg[par][:, k, r_ * 128:(r_ + 1) * 128], wo[:, k, hh * 512:(hh + 1) * 512], k == 0, k == 7,
                                    r=["mg%d_%d" % (par, k), "wo"], w=[ypk])
                        self.act(ysb[yi][:, hh * 512:(hh + 1) * 512], yp[:], AF.Identity, r=[ypk], w=["ysb%d_%d" % (yi, hh)])
                    r0 = bi * 512 + r_ * 128
                    self.dma("sp", self.YS[r0:r0 + 128, :], ysb[yi][:], r=["ysb%d_0" % yi, "ysb%d_1" % yi], w=["YS"])
            self.end_stage()

    def layer_norm(self, z, zk, xh, xhk, tmp, mvk):
        P = self.P
        stt_, mv, sd = tmp["st"], tmp["mv"], tmp["sd"]
        for c in range(2):
            P.op("dve", lambda E, o=stt_[:, c * 6:(c + 1) * 6], i=z[:, c * 512:(c + 1) * 512]: E.bn_stats(out=o, in_=i), reads=[zk], writes=[mvk + "st%d" % c])
        P.op("dve", lambda E, o=mv[:], i=stt_[:]: E.bn_aggr(out=o, in_=i), reads=[mvk + "st0", mvk + "st1"], writes=[mvk + "mv"])
        self.act(sd[:, 0:1], mv[:, 1:2], AF.Sqrt, r=[mvk + "mv"], w=[mvk + "sd"], bias=self.c_eps[:, 0:1])
        P.op("dve", lambda E, o=sd[:, 1:2], i=sd[:, 0:1]: E.reciprocal(out=o, in_=i), reads=[mvk + "sd"], writes=[mvk + "rstd"])
        self.ts("dve", sd[:, 2:3], mv[:, 0:1], sd[:, 1:2], ALU.mult, -1.0, ALU.mult, r=[mvk + "mv", mvk + "rstd"], w=[mvk + "nmr"])
        self.act(xh, z[:], AF.Identity, r=[zk, mvk + "rstd", mvk + "nmr"], w=[xhk], scale=sd[:, 1:2], bias=sd[:, 2:3])

    def load_row(self, q, dst, src_row, key):
        self.dma(q, dst, src_row.partition_broadcast(128)[:, 0, :], w=[key])

    def st_ln1(self, l):
        nc, P = self.nc, self.P
        with ExitStack() as st:
            A = lambda n, s, dt=F32: st.enter_context(nc.sbuf_tensor(self.uid(n), list(s), dt))
            g1p = A("g1p", [128, D]); lg_ = A("lg_", [128, D]); lb_ = A("lb_", [128, D]); s2p = A("s2p", [128, D]); sh2 = A("sh2", [128, D])
            wr = A("wr", [128, 8, NE]); br = A("br", [128, NE])
            csel = A("csel", [128, NE], BF16)
            modrow = lambda j0: self.MODT[l:l + 1, j0:j0 + 8, :].rearrange("a j p -> a (j p)")
            self.load_row("sp", g1p[:], modrow(16), "g1p")
            self.load_row("sp", sh2[:], modrow(24), "sh2")
            self.load_row("sp", s2p[:], modrow(32), "s2p")
            self.load_row("sp", lg_[:], self.ln1_g[l:l + 1, :], "lg_")
            self.load_row("sp", lb_[:], self.ln1_b[l:l + 1, :], "lb_")
            self.load_row("sp", br[:], self.b_router[l:l + 1, :], "br")
            self.dma("sp", wr[:], self.w_router[l].rearrange("(k p) e -> p k e", p=128), w=["wr"])
            self.ts("dve", g1p[:], g1p[:], 1.0, ALU.add, r=["g1p"], w=["g1p"])
            self.ts("dve", s2p[:], s2p[:], 1.0, ALU.add, r=["s2p"], w=["s2p"])
            self.ms("dve", csel[:], 0.0, w=["csel"])
            iotaC = self.c_cst[:, 1024:1056]
            T = []
            NB = 4
            for i in range(NB):
                d = {n: A("%s%d" % (n, i), [128, D]) for n in ("x", "y", "z", "xh", "x1", "h2")}
                d["h2T"] = A("h2T%d" % i, [128, 8, 128])
                d["st"] = A("st%d" % i, [128, 12]); d["mv"] = A("mv%d" % i, [128, 2]); d["sd"] = A("sd%d" % i, [128, 3])
                d["lgt"] = A("lgt%d" % i, [128, NE]); d["top"] = A("top%d" % i, [128, 8]); d["sel"] = A("sel%d" % i, [128, NE], BF16)
                d["SL"] = A("SL%d" % i, [128, NE]); d["oh"] = A("oh%d" % i, [128, NE]); d["sf"] = A("sf%d" % i, [128, 4])
                d["e4"] = A("e4%d" % i, [128, 4]); d["nt"] = A("nt%d" % i, [128, 2]); d["ovf"] = A("ovf%d" % i, [128, NE])
                T.append(d)
            def loads1(t):
                i_ = t % NB
                self.dma("sp", T[i_]["x"][:], self.XS[t * 128:(t + 1) * 128, :], w=["x%d" % i_])
                self.dma("sp", T[i_]["y"][:], self.YS[t * 128:(t + 1) * 128, :], w=["y%d" % i_])
            loads1(0); loads1(1)
            for t in range(NT):
                if t + 2 < NT:
                    loads1(t + 2)
                i = t % NB; d = T[i]
                kx = lambda nm: "%s%d" % (nm, i)
                rows = slice(t * 128, (t + 1) * 128)
                self.tt("dve", d["z"][:], d["y"][:], g1p[:], ALU.mult, r=[kx("y"), "g1p"], w=[kx("z")])
                self.stt(d["z"][:], d["x"][:], ALPHA, d["z"][:], ALU.mult, ALU.add, r=[kx("x"), kx("z")], w=[kx("z")])
                self.layer_norm(d["z"], kx("z"), d["xh"][:], kx("xh"), d, kx("m"))
                self.tt("dve", d["x1"][:], d["xh"][:], lg_[:], ALU.mult, r=[kx("xh"), "lg_"], w=[kx("x1")])
                self.tt("pool", d["x1"][:], d["x1"][:], lb_[:], ALU.add, r=[kx("x1"), "lb_"], w=[kx("x1")])
                self.dma("sp", self.XS[rows, :], d["x1"][:], r=[kx("x1")], w=["XSw"])
                self.tt("dve", d["h2"][:], d["x1"][:], s2p[:], ALU.mult, r=[kx("x1"), "s2p"], w=[kx("h2")])
                self.tt("pool", d["h2"][:], d["h2"][:], sh2[:], ALU.add, r=[kx("h2"), "sh2"], w=[kx("h2")])
                if self.nst < 6:
                    continue
                for hh in range(2):
                    pb = self.ps[hh]; pk = "ps%d" % hh
                    for k4 in range(4):
                        k = hh * 4 + k4
                        self.tr(pb[:, k4 * 128:(k4 + 1) * 128], d["h2"][:, k * 128:(k + 1) * 128], self.c_ident, r=[kx("h2")], w=[pk])
                    self.act(d["h2T"][:, hh * 4:(hh + 1) * 4, :].rearrange("p k t -> p (k t)"), pb[:], AF.Identity, r=[pk], w=[kx("h2T") + "_%d" % hh])
                lps = self.ps[2][:, 0:NE]
                for k in range(8):
                    self.mm(lps, d["h2T"][:, k, :], wr[:, k, :], k == 0, k == 7, r=[kx("h2T") + "_%d" % (k // 4), "wr"], w=["ps2"])
                self.tt("dve", d["lgt"][:], lps, br[:], ALU.add, r=["ps2", "br"], w=[kx("lgt")])
                P.op("dve", lambda E, o=d["top"][:], i_=d["lgt"][:]: E.max(out=o, in_=i_), reads=[kx("lgt")], writes=[kx("top")])
                self.ts("dve", d["sel"][:], d["lgt"][:], d["top"][:, 3:4], ALU.is_ge, r=[kx("lgt"), kx("top")], w=[kx("sel")])
                pps = self.ps[3][:, 0:NE]
                self.mm(pps, self.c_strb[:], d["sel"][:], True, False, r=[kx("sel")], w=["ps3"])
                self.mm(pps, self.c_onesb[:], csel[:], False, True, r=["csel"], w=["ps3"])
                self.tt("pool", csel[:], csel[:], d["sel"][:], ALU.add, r=["csel", kx("sel")], w=["csel"])
                self.ts("dve", d["ovf"][:], pps, float(CAP), ALU.is_ge, 1.0e7, ALU.mult, r=["ps3"], w=[kx("ovf")])
                self.tt("dve", d["SL"][:], pps, iotaC, ALU.add, r=["ps3"], w=[kx("SL")])
                self.tt("dve", d["SL"][:], d["SL"][:], d["ovf"][:], ALU.add, r=[kx("SL"), kx("ovf")], w=[kx("SL")])
                for k in range(4):
                    self.ts("dve", d["oh"][:], d["lgt"][:], d["top"][:, k:k + 1], ALU.is_equal, r=[kx("lgt"), kx("top")], w=[kx("oh")])
                    self.tt("dve", d["oh"][:], d["oh"][:], d["SL"][:], ALU.mult, r=[kx("oh"), kx("SL")], w=[kx("oh")])
                    P.op("dve", lambda E, o=d["sf"][:, k:k + 1], i_=d["oh"][:]: E.tensor_reduce(out=o, in_=i_, axis=mybir.AxisListType.X, op=ALU.add),
                         reads=[kx("oh")], writes=[kx("sf")])
                self.cp("dve", self.slot[:, t, :], d["sf"][:], r=[kx("sf")], w=["slot%d" % t])
                self.ts("dve", d["nt"][:, 0:1], d["top"][:, 0:1], -1.0, ALU.mult, r=[kx("top")], w=[kx("nt")])
                self.act(d["e4"][:], d["top"][:, 0:4], AF.Exp, r=[kx("top"), kx("nt")], w=[kx("e4")], bias=d["nt"][:, 0:1])
                P.op("dve", lambda E, o=d["nt"][:, 1:2], i_=d["e4"][:]: E.tensor_reduce(out=o, in_=i_, axis=mybir.AxisListType.X, op=ALU.add),
                     reads=[kx("e4")], writes=[kx("nt") + "s"])
                P.op("dve", lambda E, o=d["nt"][:, 1:2]: E.reciprocal(out=o, in_=o), reads=[kx("nt") + "s"], writes=[kx("nt") + "s"])
                self.ts("dve", self.wk[:, t, :], d["e4"][:], d["nt"][:, 1:2], ALU.mult, r=[kx("e4"), kx("nt") + "s"], w=["wk%d" % t])
                for k in range(4):
                    P.dma("pool", lambda E, o=self.XG, ix=self.slot[:, t, k:k + 1], i_=d["h2"][:]: E.indirect_dma_start(
                        out=o, out_offset=bass.IndirectOffsetOnAxis(ap=ix, axis=0), in_=i_, in_offset=None,
                        bounds_check=self.bc_reg(E), oob_is_err=False), reads=[kx("h2"), "slot%d" % t], writes=["XG"])
            if self.nst >= 6:
                cnt_sb = A("cnt_sb", [128, NE], I32)
                cps = self.ps[2][:, 0:NE]
                self.mm(cps, self.c_onesb[:], csel[:], True, True, r=["csel"], w=["ps2"])
                self.cp("dve", cnt_sb[:], cps, r=["ps2"], w=["cnt_sb"])
                self.dma("sp", self.CNT, cnt_sb[0:1, :], r=["cnt_sb"], w=["CNT"])
            self.end_stage()

    def st_experts(self, l):
        nc, P = self.nc, self.P
        blocks = []
        r = 0
        for nr in (512, 384, 128):
            if r < CAP:
                blocks.append((r, min(nr, CAP - r)))
                r += min(nr, CAP - r)
        assert r == CAP
        with ExitStack() as st:
            A = lambda n, s, dt=F32: st.enter_context(nc.sbuf_tensor(self.uid(n), list(s), dt))
            W = [A("W%d" % i, [128, 8, D], BF16) for i in range(6)]
            bcol = [A("bcol%d" % i, [128, 16]) for i in range(2)]
            bd = [A("bd%d" % i, [128, D]) for i in range(2)]
            X = [A("X%d" % i, [128, 4, D], BF16) for i in range(2)]
            XT = [A("XT%d" % i, [128, 8, 512], BF16) for i in range(2)]
            AT = [A("AT%d" % i, [128, 8, 512], BF16) for i in range(2)]
            tm = [{n: A("%s%d" % (n, i), [128, 512]) for n in ("gsb", "ssb", "u1")} for i in range(2)]
            Ysb = [A("Ysb%d" % i, [128, D]) for i in range(2)]

            def load_w(e):
                s0 = (e % 2) * 3
                for i, src in enumerate((self.w_gate, self.w_up, self.w_down)):
                    for k2 in range(2):
                        self.dma("pool", W[s0 + i][:, k2 * 4:(k2 + 1) * 4, :], src[l, e, k2 * 512:(k2 + 1) * 512, :].rearrange("(k p) f -> p k f", p=128),
                                 w=["W%d_%d" % (s0 + i, k2)])
                self.dma("sp", bcol[e % 2][:], self.bgu[l, e], w=["bcol%d" % (e % 2)])
                self.load_row("sp", bd[e % 2][:], self.b_down[l, e:e + 1, :], "bd%d" % (e % 2))

            units = [(e, r0, nr) for e in range(NE) for (r0, nr) in blocks]
            NU = len(units)
            cnt = dict(m=0, y=0)

            def loadX(n):
                e, r0, nr = units[n]
                xi = n % 2
                g0 = e * CAP + r0
                self.dma("pool", X[xi][:, 0:nr // 128, :], self.XG[g0:g0 + nr, :].rearrange("(j p) d -> p j d", p=128), w=["X%d" % xi])

            def phT(n):
                e, r0, nr = units[n]
                xi = n % 2
                for k in range(8):
                    tp = self.ps[k % 2][:, 0:256].bitcast(BF16); tpk = "ps%d" % (k % 2)
                    for j in range(nr // 128):
                        self.tr(tp[:, j * 128:(j + 1) * 128], X[xi][:, j, k * 128:(k + 1) * 128], self.c_identb[:], r=["X%d" % xi], w=[tpk])
                    if k % 2 == 0:
                        self.act(XT[xi][:, k, 0:nr], tp[:, 0:nr], AF.Identity, r=[tpk], w=["XT%d_%d" % (xi, k)])
                    else:
                        self.cp("dve", XT[xi][:, k, 0:nr], tp[:, 0:nr], r=[tpk], w=["XT%d_%d" % (xi, k)])

            def phGU(n):
                e, r0, nr = units[n]
                xi = n % 2
                s0 = (e % 2) * 3
                Wg, Wu = W[s0], W[s0 + 1]
                wkeys = lambda i: ["W%d_0" % (s0 + i), "W%d_1" % (s0 + i)]
                bc = bcol[e % 2]; bck = "bcol%d" % (e % 2)
                xtk = ["XT%d_%d" % (xi, k) for k in range(8)]

                def tail(m, ti):
                    t_ = tm[ti]
                    self.tt("pool", t_["ssb"][:, 0:nr], t_["gsb"][:, 0:nr], t_["ssb"][:, 0:nr], ALU.mult, r=["gsb%d" % ti, "ssb%d" % ti], w=["ssb%d" % ti])
                    self.stt(AT[xi][:, m, 0:nr], t_["u1"][:, 0:nr], 1.0, t_["ssb"][:, 0:nr], ALU.add, ALU.mult, r=["u1%d" % ti, "ssb%d" % ti], w=["AT%d_%d" % (xi, m)])
                ti = 0
                for m in range(8):
                    ti = cnt["m"] % 2; cnt["m"] += 1
                    t_ = tm[ti]
                    gp = self.ps[2 + ti]; gpk = "ps%d" % (2 + ti)
                    up = self.ps[4 + ti]; upk = "ps%d" % (4 + ti)
                    for k in range(8):
                        self.mm(gp[:, 0:nr], Wg[:, k, m * 128:(m + 1) * 128], XT[xi][:, k, 0:nr], k == 0, k == 7, r=wkeys(0) + [xtk[k]], w=[gpk])
                    for k in range(8):
                        self.mm(up[:, 0:nr], Wu[:, k, m * 128:(m + 1) * 128], XT[xi][:, k, 0:nr], k == 0, k == 7, r=wkeys(1) + [xtk[k]], w=[upk])
                    self.ts("dve", t_["gsb"][:, 0:nr], gp[:, 0:nr], bc[:, m:m + 1], ALU.add, 7.0, ALU.min, r=[gpk, bck], w=["gsb%d" % ti])
                    self.act(t_["u1"][:, 0:nr], up[:, 0:nr], AF.Identity, r=[upk, bck], w=["u1%d" % ti], bias=bc[:, 8 + m:9 + m])
                    self.act(t_["ssb"][:, 0:nr], t_["gsb"][:, 0:nr], AF.Sigmoid, r=["gsb%d" % ti], w=["ssb%d" % ti], scale=1.702)
                    self.ts("dve", t_["u1"][:, 0:nr], t_["u1"][:, 0:nr], 7.0, ALU.min, -7.0, ALU.max, r=["u1%d" % ti], w=["u1%d" % ti])
                    if m > 0:
                        tail(m - 1, 1 - ti)
                tail(7, ti)

            def phY(n):
                e, r0, nr = units[n]
                xi = n % 2
                s0 = (e % 2) * 3
                Wd = W[s0 + 2]
                wk_ = ["W%d_0" % (s0 + 2), "W%d_1" % (s0 + 2)]
                atk = ["AT%d_%d" % (xi, k) for k in range(8)]
                g0 = e * CAP + r0
                for j in range(nr // 128):
                    yi = cnt["y"] % 2; cnt["y"] += 1
                    for hh in range(2):
                        yb = 6 + hh
                        yp = self.ps[yb]; ypk = "ps%d" % yb
                        for k in range(8):
                            self.mm(yp[:], AT[xi][:, k, j * 128:(j + 1) * 128], Wd[:, k, hh * 512:(hh + 1) * 512], k == 0, k == 7, r=wk_ + [atk[k]], w=[ypk])
                        self.tt("dve", Ysb[yi][:, hh * 512:(hh + 1) * 512], yp[:], bd[e % 2][:, hh * 512:(hh + 1) * 512], ALU.add,
                                r=[ypk, "bd%d" % (e % 2)], w=["Ysb%d_%d" % (yi, hh)])
                    self.dma("sp", self.YG[g0 + j * 128:g0 + (j + 1) * 128, :], Ysb[yi][:], r=["Ysb%d_0" % yi, "Ysb%d_1" % yi], w=["YG"])

            def cond(n, fn):
                e, r0, nr = units[n]
                if r0 == 0 or not DYN_SKIP:
                    fn(n)
                else:
                    P.begin_region()
                    fn(n)
                    P.end_region(l * NE + e, self.CNT[0:1, e:e + 1], r0)

            load_w(0)
            cond(0, loadX)
            if NU > 1:
                cond(1, loadX)
            cond(0, phT)
            for n in range(NU):
                e, r0, nr = units[n]
                if r0 == 0 and e + 1 < NE:
                    load_w(e + 1)
                cond(n, phGU)
                if n + 1 < NU:
                    cond(n + 1, phT)
                if n + 2 < NU:
                    cond(n + 2, loadX)
                cond(n, phY)
            self.end_stage()

    def st_ln2(self, l):
        nc, P = self.nc, self.P
        last = (l == self.layers - 1)
        with ExitStack() as st:
            A = lambda n, s, dt=F32: st.enter_context(nc.sbuf_tensor(self.uid(n), list(s), dt))
            g2p = A("g2p", [128, D]); lg_ = A("lg2", [128, D]); lb_ = A("lb2", [128, D])
            self.load_row("sp", g2p[:], self.MODT[l:l + 1, 40:48, :].rearrange("a j p -> a (j p)"), "g2p")
            self.load_row("sp", lg_[:], self.ln2_g[l:l + 1, :], "lg2")
            self.load_row("sp", lb_[:], self.ln2_b[l:l + 1, :], "lb2")
            self.ts("dve", g2p[:], g2p[:], 1.0, ALU.add, r=["g2p"], w=["g2p"])
            T = []
            NB = 4
            for i in range(NB):
                d = {n: A("%s%d" % (n, i), [128, D]) for n in ("x", "Y0", "Y1", "Y2", "Y3", "z", "xh")}
                d["st"] = A("st%d" % i, [128, 12]); d["mv"] = A("mv%d" % i, [128, 2]); d["sd"] = A("sd%d" % i, [128, 3])
                T.append(d)
            def loads2(t):
                i_ = t % NB
                self.dma("sp", T[i_]["x"][:], self.XS[t * 128:(t + 1) * 128, :], w=["x%d" % i_])
                for k in range(4):
                    P.dma("pool", lambda E, o=T[i_]["Y%d" % k][:], ix=self.slot[:, t, k:k + 1], i_=self.YG: E.indirect_dma_start(
                        out=o, out_offset=None, in_=i_, in_offset=bass.IndirectOffsetOnAxis(ap=ix, axis=0),
                        bounds_check=self.bc_reg(E), oob_is_err=False), reads=[], writes=["Y%d%d" % (k, i_)])
            loads2(0); loads2(1)
            for t in range(NT):
                if t + 2 < NT:
                    loads2(t + 2)
                i = t % NB; d = T[i]
                kx = lambda nm: "%s%d" % (nm, i)
                rows = slice(t * 128, (t + 1) * 128)
                self.act(d["z"][:], d["Y0"][:], AF.Identity, r=[kx("Y0")], w=[kx("z")], scale=self.wk[:, t, 0:1])
                for k in range(1, 4):
                    self.stt(d["z"][:], d["Y%d" % k][:], self.wk[:, t, k:k + 1], d["z"][:], ALU.mult, ALU.add, r=[kx("Y%d" % k), kx("z")], w=[kx("z")])
                self.tt("pool", d["z"][:], d["z"][:], g2p[:], ALU.mult, r=[kx("z"), "g2p"], w=[kx("z")])
                self.stt(d["z"][:], d["x"][:], ALPHA, d["z"][:], ALU.mult, ALU.add, r=[kx("x"), kx("z")], w=[kx("z")])
                self.layer_norm(d["z"], kx("z"), d["xh"][:], kx("xh"), d, kx("m"))
                self.tt("dve", d["xh"][:], d["xh"][:], lg_[:], ALU.mult, r=[kx("xh"), "lg2"], w=[kx("xh")])
                self.tt("pool", d["xh"][:], d["xh"][:], lb_[:], ALU.add, r=[kx("xh"), "lb2"], w=[kx("xh")])
                dst = self.out if (last and self.upto == "all" and self.layers == DEPTH) else self.XS
                self.dma("sp", dst[rows, :], d["xh"][:], r=[kx("xh")], w=["XSw"])
            self.end_stage()


def make_consts():
    c = np.zeros((128, NCST), np.float32)
    c[:, 0:128] = np.eye(128, dtype=np.float32)
    c[:, 128:256] = 1.0
    s = np.arange(128)[:, None]; t = np.arange(128)[None, :]
    c[:, 256:384] = (s <= t)
    c[:, 384:512] = (s < t)
    m = np.ones(512, np.float32); m[0::64] = 0.0
    c[:, 512:1024] = m[None, :]
    c[:, 1024:1056] = (np.arange(NE, dtype=np.float32) * CAP)[None, :]
    u = ((s <= t) & ((s // 64) == (t // 64))).astype(np.uint32)
    return c, u


def host_inputs(inp, b, layers=DEPTH, experts=True):
    f = lambda a: np.ascontiguousarray(a, dtype=np.float32)
    c, u = make_consts()
    L = layers
    m = {
        "x": f(inp["x"][b]),
        "cT": f(inp["c"][b].reshape(8, 128).T),
        "w_in": inp["w_in"][:L],
        "ffb": f(inp["fox_f_bias"].T),
        "lbl": f(inp["hg_lb_logits"].reshape(DEPTH, 8, 128).transpose(2, 0, 1)),
        "normw": f(inp["hg_norm_w"].T),
        "w_branch": inp["w_branch"][:L], "w_out": inp["w_out"][:L], "ada_w": inp["ada_w"][:L],
        "adaB": f(inp["ada_b"].reshape(DEPTH, 48, 128).transpose(2, 0, 1)),
        "ln1_g": f(inp["ln1_g"]), "ln1_b": f(inp["ln1_b"]), "ln2_g": f(inp["ln2_g"]), "ln2_b": f(inp["ln2_b"]),
        "w_router": f(inp["w_router"]), "b_router": f(inp["b_router"]),
        "bgu": f(np.concatenate([inp["b_gate"].reshape(DEPTH, NE, 8, 128).transpose(0, 1, 3, 2),
                                 inp["b_up"].reshape(DEPTH, NE, 8, 128).transpose(0, 1, 3, 2)], axis=3)),
        "b_down": f(inp["b_down"]),
        "cst": c, "cstu": u,
    }
    if experts:
        m["w_gate"] = inp["w_gate"][:L]; m["w_up"] = inp["w_up"][:L]; m["w_down"] = inp["w_down"][:L]
    return m


_CACHE = {}


def kernel(**inputs):
    inp = {k: np.asarray(v) for k, v in inputs.items()}
    if "nc" not in _CACHE:
        _CACHE["nc"] = K().nc
    nc = _CACHE["nc"]
    shared = host_inputs(inp, 0)
    in_maps = []
    for b in range(8):
        m = dict(shared)
        m["x"] = np.ascontiguousarray(inp["x"][b], dtype=np.float32)
        m["cT"] = np.ascontiguousarray(inp["c"][b].reshape(8, 128).T, dtype=np.float32)
        in_maps.append(m)
    res = run_bass_kernel_spmd(nc, in_maps, core_ids=list(range(8)))
    return np.stack([r["out"] for r in res.results], axis=0).astype(np.float32)
```

```python
from contextlib import ExitStack
import numpy as np
import concourse.bass as bass
import concourse.mybir as mybir
from concourse.bass_utils import run_bass_kernel_spmd

F32 = mybir.dt.float32
F32R = mybir.dt.float32r
BF16 = mybir.dt.bfloat16
I32 = mybir.dt.int32
U32 = mybir.dt.uint32
AF = mybir.ActivationFunctionType
ALU = mybir.AluOpType

S = 4096
D = 1024
DEPTH = 4
NT = S // 128
HG_H = 8
FX_H = 16
NE = 32
PIN = 9232
ALPHA = float((2 * DEPTH) ** 0.25)
LN_EPS = 1e-5
RMS_EPS = 1e-6
CT = 7
CAP = CT * 128
DYN_SKIP = True
NCST = 1056
OFF = dict(hq=0, hf=1024, hi=2048, hg=3072, aq=4096, ak=5120, av=6144, af=7168, ga=7184, gb=8208)

ENGS = ("pe", "act", "dve", "pool", "sp")
NDSEM = 8


class Prog:
    def __init__(self, nc, stack):
        self.nc = nc
        self.sem = {n: stack.enter_context(nc.semaphore("s_" + n)) for n in ENGS}
        self.dsem = {}
        for q in ("sp", "act", "pool"):
            for i in range(NDSEM):
                self.dsem[(q, i)] = stack.enter_context(nc.semaphore("d_%s%d" % (q, i)))
        self.cnt = {k: 0 for k in list(self.sem) + list(self.dsem)}
        self.seen = {n: {} for n in ENGS}
        self.last_w = {}
        self.readers = {}
        self.q = {n: [] for n in ENGS}
        self.dma_i = {"sp": 0, "act": 0, "pool": 0}
        self.dma_pending = {}
        self.n_inst = 0
        self.in_region = False
        self.cregs = {}
        self.creg_owner = {}

    def all_sems(self):
        return list(self.sem.values()) + list(self.dsem.values())

    def _deps(self, reads, writes):
        deps = []
        for k in reads:
            if k in self.last_w:
                deps.append(self.last_w[k])
        for k in writes:
            if k in self.last_w:
                deps.append(self.last_w[k])
            deps.extend(self.readers.get(k, ()))
        return deps

    def _commit(self, stamp, reads, writes):
        for k in reads:
            self.readers.setdefault(k, []).append(stamp)
        for k in writes:
            self.last_w[k] = stamp
            self.readers[k] = []

    def _waits(self, eng, deps):
        need = {}
        for (sk, v) in deps:
            if sk == "pe" and eng == "pe":
                continue
            if self.seen[eng].get(sk, 0) >= v:
                continue
            if need.get(sk, 0) < v:
                need[sk] = v
        for sk, v in need.items():
            self.seen[eng][sk] = v
        return list(need.items())

    def _semh(self, sk):
        return self.sem[sk] if isinstance(sk, str) else self.dsem[sk]

    def op(self, eng, fn, reads=(), writes=()):
        deps = self._deps(reads, writes)
        waits = self._waits(eng, deps)
        self.cnt[eng] += 1
        stamp = (eng, self.cnt[eng])
        self._commit(stamp, reads, writes)
        semh = self.sem[eng]
        wl = [(self._semh(sk), v) for sk, v in waits]

        def run(E, fn=fn, wl=wl, semh=semh):
            for h, v in wl:
                E.wait_ge(h, v)
            fn(E).then_inc(semh, 1)
        if self.in_region:
            self.reg_ops[eng].append(run)
            self._reg_note(eng, eng, self.cnt[eng] - 1, 1, waits)
        else:
            self.q[eng].append(run)
        self.n_inst += 1
        return stamp

    def dma(self, q, fn, reads=(), writes=()):
        i = self.dma_i[q] % NDSEM
        self.dma_i[q] += 1
        sk = (q, i)
        deps = self._deps(reads, writes)
        if sk in self.dma_pending:
            deps.append(self.dma_pending[sk])
        waits = self._waits(q, deps)
        self.cnt[sk] += 16
        stamp = (sk, self.cnt[sk])
        self.dma_pending[sk] = stamp
        self._commit(stamp, reads, writes)
        semh = self.dsem[sk]
        wl = [(self._semh(s), v) for s, v in waits]

        def run(E, fn=fn, wl=wl, semh=semh):
            for h, v in wl:
                E.wait_ge(h, v)
            fn(E).then_inc(semh, 16)
        if self.in_region:
            self.reg_ops[q].append(run)
            self._reg_note(q, sk, self.cnt[sk] - 16, 16, waits)
        else:
            self.q[q].append(run)
        self.n_inst += 1
        return stamp

    def _reg_note(self, eng, sk, before, inc, waits):
        d = self.reg_inc[eng].setdefault(sk, [before, 0])
        d[1] += inc
        for wsk, v in waits:
            if self.reg_wait[eng].get(wsk, 0) < v:
                self.reg_wait[eng][wsk] = v

    def begin_region(self):
        self.reg_ops = {e: [] for e in ENGS}
        self.reg_inc = {e: {} for e in ENGS}
        self.reg_wait = {e: {} for e in ENGS}
        self.in_region = True

    def end_region(self, cond_key, cond_ap, thr):
        self.in_region = False
        for eng in ENGS:
            ops = self.reg_ops[eng]
            if not ops:
                continue
            incs = [(self._semh(sk), b, i) for sk, (b, i) in self.reg_inc[eng].items()]
            waits = [(self._semh(sk), v) for sk, v in self.reg_wait[eng].items()]
            slot = cond_key % 3
            own = self.creg_owner.setdefault(eng, {})
            need_load = own.get(slot) != cond_key
            own[slot] = cond_key

            def run(E, ops=ops, incs=incs, waits=waits, eng=eng, slot=slot, need_load=need_load):
                if (eng, slot) not in self.cregs:
                    self.cregs[(eng, slot)] = E.alloc_register("creg_%s%d" % (eng, slot))
                reg = self.cregs[(eng, slot)]
                if need_load:
                    E.reg_load(reg, cond_ap)
                with E.If_lt(reg, thr + 1):
                    for h, b, i in incs:
                        if b > 0:
                            E.wait_ge(h, b)
                        E.sem_inc(h, i)
                    for h, v in waits:
                        E.wait_ge(h, v)
                with E.Else():
                    for r in ops:
                        r(E)
            self.q[eng].append(run)

    def barrier(self):
        for eng in ENGS:
            wl = []
            for sk, v in self.cnt.items():
                if v == 0 or sk == eng:
                    continue
                if self.seen[eng].get(sk, 0) >= v:
                    continue
                self.seen[eng][sk] = v
                wl.append((self._semh(sk), v))
            if wl:
                def run(E, wl=wl):
                    for h, v in wl:
                        E.wait_ge(h, v)
                self.q[eng].append(run)
        self.last_w.clear()
        self.readers.clear()

    def flush(self, block):
        for eng, dec in (("sp", block.sync), ("pe", block.tensor), ("act", block.scalar),
                         ("dve", block.vector), ("pool", block.gpsimd)):
            lst = self.q[eng]
            if not lst:
                continue

            def body(E, lst=lst):
                for r in lst:
                    r(E)
            dec(body)
            self.q[eng] = []


STAGES = ["proj", "hgrn", "fox", "merge", "ln1", "experts", "ln2"]


class K:
    def __init__(self, layers=DEPTH, upto="all", dbg=()):
        self.layers = L = layers
        self.upto = upto
        self.nst = len(STAGES) if upto == "all" else STAGES.index(upto) + 1
        nc = self.nc = bass.Bass("TRN2", target_bir_lowering=False)
        IN = lambda n, s, dt=F32: nc.dram_tensor(n, list(s), dt, kind="ExternalInput").ap()
        SC = lambda n, s, dt=F32: nc.dram_tensor(n, list(s), dt, kind=("ExternalOutput" if n in dbg else "Internal")).ap()
        self.x_in = IN("x", [S, D])
        self.cT = IN("cT", [128, 8])
        self.w_in = IN("w_in", [L, D, PIN])
        self.ffb = IN("ffb", [16, DEPTH])
        self.lbl = IN("lbl", [128, DEPTH, 8])
        self.normw = IN("normw", [128, DEPTH])
        self.w_branch = IN("w_branch", [L, 2 * D, D])
        self.w_out = IN("w_out", [L, D, D])
        self.ada_w = IN("ada_w", [L, D, 6 * D])
        self.adaB = IN("adaB", [128, DEPTH, 48])
        self.ln1_g = IN("ln1_g", [DEPTH, D]); self.ln1_b = IN("ln1_b", [DEPTH, D])
        self.ln2_g = IN("ln2_g", [DEPTH, D]); self.ln2_b = IN("ln2_b", [DEPTH, D])
        self.w_router = IN("w_router", [DEPTH, D, NE])
        self.b_router = IN("b_router", [DEPTH, NE])
        if self.nst >= 6:
            self.w_gate = IN("w_gate", [L, NE, D, D])
            self.w_up = IN("w_up", [L, NE, D, D])
            self.w_down = IN("w_down", [L, NE, D, D])
        self.bgu = IN("bgu", [DEPTH, NE, 128, 16])
        self.b_down = IN("b_down", [DEPTH, NE, D])
        self.cst = IN("cst", [128, NCST])
        self.cstu = IN("cstu", [128, 128], U32)
        self.out = nc.dram_tensor("out", [S, D], F32, kind="ExternalOutput").ap()
        self.XS = SC("XS", [S, D])
        self.MODT = SC("MODT", [DEPTH, 48, 128])
        self.PT = {n: SC("PT_" + n, [D, S]) for n in ("hq", "hf", "hg", "aq", "ak", "ga", "gb")}
        self.PT["af"] = SC("PT_af", [16, S])
        self.PR = {n: SC("PR_" + n, [S, D]) for n in ("hi", "av")}
        self.OAT = SC("OAT", [D, S])
        self.OBT = SC("OBT", [D, S])
        self.YS = SC("YS", [S, D])
        self.XG = SC("XG", [NE * CAP, D])
        self.YG = SC("YG", [NE * CAP, D])
        self.CNT = SC("CNT", [1, NE], I32)
        self.build()

    def dma(self, q, out, in_, r=(), w=(), **kw):
        self.P.dma(q, lambda E, o=out, i=in_, kw=kw: E.dma_start(out=o, in_=i, **kw), reads=r, writes=w)

    def act(self, out, in_, func, r=(), w=(), **kw):
        self.P.op("act", lambda E, o=out, i=in_, f=func, kw=kw: E.activation(out=o, in_=i, func=f, **kw), reads=r, writes=w)

    def mm(self, out, lhsT, rhs, start, stop, r=(), w=()):
        self.P.op("pe", lambda E, o=out, a=lhsT, b=rhs, s=start, t=stop: E.matmul(o, a, b, start=s, stop=t), reads=r, writes=w)

    def tr(self, out, in_, ident, r=(), w=()):
        self.P.op("pe", lambda E, o=out, i=in_, d=ident: E.transpose(o, i, d), reads=r, writes=w)

    def tt(self, eng, out, in0, in1, op, r=(), w=()):
        self.P.op(eng, lambda E, o=out, a=in0, b=in1, p=op: E.tensor_tensor(out=o, in0=a, in1=b, op=p), reads=r, writes=w)

    def ts(self, eng, out, in0, s1, op0, s2=None, op1=None, r=(), w=()):
        if op1 is None:
            self.P.op(eng, lambda E, o=out, a=in0, s1=s1, p0=op0: E.tensor_scalar(out=o, in0=a, scalar1=s1, scalar2=None, op0=p0), reads=r, writes=w)
        else:
            self.P.op(eng, lambda E, o=out, a=in0, s1=s1, s2=s2, p0=op0, p1=op1: E.tensor_scalar(out=o, in0=a, scalar1=s1, scalar2=s2, op0=p0, op1=p1), reads=r, writes=w)

    def stt(self, out, in0, scalar, in1, op0, op1, r=(), w=()):
        self.P.op("dve", lambda E, o=out, a=in0, s=scalar, b=in1, p0=op0, p1=op1: E.scalar_tensor_tensor(out=o, in0=a, scalar=s, in1=b, op0=p0, op1=p1), reads=r, writes=w)

    def cp(self, eng, out, in_, r=(), w=()):
        self.P.op(eng, lambda E, o=out, i=in_: E.tensor_copy(out=o, in_=i), reads=r, writes=w)

    def ms(self, eng, out, val, w=()):
        self.P.op(eng, lambda E, o=out, v=val: E.memset(o, v), writes=w)

    def build(self):
        nc = self.nc
        with ExitStack() as st:
            P = self.P = Prog(nc, st)
            with nc.Block() as b0:
                @b0.sync
                def _(E):
                    for h in P.all_sems():
                        E.sem_clear(h)
            self.ps = [st.enter_context(nc.psum_tensor("psb%d" % i, [128, 512], F32)) for i in range(8)]
            A = lambda n, s, dt=F32: st.enter_context(nc.sbuf_tensor(self.uid(n), list(s), dt))
            self.c_cst = A("c_cst", [128, NCST])
            self.c_ident = self.c_cst[:, 0:128]
            self.c_ones = self.c_cst[:, 128:256]
            self.c_identb = A("c_identb", [128, 128], BF16)
            self.c_onesb = A("c_onesb", [128, 128], BF16)
            self.c_trib = A("c_trib", [128, 128], BF16)
            self.c_strb = A("c_strb", [128, 128], BF16)
            self.c_onesr = A("c_onesr", [128, 128])
            self.c_bdc = A("c_bdc", [128, 128], U32)
            self.c_eps = A("c_eps", [128, 2])
            self.modc = A("modc", [128, DEPTH, 48])
            self.sc1p = A("sc1p", [128, DEPTH, 8])
            self.sc2p = A("sc2p", [128, DEPTH, 8])
            self.lbc = A("lbc", [128, DEPTH, 8])
            self.oml = A("oml", [128, DEPTH, 8])
            self.noml = A("noml", [128, DEPTH, 8])
            self.nw = A("nw", [128, DEPTH])
            self.fb = A("fb", [16, DEPTH])
            self.slot = A("slot", [128, NT, 4], I32)
            self.wk = A("wk", [128, NT, 4])
            with nc.Block() as blk:
                self.blk = blk
                self.stage0()
                fns = [self.st_proj, self.st_hgrn, self.st_fox, self.st_merge, self.st_ln1, self.st_experts, self.st_ln2]
                for l in range(self.layers):
                    for f in fns[:self.nst]:
                        f(l)
                P.barrier()
                P.flush(blk)

    def bc_reg(self, E):
        if getattr(self, "_bc", None) is None:
            self._bc = E.to_reg(NE * CAP - 1)
        return self._bc

    def uid(self, n):
        self._uid = getattr(self, "_uid", 0) + 1
        return "%s_u%d" % (n, self._uid)

    def end_stage(self):
        self.P.barrier()
        self.P.flush(self.blk)

    def stage0(self):
        nc, P = self.nc, self.P
        with ExitStack() as st:
            A = lambda n, s, dt=F32: st.enter_context(nc.sbuf_tensor(self.uid(n), list(s), dt))
            self.dma("sp", self.c_cst[:], self.cst, w=["cst"])
            self.dma("sp", self.c_bdc[:], self.cstu, w=["bdc"])
            self.dma("sp", self.nw[:], self.normw, w=["nw"])
            self.dma("sp", self.fb[:], self.ffb, w=["fb"])
            self.cp("dve", self.c_identb[:], self.c_ident, r=["cst"], w=["identb"])
            self.cp("dve", self.c_onesb[:], self.c_ones, r=["cst"], w=["onesb"])
            self.cp("dve", self.c_trib[:], self.c_cst[:, 256:384], r=["cst"], w=["trib"])
            self.cp("dve", self.c_strb[:], self.c_cst[:, 384:512], r=["cst"], w=["strb"])
            self.act(self.c_onesr[:].bitcast(F32R), self.c_ones, AF.Identity, r=["cst"], w=["onesr"])
            self.ms("dve", self.c_eps[:, 0:1], LN_EPS, w=["eps"])
            self.ms("dve", self.c_eps[:, 1:2], RMS_EPS, w=["eps"])
            for i in range(8):
                self.dma("sp" if i % 2 == 0 else "act", self.XS[i * 512:(i + 1) * 512, :], self.x_in[i * 512:(i + 1) * 512, :], w=["XS"])
            lg = A("lg", [128, DEPTH, 8]); ex = A("ex", [128, DEPTH, 8]); mx = A("mx", [128, 8]); sm = A("sm", [128, 8])
            self.dma("sp", lg[:], self.lbl, w=["lg"])
            self.tt("dve", mx[:], lg[:, 0, :], lg[:, 1, :], ALU.max, r=["lg"], w=["mx"])
            for l in (2, 3):
                self.tt("dve", mx[:], mx[:], lg[:, l, :], ALU.max, r=["lg", "mx"], w=["mx"])
            for l in range(DEPTH):
                self.tt("dve", lg[:, l, :], lg[:, l, :], mx[:], ALU.subtract, r=["lg", "mx"], w=["lg"])
            self.act(ex[:], lg[:], AF.Exp, r=["lg"], w=["ex"])
            self.tt("dve", sm[:], ex[:, 0, :], ex[:, 1, :], ALU.add, r=["ex"], w=["sm"])
            for l in (2, 3):
                self.tt("dve", sm[:], sm[:], ex[:, l, :], ALU.add, r=["ex", "sm"], w=["sm"])
            P.op("dve", lambda E: E.reciprocal(out=sm[:], in_=sm[:]), reads=["sm"], writes=["sm"])
            for l in range(DEPTH):
                self.tt("dve", ex[:, l, :], ex[:, l, :], sm[:], ALU.mult, r=["ex", "sm"], w=["ex"])
            self.ms("dve", self.lbc[:, 0, :], 0.0, w=["lbc"])
            self.cp("dve", self.lbc[:, 1, :], ex[:, 1, :], r=["ex"], w=["lbc"])
            for l in (2, 3):
                self.tt("dve", self.lbc[:, l, :], self.lbc[:, l - 1, :], ex[:, l, :], ALU.add, r=["ex", "lbc"], w=["lbc"])
            self.ts("dve", self.lbc[:], self.lbc[:], 0.0, ALU.max, 1.0, ALU.min, r=["lbc"], w=["lbc"])
            self.ts("dve", self.oml[:], self.lbc[:], -1.0, ALU.mult, 1.0, ALU.add, r=["lbc"], w=["oml"])
            self.ts("dve", self.noml[:], self.lbc[:], 1.0, ALU.mult, -1.0, ALU.add, r=["lbc"], w=["noml"])
            ct = A("ct", [128, 8]); ca = A("ca", [128, 8])
            self.dma("sp", ct[:], self.cT, w=["ct"])
            self.act(ca[:], ct[:], AF.Silu, r=["ct"], w=["ca"])
            self.dma("sp", self.modc[:], self.adaB, w=["modc_b"])
            wbuf = [A("adaw%d" % i, [128, 8, 1024]) for i in range(2)]
            mps = self.ps[0]
            n = 0
            for l in range(self.layers):
                for g in range(6):
                    wb = wbuf[n % 2]; key = "adaw%d" % (n % 2)
                    src = self.ada_w[l, :, g * 1024:(g + 1) * 1024].rearrange("(k p) f -> p k f", p=128)
                    self.dma("sp" if n % 2 == 0 else "act", wb[:], src, w=[key])
                    for m in range(8):
                        col = l * 48 + g * 8 + m
                        for k in range(8):
                            self.mm(mps[:, col:col + 1], wb[:, k, m * 128:(m + 1) * 128], ca[:, k:k + 1], k == 0, k == 7, r=[key, "ca"], w=["mps"])
                    n += 1
            L = self.layers
            self.tt("dve", self.modc[:, 0:L, :], self.modc[:, 0:L, :], mps[:, 0:L * 48].rearrange("p (l j) -> p l j", j=48), ALU.add,
                    r=["mps", "modc_b"], w=["modc"])
            self.ts("dve", self.sc1p[:, 0:L, :], self.modc[:, 0:L, 8:16], 1.0, ALU.add, r=["modc"], w=["sc1p"])
            self.ts("dve", self.sc2p[:, 0:L, :], self.modc[:, 0:L, 32:40], 1.0, ALU.add, r=["modc"], w=["sc2p"])
            mt = A("mt", [48, 128])
            for l in range(L):
                tp = self.ps[1]
                self.tr(tp[0:48, 0:128], self.modc[:, l, :], self.c_ident, r=["modc", "cst"], w=["tp"])
                self.act(mt[:], tp[0:48, 0:128], AF.Identity, r=["tp"], w=["mt"])
                self.dma("sp", self.MODT[l], mt[:], r=["mt"], w=["MODT"])
            self.end_stage()

    def st_proj(self, l):
        nc, P = self.nc, self.P
        HALF = S // 2
        NG = HALF // 512
        with ExitStack() as st:
            A = lambda n, s, dt=F32: st.enter_context(nc.sbuf_tensor(self.uid(n), list(s), dt))
            hT = A("hT", [128, 8, HALF])
            xb = [A("xb%d" % i, [128, 4, D]) for i in range(2)]
            wb = [A("wb%d" % i, [128, 8, 512]) for i in range(2)]
            sg = [A("sg%d" % i, [128, HALF]) for i in range(2)]
            so = [A("so%d" % i, [128, 512]) for i in range(2)]
            cnt = dict(w=0, sg=0, so=0, ps=0)

            def nxt(k, n):
                v = cnt[k] % n; cnt[k] += 1
                return v

            def load_w(c0, ncol):
                i = nxt("w", 2)
                src = self.w_in[l, :, c0:c0 + ncol].rearrange("(k p) f -> p k f", p=128)
                self.dma("pool", wb[i][:, :, 0:ncol].bitcast(F32R), src, w=["wb%d" % i])
                return wb[i], "wb%d" % i

            for half in range(2):
                t0 = half * HALF
                for g in range(NG):
                    xi = g % 2
                    src = self.XS[t0 + g * 512:t0 + (g + 1) * 512, :].rearrange("(j p) d -> p j d", p=128)
                    self.dma("sp", xb[xi][:], src, w=["xb%d" % xi])
                    for k in range(8):
                        pi = nxt("ps", 4); pb = self.ps[pi]; pk = "ps%d" % pi
                        for j in range(4):
                            self.tr(pb[:, j * 128:(j + 1) * 128], xb[xi][:, j, k * 128:(k + 1) * 128], self.c_ident, r=["xb%d" % xi], w=[pk])
                        self.act(hT[:, k, g * 512:(g + 1) * 512].bitcast(F32R), pb[:], AF.Identity, r=[pk], w=["hT%d" % g],
                                 scale=self.sc1p[:, l, k:k + 1], bias=self.modc[:, l, k:k + 1])
                fm = [("hq", AF.Silu), ("hf", AF.Sigmoid), ("hg", AF.Silu), ("aq", AF.Identity), ("ak", AF.Identity),
                      ("ga", AF.Sigmoid), ("gb", AF.Sigmoid)]
                for name, fn in fm:
                    for c4 in range(2):
                        w, wkey = load_w(OFF[name] + c4 * 512, 512)
                        for cc in range(4):
                            si = nxt("sg", 2)
                            for g in range(NG):
                                pi = nxt("ps", 4); pb = self.ps[pi]; pk = "ps%d" % pi
                                for k in range(8):
                                    self.mm(pb[:], w[:, k, cc * 128:(cc + 1) * 128].bitcast(F32R), hT[:, k, g * 512:(g + 1) * 512].bitcast(F32R),
                                            k == 0, k == 7, r=[wkey, "hT%d" % g], w=[pk])
                                self.act(sg[si][:, g * 512:(g + 1) * 512], pb[:], fn, r=[pk], w=["sg%d_%d" % (si, g)])
                            r0 = c4 * 512 + cc * 128
                            self.dma("sp", self.PT[name][r0:r0 + 128, t0:t0 + HALF], sg[si][:], r=["sg%d_%d" % (si, g) for g in range(NG)], w=["PT_" + name])
                w, wkey = load_w(OFF["af"], 16)
                si = nxt("sg", 2)
                for g in range(NG):
                    pi = nxt("ps", 4); pb = self.ps[pi]; pk = "ps%d" % pi
                    for k in range(8):
                        self.mm(pb[0:16, :], w[:, k, 0:16].bitcast(F32R), hT[:, k, g * 512:(g + 1) * 512].bitcast(F32R), k == 0, k == 7,
                                r=[wkey, "hT%d" % g], w=[pk])
                    self.act(sg[si][0:16, g * 512:(g + 1) * 512], pb[0:16, :], AF.Identity, r=[pk], w=["sg%d_%d" % (si, g)])
                self.dma("sp", self.PT["af"][:, t0:t0 + HALF], sg[si][0:16, :], r=["sg%d_%d" % (si, g) for g in range(NG)], w=["PT_af"])
                for name in ("hi", "av"):
                    for c4 in range(2):
                        w, wkey = load_w(OFF[name] + c4 * 512, 512)
                        for tt_ in range(HALF // 128):
                            g = tt_ // 4
                            pi = nxt("ps", 4); pb = self.ps[pi]; pk = "ps%d" % pi
                            for k in range(8):
                                self.mm(pb[:], hT[:, k, tt_ * 128:(tt_ + 1) * 128].bitcast(F32R), w[:, k, :].bitcast(F32R), k == 0, k == 7,
                                        r=[wkey, "hT%d" % g], w=[pk])
                            oi = nxt("so", 2)
                            self.cp("dve", so[oi][:], pb[:], r=[pk], w=["so%d" % oi])
                            r0 = t0 + tt_ * 128
                            self.dma("sp", self.PR[name][r0:r0 + 128, c4 * 512:(c4 + 1) * 512], so[oi][:], r=["so%d" % oi], w=["PR_" + name])
            self.end_stage()

    def st_hgrn(self, l):
        nc, P = self.nc, self.P
        NCH = 3
        with ExitStack() as st:
            A = lambda n, s, dt=F32: st.enter_context(nc.sbuf_tensor(self.uid(n), list(s), dt))
            Sst = [A("S%d" % h, [128, 128]) for h in range(HG_H)]
            Sbf = [A("Sb%d" % h, [128, 128], BF16) for h in range(HG_H)]
            ATm = [[A("ATm%d_%d" % (c, i), [128, 128], BF16) for i in range(2)] for c in range(NCH)]
            khT = [[A("khT%d_%d" % (c, i), [128, 128], BF16) for i in range(2)] for c in range(NCH)]
            ptn = ["SQ", "SF", "kk", "gg", "BB", "D1", "D4", "E1", "E2", "E3", "E4"]
            PTs = [{n: A("%s_%d" % (n, i), [128, 512]) for n in ptn} for i in range(2)]
            US = []
            for i in range(2 * NCH):
                d = {n: A("%s_%d" % (n, i), [128, 512], BF16) for n in ("qt", "kt", "qh", "kh")}
                d["dec"] = A("dec_%d" % i, [128, 8])
                for n in ("I", "IA", "IB"):
                    d[n] = A("%s_%d" % (n, i), [128, 4, 128], BF16)
                d["SG"] = A("SG_%d" % i, [128, 512]); d["Ob"] = A("Ob_%d" % i, [128, 512])
                US.append(d)
            EP = [{n: A("%s_%d" % (n, i), [128, 512]) for n in ("sq", "rs", "on")} for i in range(2)]
            for h in range(HG_H):
                self.ms("dve", Sst[h][:], 0.0, w=["S%d" % h])
                self.ms("pool", Sbf[h][:], 0.0, w=["Sb%d" % h])
            for c in range(NCH):
                for i in range(2):
                    self.ms("pool", ATm[c][i][:], 0.0, w=["ATm%d_%d" % (c, i)])
            for i in range(2 * NCH):
                self.ms("pool", US[i]["IA"][:], 0.0, w=["IA_%d" % i])
                self.ms("pool", US[i]["IB"][:], 0.0, w=["IB_%d" % i])
            scanmask = self.c_cst[:, 512:1024]
            NU = (S // 512) * HG_H

            def unit(n):
                bi, h = divmod(n, HG_H)
                return bi, h, slice(bi * 512, (bi + 1) * 512), slice(h * 128, (h + 1) * 128)

            def prologue(n):
                bi, h, cols, rows = unit(n)
                t = PTs[n % 2]; u = US[n % (2 * NCH)]
                tk = lambda nm: "%s_%d" % (nm, n % 2)
                uk = lambda nm: "%s_%d" % (nm, n % (2 * NCH))
                self.dma("sp", t["SQ"][:], self.PT["hq"][rows, cols], w=[tk("SQ")])
                self.dma("sp", t["SF"][:], self.PT["hf"][rows, cols], w=[tk("SF")])
                self.dma("sp", u["SG"][:], self.PT["hg"][rows, cols], w=[uk("SG")])
                isrc = self.PR["hi"][cols, rows].rearrange("(j p) v -> p j v", p=128)
                self.dma("pool", u["I"][:], isrc, w=[uk("I")])
                self.dma("pool", u["IA"][0:64, :, :], isrc[0:64], w=[uk("IA")])
                self.dma("pool", u["IB"][64:128, :, :], isrc[64:128], w=[uk("IB")])
                oml = self.oml[:, l, h:h + 1]; noml = self.noml[:, l, h:h + 1]; lb = self.lbc[:, l, h:h + 1]
                self.ts("dve", t["kk"][:], t["SF"][:], noml, ALU.mult, oml, ALU.add, r=[tk("SF")], w=[tk("kk")])
                self.act(t["gg"][:], t["SF"][:], AF.Ln, r=[tk("SF")], w=[tk("gg")], scale=oml, bias=lb)
                P.op("dve", lambda E, o=t["BB"][:], m=scanmask, g=t["gg"][:]: E.tensor_tensor_scan(out=o, data0=m, data1=g, initial=0.0, op0=ALU.mult, op1=ALU.add),
                     reads=[tk("gg")], writes=[tk("BB")])
                B3 = t["BB"][:].rearrange("p (c t) -> p c t", t=64)
                self.tt("pool", t["D1"][:].rearrange("p (c t) -> p c t", t=64), B3, B3[:, :, 31:32].to_broadcast([128, 8, 64]), ALU.subtract, r=[tk("BB")], w=[tk("D1")])
                self.tt("pool", t["D4"][:].rearrange("p (c t) -> p c t", t=64), B3, B3[:, :, 63:64].to_broadcast([128, 8, 64]), ALU.subtract, r=[tk("BB")], w=[tk("D4")])
                self.act(t["E1"][:], t["D1"][:], AF.Exp, r=[tk("D1")], w=[tk("E1")])
                self.act(t["E2"][:], t["D1"][:], AF.Exp, r=[tk("D1")], w=[tk("E2")], scale=-1.0)
                self.act(t["E3"][:], t["BB"][:], AF.Exp, r=[tk("BB")], w=[tk("E3")])
                self.act(t["E4"][:], t["D4"][:], AF.Exp, r=[tk("D4")], w=[tk("E4")], scale=-1.0)
                self.tt("dve", u["qt"][:], t["SQ"][:], t["E1"][:], ALU.mult, r=[tk("SQ"), tk("E1")], w=[uk("qt")])
                self.tt("pool", u["kt"][:], t["kk"][:], t["E2"][:], ALU.mult, r=[tk("kk"), tk("E2")], w=[uk("kt")])
                self.tt("dve", u["qh"][:], t["SQ"][:], t["E3"][:], ALU.mult, r=[tk("SQ"), tk("E3")], w=[uk("qh")])
                self.tt("pool", u["kh"][:], t["kk"][:], t["E4"][:], ALU.mult, r=[tk("kk"), tk("E4")], w=[uk("kh")])
                self.cp("dve", u["dec"][:], t["E3"][:].rearrange("p (c t) -> p c t", t=64)[:, :, 63], r=[tk("E3")], w=[uk("dec")])

            def tile_steps(n, c, j):
                bi, h, cols, rows = unit(n)
                u = US[n % (2 * NCH)]
                uk = lambda nm: "%s_%d" % (nm, n % (2 * NCH))
                jp = j % 2
                c0 = j * 128
                at_ps = self.ps[0][:, c * 128:(c + 1) * 128]; atk = "ps0"
                kh_ps = self.ps[1 + 2 * c][:, 0:64].bitcast(BF16); khk = "ps%d" % (1 + 2 * c)
                su = self.ps[1 + 2 * c][:, 128:256]; suk = khk
                o_ps = self.ps[2 + 2 * c][:, 0:128]; ok = "ps%d" % (2 + 2 * c)
                AT = ATm[c][jp]; ATk = "ATm%d_%d" % (c, jp)
                KT = khT[c][jp]; KTk = "khT%d_%d" % (c, jp)
                Sk, Sbk = "S%d" % h, "Sb%d" % h

                def s1():
                    self.mm(at_ps, u["kt"][:, c0:c0 + 128], u["qt"][:, c0:c0 + 128], True, True, r=[uk("kt"), uk("qt")], w=[atk])
                    self.tr(kh_ps, u["kh"][:, c0:c0 + 128], self.c_identb[:], r=[uk("kh")], w=[khk])

                def s2():
                    P.op("dve", lambda E, o=AT[:], m=self.c_bdc[:], d=at_ps: E.copy_predicated(out=o, mask=m, data=d), reads=[atk], writes=[ATk])
                    self.act(KT[:], kh_ps, AF.Identity, r=[khk], w=[KTk])

                def s3():
                    self.mm(o_ps, u["I"][:, j, :], AT[:], True, False, r=[uk("I"), ATk], w=[ok])
                    self.mm(o_ps[:, 0:64], Sbf[h][:], u["qh"][:, c0:c0 + 64], False, False, r=[Sbk, uk("qh")], w=[ok])
                    self.mm(su, KT[:], u["IA"][:, j, :], True, True, r=[KTk, uk("IA")], w=[suk])

                def s4():
                    self.stt(Sbf[h][:], Sst[h][:], u["dec"][:, 2 * j:2 * j + 1], su, ALU.mult, ALU.add, r=[suk, uk("dec"), Sk], w=[Sbk])
                    self.stt(Sst[h][:], Sst[h][:], u["dec"][:, 2 * j:2 * j + 1], su, ALU.mult, ALU.add, r=[suk, uk("dec"), Sk], w=[Sk])

                def s5():
                    self.mm(o_ps[:, 64:128], Sbf[h][:], u["qh"][:, c0 + 64:c0 + 128], False, True, r=[Sbk, uk("qh")], w=[ok])
                    self.mm(su, KT[:], u["IB"][:, j, :], True, True, r=[KTk, uk("IB")], w=[suk])

                def s6():
                    self.act(u["Ob"][:, c0:c0 + 128], o_ps, AF.Identity, r=[ok], w=[uk("Ob") + "_%d" % j])
                    self.stt(Sbf[h][:], Sst[h][:], u["dec"][:, 2 * j + 1:2 * j + 2], su, ALU.mult, ALU.add, r=[suk, uk("dec"), Sk], w=[Sbk])
                    self.stt(Sst[h][:], Sst[h][:], u["dec"][:, 2 * j + 1:2 * j + 2], su, ALU.mult, ALU.add, r=[suk, uk("dec"), Sk], w=[Sk])
                return [s1, s2, s3, s4, s5, s6]

            def epilogue_steps(n, c):
                bi, h, cols, rows = unit(n)
                u = US[n % (2 * NCH)]; e = EP[c % 2]
                uk = lambda nm: "%s_%d" % (nm, n % (2 * NCH))
                ek = lambda nm: "%s_%d" % (nm, c % 2)
                obk = [uk("Ob") + "_%d" % j for j in range(4)]

                def e1():
                    self.act(e["sq"][:].bitcast(F32R), u["Ob"][:], AF.Square, r=obk, w=[ek("sq")])

                def e2():
                    self.mm(self.ps[7][:], self.c_onesr[:].bitcast(F32R), e["sq"][:].bitcast(F32R), True, True, r=[ek("sq")], w=["ps7"])

                def e3():
                    self.act(e["rs"][:], self.ps[7][:], AF.Ln, r=["ps7"], w=[ek("rs")], scale=1.0 / 128.0, bias=self.c_eps[:, 1:2])
                    self.act(e["rs"][:], e["rs"][:], AF.Exp, r=[ek("rs")], w=[ek("rs")], scale=-0.5)

                def e4():
                    self.tt("dve", e["on"][:], u["Ob"][:], e["rs"][:], ALU.mult, r=obk + [ek("rs")], w=[ek("on")])
                    self.stt(e["on"][:], e["on"][:], self.nw[:, l:l + 1], u["SG"][:], ALU.mult, ALU.mult, r=[ek("on"), uk("SG")], w=[ek("on")])
                    self.dma("sp", self.OAT[rows, cols], e["on"][:], r=[ek("on")], w=["OAT"])
                return [e1, e2, e3, e4]

            for n in range(min(NCH, NU)):
                prologue(n)
            for g0 in range(0, NU, NCH):
                grp = list(range(g0, min(g0 + NCH, NU)))
                for j in range(4):
                    steps = [tile_steps(n, c, j) for c, n in enumerate(grp)]
                    for si in range(6):
                        for stp in steps:
                            stp[si]()
                    nxt = g0 + NCH + j
                    if j < NCH and nxt < NU:
                        prologue(nxt)
                for c, n in enumerate(grp):
                    for stp in epilogue_steps(n, c):
                        stp()
            self.end_stage()

    def st_fox(self, l):
        nc, P = self.nc, self.P
        with ExitStack() as st:
            A = lambda n, s, dt=F32: st.enter_context(nc.sbuf_tensor(self.uid(n), list(s), dt))
            ft = A("ft", [16, S]); Fc = A("Fc", [16, S])
            Ftok = A("Ftok", [128, NT, 16]); FrefB = A("FrefB", [128, 16, 8]); rbd = A("rbd", [16, 16, 8])
            vaug = [A("vaug%d" % i, [128, NT, 128], BF16) for i in range(2)]
            QA = [A("QA%d" % i, [128, S], BF16) for i in range(2)]
            QB = [A("QB%d" % i, [128, S], BF16) for i in range(2)]
            KP = [A("KP%d" % i, [128, S], BF16) for i in range(2)]
            e64 = A("e64", [128, 128]); e64r = A("e64r", [128, 128])
            bias = [A("bias%d" % i, [128, NT, 8]) for i in range(2)]
            Pb = [A("Pb%d" % i, [128, 512], BF16) for i in range(4)]
            Osb = [A("Osb%d" % i, [64, 512]) for i in range(2)]
            rc = [A("rc%d" % i, [128, 512]) for i in range(2)]
            zz = A("zz", [128, 512])
            ob = [A("ob%d" % i, [64, 512]) for i in range(2)]
            for i in range(2):
                self.ms("pool", vaug[i][:], 0.0, w=["vaug%d" % i])
                self.ms("pool", vaug[i][:, :, 64:65], 1.0, w=["vaug%d" % i])
                self.ms("pool", QA[i][64:128, :], 0.0, w=["QA%d" % i])
                self.ms("pool", QB[i][0:64, :], 0.0, w=["QB%d" % i])
            self.ms("dve", zz[:], 0.0, w=["zz"])
            self.ms("dve", e64[:], 0.0, w=["e64"])
            self.ms("dve", e64[64:65, :], 1.0, w=["e64"])
            self.act(e64r[:].bitcast(F32R), e64[:], AF.Identity, r=["e64"], w=["e64r"])
            for i in range(2):
                self.act(rc[i][:].bitcast(F32R), zz[:], AF.Identity, r=["zz"], w=["rc%d" % i])
            self.dma("sp", ft[:], self.PT["af"], w=["ft"])
            self.act(ft[:], ft[:], AF.Sigmoid, r=["ft"], w=["ft"], bias=self.fb[:, l:l + 1])
            self.act(ft[:], ft[:], AF.Ln, r=["ft"], w=["ft"])
            P.op("dve", lambda E: E.tensor_tensor_scan(out=Fc[:], data0=self.c_ones[0:16, 0:1].to_broadcast([16, S]), data1=ft[:], initial=0.0, op0=ALU.mult, op1=ALU.add),
                 reads=["ft"], writes=["Fc"])
            for i in range(NT):
                self.tr(self.ps[0][:, i * 16:(i + 1) * 16], Fc[0:16, i * 128:(i + 1) * 128], self.c_ident[0:16, 0:16], r=["Fc"], w=["ps0"])
            self.act(Ftok[:].rearrange("p i h -> p (i h)"), self.ps[0][:], AF.Identity, r=["ps0"], w=["Ftok"])
            fref = Fc[:].rearrange("h (j t) -> h j t", t=512)[:, :, 256]
            self.tt("dve", rbd[:], self.c_ident[0:16, 0:16].unsqueeze(2).to_broadcast([16, 16, 8]), fref.unsqueeze(1).to_broadcast([16, 16, 8]), ALU.mult,
                    r=["Fc"], w=["rbd"])
            self.mm(self.ps[1][:, 0:128], self.c_ones[0:16, :], rbd[:].rearrange("h a j -> h (a j)"), True, True, r=["rbd"], w=["ps1"])
            self.act(FrefB[:].rearrange("p a j -> p (a j)"), self.ps[1][:, 0:128], AF.Identity, r=["ps1"], w=["FrefB"])
            LA = 3
            its = []
            for h in range(FX_H):
                for j in range(S // 512):
                    n_i = 4 * j + 4
                    for i in range(n_i):
                        its.append((h, j, i, n_i))
            hstate = {}

            def head_loads(h):
                par = h % 2; pp = (h // 2) % 2
                rows = slice(h * 64, (h + 1) * 64)
                if h % 2 == 0:
                    self.dma("pool", KP[pp][:], self.PT["ak"][h * 64:(h + 2) * 64, :], w=["KP%d" % pp])
                    self.dma("pool", QA[pp][0:64, :], self.PT["aq"][rows, :], w=["QA%d" % pp])
                    qsrc = QA[pp]; qkey = "QA%d" % pp
                else:
                    self.dma("pool", QB[pp][64:128, :], self.PT["aq"][rows, :], w=["QB%d" % pp])
                    qsrc = QB[pp]; qkey = "QB%d" % pp
                self.dma("pool", vaug[par][:, :, 0:64], self.PR["av"][:, rows].rearrange("(i p) v -> p i v", p=128), w=["vaug%d" % par])
                self.tt("dve", bias[par][:], FrefB[:, h, :].unsqueeze(1).to_broadcast([128, NT, 8]), Ftok[:, :, h:h + 1].to_broadcast([128, NT, 8]), ALU.subtract,
                        r=["FrefB", "Ftok"], w=["bias%d" % par])
                hstate[h] = (par, pp, qsrc, qkey, rows)

            def emit_S(n):
                h, j, i, n_i = its[n]
                if h not in hstate:
                    head_loads(h)
                par, pp, qsrc, qkey, rows = hstate[h]
                cs = max(i - 4 * j, 0) * 128
                si = n % 4
                self.mm(self.ps[si][:, cs:512], KP[pp][:, i * 128:(i + 1) * 128], qsrc[:, j * 512 + cs:(j + 1) * 512], True, True,
                        r=["KP%d" % pp, qkey], w=["ps%d" % si])

            deferred = []

            def run_deferred(force=False):
                for d_ in list(deferred):
                    d_[0] -= 1
                    if d_[0] <= 0 or force:
                        d_[1]()
                        deferred.remove(d_)

            nblk = 0
            for n in range(min(LA, len(its))):
                emit_S(n)
            for n in range(len(its)):
                h, j, i, n_i = its[n]
                par, pp, qsrc, qkey, rows = hstate[h]
                if i == 0 and j == 0 and h + 1 < FX_H and (h + 1) not in hstate:
                    head_loads(h + 1)
                if n + LA < len(its):
                    emit_S(n + LA)
                r_ = i - 4 * j
                cs = max(r_, 0) * 128
                si = n % 4; skey = "ps%d" % si
                if i == 0:
                    oi = nblk % 2; nblk += 1
                o_ps = self.ps[4 + oi]; okey = "ps%d" % (4 + oi)
                pi = n % 4; pkey = "Pb%d" % pi
                pt = Pb[pi][:, cs:512]
                self.act(pt, self.ps[si][:, cs:512], AF.Exp, r=[skey, "bias%d" % par], w=[pkey], scale=0.125, bias=bias[par][:, i, j:j + 1])
                if r_ >= 0:
                    self.tt("pool", Pb[pi][:, cs:cs + 128], Pb[pi][:, cs:cs + 128], self.c_trib[:], ALU.mult, r=[pkey], w=[pkey])
                self.mm(o_ps[:, cs:512], vaug[par][:, i, :], pt, i == 0, i == n_i - 1, r=["vaug%d" % par, pkey], w=[okey])
                run_deferred()
                if i == n_i - 1:
                    self.cp("dve", Osb[oi][:], o_ps[0:64, :], r=[okey], w=["Osb%d" % oi])

                    def _rcp(E, o=rc[oi][64:65, :].bitcast(F32R), i_=o_ps[64:65, :]):
                        with nc.allow_low_precision(reason="fp32r operand for the broadcast matmul"):
                            return E.reciprocal(out=o, in_=i_)
                    P.op("dve", _rcp, reads=[okey], writes=["rc%d" % oi])

                    def part_b(oi=oi, rows=rows, j=j):
                        self.mm(self.ps[6][:], e64r[:].bitcast(F32R), rc[oi][:].bitcast(F32R), True, True, r=["rc%d" % oi, "e64r"], w=["ps6"])
                        self.tt("dve", ob[oi][:], Osb[oi][:], self.ps[6][0:64, :], ALU.mult, r=["Osb%d" % oi, "ps6"], w=["ob%d" % oi])
                        self.dma("sp", self.OBT[rows, j * 512:(j + 1) * 512], ob[oi][:], r=["ob%d" % oi], w=["OBT"])
                    deferred.append([3, part_b])
            run_deferred(force=True)
            self.end_stage()

    def st_merge(self, l):
        nc, P = self.nc, self.P
        with ExitStack() as st:
            A = lambda n, s, dt=F32: st.enter_context(nc.sbuf_tensor(self.uid(n), list(s), dt))
            wbr = A("wbr", [128, 16, D], BF16)
            wo = A("wo", [128, 8, D], BF16)
            inb = []
            for i in range(2):
                inb.append({n: A("%s%d" % (n, i), [128, 8, 512], BF16) for n in ("oa", "ob", "ga", "gb")})
            mg = [A("mg%d" % i, [128, 8, 512], BF16) for i in range(2)]
            t1 = [A("t1_%d" % i, [128, 512]) for i in range(2)]
            t2 = [A("t2_%d" % i, [128, 512]) for i in range(2)]
            ysb = [A("ysb%d" % i, [128, D]) for i in range(2)]
            for k2 in range(4):
                self.dma("pool", wbr[:, k2 * 4:(k2 + 1) * 4, :], self.w_branch[l, k2 * 512:(k2 + 1) * 512, :].rearrange("(k p) d -> p k d", p=128), w=["wbr"])
            for k2 in range(2):
                self.dma("pool", wo[:, k2 * 4:(k2 + 1) * 4, :], self.w_out[l, k2 * 512:(k2 + 1) * 512, :].rearrange("(k p) d -> p k d", p=128), w=["wo"])
            nt_ = 0; ny = 0
            for bi in range(S // 512):
                par = bi % 2
                cols = slice(bi * 512, (bi + 1) * 512)
                ib = inb[par]
                for nm, src in (("oa", self.OAT), ("ob", self.OBT), ("ga", self.PT["ga"]), ("gb", self.PT["gb"])):
                    self.dma("pool", ib[nm][:], src[:, cols].rearrange("(k p) t -> p k t", p=128), w=["%s%d" % (nm, par)])
                for m in range(8):
                    pa = self.ps[(2 * m) % 4]; pak = "ps%d" % ((2 * m) % 4)
                    pb_ = self.ps[(2 * m + 1) % 4]; pbk = "ps%d" % ((2 * m + 1) % 4)
                    for k in range(8):
                        self.mm(pa[:], wbr[:, k, m * 128:(m + 1) * 128], ib["oa"][:, k, :], k == 0, k == 7, r=["wbr", "oa%d" % par], w=[pak])
                    for k in range(8):
                        self.mm(pb_[:], wbr[:, 8 + k, m * 128:(m + 1) * 128], ib["ob"][:, k, :], k == 0, k == 7, r=["wbr", "ob%d" % par], w=[pbk])
                    ti = nt_ % 2; nt_ += 1
                    self.tt("dve", t1[ti][:], pa[:], ib["ga"][:, m, :], ALU.mult, r=[pak, "ga%d" % par], w=["t1_%d" % ti])
                    self.tt("dve", t2[ti][:], pb_[:], ib["gb"][:, m, :], ALU.mult, r=[pbk, "gb%d" % par], w=["t2_%d" % ti])
                    self.tt("pool", mg[par][:, m, :], t1[ti][:], t2[ti][:], ALU.add, r=["t1_%d" % ti, "t2_%d" % ti], w=["mg%d_%d" % (par, m)])
                for r_ in range(4):
                    yi = ny % 2; ny += 1
                    for hh in range(2):
                        yp = self.ps[4 + 2 * yi + hh]; ypk = "ps%d" % (4 + 2 * yi + hh)
                        for k in range(8):
                            self.mm(yp[:], mg[par][:, k, r_ * 128:(r_ + 1) * 128], wo[:, k, hh * 512:(hh + 1) * 512], k == 0, k == 7,
                                    r=["mg%d_%d" % (par, k), "wo"], w=[ypk])
                        self.act(ysb[yi][:, hh * 512:(hh + 1) * 512], yp[:], AF.Identity, r=[ypk], w=["ysb%d_%d" % (yi, hh)])
                    r0 = bi * 512 + r_ * 128
                    self.dma("sp", self.YS[r0:r0 + 128, :], ysb[yi][:], r=["ysb%d_0" % yi, "ysb%d_1" % yi], w=["YS"])
            self.end_stage()

    def layer_norm(self, z, zk, xh, xhk, tmp, mvk):
        P = self.P
        stt_, mv, sd = tmp["st"], tmp["mv"], tmp["sd"]
        for c in range(2):
            P.op("dve", lambda E, o=stt_[:, c * 6:(c + 1) * 6], i=z[:, c * 512:(c + 1) * 512]: E.bn_stats(out=o, in_=i), reads=[zk], writes=[mvk + "st%d" % c])
        P.op("dve", lambda E, o=mv[:], i=stt_[:]: E.bn_aggr(out=o, in_=i), reads=[mvk + "st0", mvk + "st1"], writes=[mvk + "mv"])
        self.act(sd[:, 0:1], mv[:, 1:2], AF.Sqrt, r=[mvk + "mv"], w=[mvk + "sd"], bias=self.c_eps[:, 0:1])
        P.op("dve", lambda E, o=sd[:, 1:2], i=sd[:, 0:1]: E.reciprocal(out=o, in_=i), reads=[mvk + "sd"], writes=[mvk + "rstd"])
        self.ts("dve", sd[:, 2:3], mv[:, 0:1], sd[:, 1:2], ALU.mult, -1.0, ALU.mult, r=[mvk + "mv", mvk + "rstd"], w=[mvk + "nmr"])
        self.act(xh, z[:], AF.Identity, r=[zk, mvk + "rstd", mvk + "nmr"], w=[xhk], scale=sd[:, 1:2], bias=sd[:, 2:3])

    def load_row(self, q, dst, src_row, key):
        self.dma(q, dst, src_row.partition_broadcast(128)[:, 0, :], w=[key])

    def st_ln1(self, l):
        nc, P = self.nc, self.P
        with ExitStack() as st:
            A = lambda n, s, dt=F32: st.enter_context(nc.sbuf_tensor(self.uid(n), list(s), dt))
            g1p = A("g1p", [128, D]); lg_ = A("lg_", [128, D]); lb_ = A("lb_", [128, D]); s2p = A("s2p", [128, D]); sh2 = A("sh2", [128, D])
            wr = A("wr", [128, 8, NE]); br = A("br", [128, NE])
            csel = A("csel", [128, NE], BF16)
            modrow = lambda j0: self.MODT[l:l + 1, j0:j0 + 8, :].rearrange("a j p -> a (j p)")
            self.load_row("sp", g1p[:], modrow(16), "g1p")
            self.load_row("sp", sh2[:], modrow(24), "sh2")
            self.load_row("sp", s2p[:], modrow(32), "s2p")
            self.load_row("sp", lg_[:], self.ln1_g[l:l + 1, :], "lg_")
            self.load_row("sp", lb_[:], self.ln1_b[l:l + 1, :], "lb_")
            self.load_row("sp", br[:], self.b_router[l:l + 1, :], "br")
            self.dma("sp", wr[:], self.w_router[l].rearrange("(k p) e -> p k e", p=128), w=["wr"])
            self.ts("dve", g1p[:], g1p[:], 1.0, ALU.add, r=["g1p"], w=["g1p"])
            self.ts("dve", s2p[:], s2p[:], 1.0, ALU.add, r=["s2p"], w=["s2p"])
            self.ms("dve", csel[:], 0.0, w=["csel"])
            iotaC = self.c_cst[:, 1024:1056]
            T = []
            NB = 4
            for i in range(NB):
                d = {n: A("%s%d" % (n, i), [128, D]) for n in ("x", "y", "z", "xh", "x1", "h2")}
                d["h2T"] = A("h2T%d" % i, [128, 8, 128])
                d["st"] = A("st%d" % i, [128, 12]); d["mv"] = A("mv%d" % i, [128, 2]); d["sd"] = A("sd%d" % i, [128, 3])
                d["lgt"] = A("lgt%d" % i, [128, NE]); d["top"] = A("top%d" % i, [128, 8]); d["sel"] = A("sel%d" % i, [128, NE], BF16)
                d["SL"] = A("SL%d" % i, [128, NE]); d["oh"] = A("oh%d" % i, [128, NE]); d["sf"] = A("sf%d" % i, [128, 4])
                d["e4"] = A("e4%d" % i, [128, 4]); d["nt"] = A("nt%d" % i, [128, 2]); d["ovf"] = A("ovf%d" % i, [128, NE])
                T.append(d)
            def loads1(t):
                i_ = t % NB
                self.dma("sp", T[i_]["x"][:], self.XS[t * 128:(t + 1) * 128, :], w=["x%d" % i_])
                self.dma("sp", T[i_]["y"][:], self.YS[t * 128:(t + 1) * 128, :], w=["y%d" % i_])
            loads1(0); loads1(1)
            for t in range(NT):
                if t + 2 < NT:
                    loads1(t + 2)
                i = t % NB; d = T[i]
                kx = lambda nm: "%s%d" % (nm, i)
                rows = slice(t * 128, (t + 1) * 128)
                self.tt("dve", d["z"][:], d["y"][:], g1p[:], ALU.mult, r=[kx("y"), "g1p"], w=[kx("z")])
                self.stt(d["z"][:], d["x"][:], ALPHA, d["z"][:], ALU.mult, ALU.add, r=[kx("x"), kx("z")], w=[kx("z")])
                self.layer_norm(d["z"], kx("z"), d["xh"][:], kx("xh"), d, kx("m"))
                self.tt("dve", d["x1"][:], d["xh"][:], lg_[:], ALU.mult, r=[kx("xh"), "lg_"], w=[kx("x1")])
                self.tt("pool", d["x1"][:], d["x1"][:], lb_[:], ALU.add, r=[kx("x1"), "lb_"], w=[kx("x1")])
                self.dma("sp", self.XS[rows, :], d["x1"][:], r=[kx("x1")], w=["XSw"])
                self.tt("dve", d["h2"][:], d["x1"][:], s2p[:], ALU.mult, r=[kx("x1"), "s2p"], w=[kx("h2")])
                self.tt("pool", d["h2"][:], d["h2"][:], sh2[:], ALU.add, r=[kx("h2"), "sh2"], w=[kx("h2")])
                if self.nst < 6:
                    continue
                for hh in range(2):
                    pb = self.ps[hh]; pk = "ps%d" % hh
                    for k4 in range(4):
                        k = hh * 4 + k4
                        self.tr(pb[:, k4 * 128:(k4 + 1) * 128], d["h2"][:, k * 128:(k + 1) * 128], self.c_ident, r=[kx("h2")], w=[pk])
                    self.act(d["h2T"][:, hh * 4:(hh + 1) * 4, :].rearrange("p k t -> p (k t)"), pb[:], AF.Identity, r=[pk], w=[kx("h2T") + "_%d" % hh])
                lps = self.ps[2][:, 0:NE]
                for k in range(8):
                    self.mm(lps, d["h2T"][:, k, :], wr[:, k, :], k == 0, k == 7, r=[kx("h2T") + "_%d" % (k // 4), "wr"], w=["ps2"])
                self.tt("dve", d["lgt"][:], lps, br[:], ALU.add, r=["ps2", "br"], w=[kx("lgt")])
                P.op("dve", lambda E, o=d["top"][:], i_=d["lgt"][:]: E.max(out=o, in_=i_), reads=[kx("lgt")], writes=[kx("top")])
                self.ts("dve", d["sel"][:], d["lgt"][:], d["top"][:, 3:4], ALU.is_ge, r=[kx("lgt"), kx("top")], w=[kx("sel")])
                pps = self.ps[3][:, 0:NE]
                self.mm(pps, self.c_strb[:], d["sel"][:], True, False, r=[kx("sel")], w=["ps3"])
                self.mm(pps, self.c_onesb[:], csel[:], False, True, r=["csel"], w=["ps3"])
                self.tt("pool", csel[:], csel[:], d["sel"][:], ALU.add, r=["csel", kx("sel")], w=["csel"])
                self.ts("dve", d["ovf"][:], pps, float(CAP), ALU.is_ge, 1.0e7, ALU.mult, r=["ps3"], w=[kx("ovf")])
                self.tt("dve", d["SL"][:], pps, iotaC, ALU.add, r=["ps3"], w=[kx("SL")])
                self.tt("dve", d["SL"][:], d["SL"][:], d["ovf"][:], ALU.add, r=[kx("SL"), kx("ovf")], w=[kx("SL")])
                for k in range(4):
                    self.ts("dve", d["oh"][:], d["lgt"][:], d["top"][:, k:k + 1], ALU.is_equal, r=[kx("lgt"), kx("top")], w=[kx("oh")])
                    self.tt("dve", d["oh"][:], d["oh"][:], d["SL"][:], ALU.mult, r=[kx("oh"), kx("SL")], w=[kx("oh")])
                    P.op("dve", lambda E, o=d["sf"][:, k:k + 1], i_=d["oh"][:]: E.tensor_reduce(out=o, in_=i_, axis=mybir.AxisListType.X, op=ALU.add),
                         reads=[kx("oh")], writes=[kx("sf")])
                self.cp("dve", self.slot[:, t, :], d["sf"][:], r=[kx("sf")], w=["slot%d" % t])
                self.ts("dve", d["nt"][:, 0:1], d["top"][:, 0:1], -1.0, ALU.mult, r=[kx("top")], w=[kx("nt")])
                self.act(d["e4"][:], d["top"][:, 0:4], AF.Exp, r=[kx("top"), kx("nt")], w=[kx("e4")], bias=d["nt"][:, 0:1])
                P.op("dve", lambda E, o=d["nt"][:, 1:2], i_=d["e4"][:]: E.tensor_reduce(out=o, in_=i_, axis=mybir.AxisListType.X, op=ALU.add),
                     reads=[kx("e4")], writes=[kx("nt") + "s"])
                P.op("dve", lambda E, o=d["nt"][:, 1:2]: E.reciprocal(out=o, in_=o), reads=[kx("nt") + "s"], writes=[kx("nt") + "s"])
                self.ts("dve", self.wk[:, t, :], d["e4"][:], d["nt"][:, 1:2], ALU.mult, r=[kx("e4"), kx("nt") + "s"], w=["wk%d" % t])
                for k in range(4):
                    P.dma("pool", lambda E, o=self.XG, ix=self.slot[:, t, k:k + 1], i_=d["h2"][:]: E.indirect_dma_start(
                        out=o, out_offset=bass.IndirectOffsetOnAxis(ap=ix, axis=0), in_=i_, in_offset=None,
                        bounds_check=self.bc_reg(E), oob_is_err=False), reads=[kx("h2"), "slot%d" % t], writes=["XG"])
            if self.nst >= 6:
                cnt_sb = A("cnt_sb", [128, NE], I32)
                cps = self.ps[2][:, 0:NE]
                self.mm(cps, self.c_onesb[:], csel[:], True, True, r=["csel"], w=["ps2"])
                self.cp("dve", cnt_sb[:], cps, r=["ps2"], w=["cnt_sb"])
                self.dma("sp", self.CNT, cnt_sb[0:1, :], r=["cnt_sb"], w=["CNT"])
            self.end_stage()

    def st_experts(self, l):
        nc, P = self.nc, self.P
        blocks = []
        r = 0
        for nr in (512, 384, 128):
            if r < CAP:
                blocks.append((r, min(nr, CAP - r)))
                r += min(nr, CAP - r)
        assert r == CAP
        with ExitStack() as st:
            A = lambda n, s, dt=F32: st.enter_context(nc.sbuf_tensor(self.uid(n), list(s), dt))
            W = [A("W%d" % i, [128, 8, D], BF16) for i in range(6)]
            bcol = [A("bcol%d" % i, [128, 16]) for i in range(2)]
            bd = [A("bd%d" % i, [128, D]) for i in range(2)]
            X = [A("X%d" % i, [128, 4, D], BF16) for i in range(2)]
            XT = [A("XT%d" % i, [128, 8, 512], BF16) for i in range(2)]
            AT = [A("AT%d" % i, [128, 8, 512], BF16) for i in range(2)]
            tm = [{n: A("%s%d" % (n, i), [128, 512]) for n in ("gsb", "ssb", "u1")} for i in range(2)]
            Ysb = [A("Ysb%d" % i, [128, D]) for i in range(2)]

            def load_w(e):
                s0 = (e % 2) * 3
                for i, src in enumerate((self.w_gate, self.w_up, self.w_down)):
                    for k2 in range(2):
                        self.dma("pool", W[s0 + i][:, k2 * 4:(k2 + 1) * 4, :], src[l, e, k2 * 512:(k2 + 1) * 512, :].rearrange("(k p) f -> p k f", p=128),
                                 w=["W%d_%d" % (s0 + i, k2)])
                self.dma("sp", bcol[e % 2][:], self.bgu[l, e], w=["bcol%d" % (e % 2)])
                self.load_row("sp", bd[e % 2][:], self.b_down[l, e:e + 1, :], "bd%d" % (e % 2))

            units = [(e, r0, nr) for e in range(NE) for (r0, nr) in blocks]
            NU = len(units)
            cnt = dict(m=0, y=0)

            def loadX(n):
                e, r0, nr = units[n]
                xi = n % 2
                g0 = e * CAP + r0
                self.dma("pool", X[xi][:, 0:nr // 128, :], self.XG[g0:g0 + nr, :].rearrange("(j p) d -> p j d", p=128), w=["X%d" % xi])

            def phT(n):
                e, r0, nr = units[n]
                xi = n % 2
                for k in range(8):
                    tp = self.ps[k % 2][:, 0:256].bitcast(BF16); tpk = "ps%d" % (k % 2)
                    for j in range(nr // 128):
                        self.tr(tp[:, j * 128:(j + 1) * 128], X[xi][:, j, k * 128:(k + 1) * 128], self.c_identb[:], r=["X%d" % xi], w=[tpk])
                    if k % 2 == 0:
                        self.act(XT[xi][:, k, 0:nr], tp[:, 0:nr], AF.Identity, r=[tpk], w=["XT%d_%d" % (xi, k)])
                    else:
                        self.cp("dve", XT[xi][:, k, 0:nr], tp[:, 0:nr], r=[tpk], w=["XT%d_%d" % (xi, k)])

            def phGU(n):
                e, r0, nr = units[n]
                xi = n % 2
                s0 = (e % 2) * 3
                Wg, Wu = W[s0], W[s0 + 1]
                wkeys = lambda i: ["W%d_0" % (s0 + i), "W%d_1" % (s0 + i)]
                bc = bcol[e % 2]; bck = "bcol%d" % (e % 2)
                xtk = ["XT%d_%d" % (xi, k) for k in range(8)]

                def tail(m, ti):
                    t_ = tm[ti]
                    self.tt("pool", t_["ssb"][:, 0:nr], t_["gsb"][:, 0:nr], t_["ssb"][:, 0:nr], ALU.mult, r=["gsb%d" % ti, "ssb%d" % ti], w=["ssb%d" % ti])
                    self.stt(AT[xi][:, m, 0:nr], t_["u1"][:, 0:nr], 1.0, t_["ssb"][:, 0:nr], ALU.add, ALU.mult, r=["u1%d" % ti, "ssb%d" % ti], w=["AT%d_%d" % (xi, m)])
                ti = 0
                for m in range(8):
                    ti = cnt["m"] % 2; cnt["m"] += 1
                    t_ = tm[ti]
                    gp = self.ps[2 + ti]; gpk = "ps%d" % (2 + ti)
                    up = self.ps[4 + ti]; upk = "ps%d" % (4 + ti)
                    for k in range(8):
                        self.mm(gp[:, 0:nr], Wg[:, k, m * 128:(m + 1) * 128], XT[xi][:, k, 0:nr], k == 0, k == 7, r=wkeys(0) + [xtk[k]], w=[gpk])
                    for k in range(8):
                        self.mm(up[:, 0:nr], Wu[:, k, m * 128:(m + 1) * 128], XT[xi][:, k, 0:nr], k == 0, k == 7, r=wkeys(1) + [xtk[k]], w=[upk])
                    self.ts("dve", t_["gsb"][:, 0:nr], gp[:, 0:nr], bc[:, m:m + 1], ALU.add, 7.0, ALU.min, r=[gpk, bck], w=["gsb%d" % ti])
                    self.act(t_["u1"][:, 0:nr], up[:, 0:nr], AF.Identity, r=[upk, bck], w=["u1%d" % ti], bias=bc[:, 8 + m:9 + m])
                    self.act(t_["ssb"][:, 0:nr], t_["gsb"][:, 0:nr], AF.Sigmoid, r=["gsb%d" % ti], w=["ssb%d" % ti], scale=1.702)
                    self.ts("dve", t_["u1"][:, 0:nr], t_["u1"][:, 0:nr], 7.0, ALU.min, -7.0, ALU.max, r=["u1%d" % ti], w=["u1%d" % ti])
                    if m > 0:
                        tail(m - 1, 1 - ti)
                tail(7, ti)

            def phY(n):
                e, r0, nr = units[n]
                xi = n % 2
                s0 = (e % 2) * 3
                Wd = W[s0 + 2]
                wk_ = ["W%d_0" % (s0 + 2), "W%d_1" % (s0 + 2)]
                atk = ["AT%d_%d" % (xi, k) for k in range(8)]
                g0 = e * CAP + r0
                for j in range(nr // 128):
                    yi = cnt["y"] % 2; cnt["y"] += 1
                    for hh in range(2):
                        yb = 6 + hh
                        yp = self.ps[yb]; ypk = "ps%d" % yb
                        for k in range(8):
                            self.mm(yp[:], AT[xi][:, k, j * 128:(j + 1) * 128], Wd[:, k, hh * 512:(hh + 1) * 512], k == 0, k == 7, r=wk_ + [atk[k]], w=[ypk])
                        self.tt("dve", Ysb[yi][:, hh * 512:(hh + 1) * 512], yp[:], bd[e % 2][:, hh * 512:(hh + 1) * 512], ALU.add,
                                r=[ypk, "bd%d" % (e % 2)], w=["Ysb%d_%d" % (yi, hh)])
                    self.dma("sp", self.YG[g0 + j * 128:g0 + (j + 1) * 128, :], Ysb[yi][:], r=["Ysb%d_0" % yi, "Ysb%d_1" % yi], w=["YG"])

            def cond(n, fn):
                e, r0, nr = units[n]
                if r0 == 0 or not DYN_SKIP:
                    fn(n)
                else:
                    P.begin_region()
                    fn(n)
                    P.end_region(l * NE + e, self.CNT[0:1, e:e + 1], r0)

            load_w(0)
            cond(0, loadX)
            if NU > 1:
                cond(1, loadX)
            cond(0, phT)
            for n in range(NU):
                e, r0, nr = units[n]
                if r0 == 0 and e + 1 < NE:
                    load_w(e + 1)
                cond(n, phGU)
                if n + 1 < NU:
                    cond(n + 1, phT)
                if n + 2 < NU:
                    cond(n + 2, loadX)
                cond(n, phY)
            self.end_stage()

    def st_ln2(self, l):
        nc, P = self.nc, self.P
        last = (l == self.layers - 1)
        with ExitStack() as st:
            A = lambda n, s, dt=F32: st.enter_context(nc.sbuf_tensor(self.uid(n), list(s), dt))
            g2p = A("g2p", [128, D]); lg_ = A("lg2", [128, D]); lb_ = A("lb2", [128, D])
            self.load_row("sp", g2p[:], self.MODT[l:l + 1, 40:48, :].rearrange("a j p -> a (j p)"), "g2p")
            self.load_row("sp", lg_[:], self.ln2_g[l:l + 1, :], "lg2")
            self.load_row("sp", lb_[:], self.ln2_b[l:l + 1, :], "lb2")
            self.ts("dve", g2p[:], g2p[:], 1.0, ALU.add, r=["g2p"], w=["g2p"])
            T = []
            NB = 4
            for i in range(NB):
                d = {n: A("%s%d" % (n, i), [128, D]) for n in ("x", "Y0", "Y1", "Y2", "Y3", "z", "xh")}
                d["st"] = A("st%d" % i, [128, 12]); d["mv"] = A("mv%d" % i, [128, 2]); d["sd"] = A("sd%d" % i, [128, 3])
                T.append(d)
            def loads2(t):
                i_ = t % NB
                self.dma("sp", T[i_]["x"][:], self.XS[t * 128:(t + 1) * 128, :], w=["x%d" % i_])
                for k in range(4):
                    P.dma("pool", lambda E, o=T[i_]["Y%d" % k][:], ix=self.slot[:, t, k:k + 1], i_=self.YG: E.indirect_dma_start(
                        out=o, out_offset=None, in_=i_, in_offset=bass.IndirectOffsetOnAxis(ap=ix, axis=0),
                        bounds_check=self.bc_reg(E), oob_is_err=False), reads=[], writes=["Y%d%d" % (k, i_)])
            loads2(0); loads2(1)
            for t in range(NT):
                if t + 2 < NT:
                    loads2(t + 2)
                i = t % NB; d = T[i]
                kx = lambda nm: "%s%d" % (nm, i)
                rows = slice(t * 128, (t + 1) * 128)
                self.act(d["z"][:], d["Y0"][:], AF.Identity, r=[kx("Y0")], w=[kx("z")], scale=self.wk[:, t, 0:1])
                for k in range(1, 4):
                    self.stt(d["z"][:], d["Y%d" % k][:], self.wk[:, t, k:k + 1], d["z"][:], ALU.mult, ALU.add, r=[kx("Y%d" % k), kx("z")], w=[kx("z")])
                self.tt("pool", d["z"][:], d["z"][:], g2p[:], ALU.mult, r=[kx("z"), "g2p"], w=[kx("z")])
                self.stt(d["z"][:], d["x"][:], ALPHA, d["z"][:], ALU.mult, ALU.add, r=[kx("x"), kx("z")], w=[kx("z")])
                self.layer_norm(d["z"], kx("z"), d["xh"][:], kx("xh"), d, kx("m"))
                self.tt("dve", d["xh"][:], d["xh"][:], lg_[:], ALU.mult, r=[kx("xh"), "lg2"], w=[kx("xh")])
                self.tt("pool", d["xh"][:], d["xh"][:], lb_[:], ALU.add, r=[kx("xh"), "lb2"], w=[kx("xh")])
                dst = self.out if (last and self.upto == "all" and self.layers == DEPTH) else self.XS
                self.dma("sp", dst[rows, :], d["xh"][:], r=[kx("xh")], w=["XSw"])
            self.end_stage()


def make_consts():
    c = np.zeros((128, NCST), np.float32)
    c[:, 0:128] = np.eye(128, dtype=np.float32)
    c[:, 128:256] = 1.0
    s = np.arange(128)[:, None]; t = np.arange(128)[None, :]
    c[:, 256:384] = (s <= t)
    c[:, 384:512] = (s < t)
    m = np.ones(512, np.float32); m[0::64] = 0.0
    c[:, 512:1024] = m[None, :]
    c[:, 1024:1056] = (np.arange(NE, dtype=np.float32) * CAP)[None, :]
    u = ((s <= t) & ((s // 64) == (t // 64))).astype(np.uint32)
    return c, u


def host_inputs(inp, b, layers=DEPTH, experts=True):
    f = lambda a: np.ascontiguousarray(a, dtype=np.float32)
    c, u = make_consts()
    L = layers
    m = {
        "x": f(inp["x"][b]),
        "cT": f(inp["c"][b].reshape(8, 128).T),
        "w_in": inp["w_in"][:L],
        "ffb": f(inp["fox_f_bias"].T),
        "lbl": f(inp["hg_lb_logits"].reshape(DEPTH, 8, 128).transpose(2, 0, 1)),
        "normw": f(inp["hg_norm_w"].T),
        "w_branch": inp["w_branch"][:L], "w_out": inp["w_out"][:L], "ada_w": inp["ada_w"][:L],
        "adaB": f(inp["ada_b"].reshape(DEPTH, 48, 128).transpose(2, 0, 1)),
        "ln1_g": f(inp["ln1_g"]), "ln1_b": f(inp["ln1_b"]), "ln2_g": f(inp["ln2_g"]), "ln2_b": f(inp["ln2_b"]),
        "w_router": f(inp["w_router"]), "b_router": f(inp["b_router"]),
        "bgu": f(np.concatenate([inp["b_gate"].reshape(DEPTH, NE, 8, 128).transpose(0, 1, 3, 2),
                                 inp["b_up"].reshape(DEPTH, NE, 8, 128).transpose(0, 1, 3, 2)], axis=3)),
        "b_down": f(inp["b_down"]),
        "cst": c, "cstu": u,
    }
    if experts:
        m["w_gate"] = inp["w_gate"][:L]; m["w_up"] = inp["w_up"][:L]; m["w_down"] = inp["w_down"][:L]
    return m


_CACHE = {}


def kernel(**inputs):
    inp = {k: np.asarray(v) for k, v in inputs.items()}
    if "nc" not in _CACHE:
        _CACHE["nc"] = K().nc
    nc = _CACHE["nc"]
    shared = host_inputs(inp, 0)
    in_maps = []
    for b in range(8):
        m = dict(shared)
        m["x"] = np.ascontiguousarray(inp["x"][b], dtype=np.float32)
        m["cT"] = np.ascontiguousarray(inp["c"][b].reshape(8, 128).T, dtype=np.float32)
        in_maps.append(m)
    res = run_bass_kernel_spmd(nc, in_maps, core_ids=list(range(8)))
    return np.stack([r["out"] for r in res.results], axis=0).astype(np.float32)
```

```python
from contextlib import ExitStack
import numpy as np
import concourse.bass as bass
import concourse.mybir as mybir
from concourse.bass_utils import run_bass_kernel_spmd

F32 = mybir.dt.float32
F32R = mybir.dt.float32r
BF16 = mybir.dt.bfloat16
I32 = mybir.dt.int32
U32 = mybir.dt.uint32
AF = mybir.ActivationFunctionType
ALU = mybir.AluOpType

S = 4096
D = 1024
DEPTH = 4
NT = S // 128
HG_H = 8
FX_H = 16
NE = 32
PIN = 9232
ALPHA = float((2 * DEPTH) ** 0.25)
LN_EPS = 1e-5
RMS_EPS = 1e-6
CT = 7
CAP = CT * 128
DYN_SKIP = True
NCST = 1056
OFF = dict(hq=0, hf=1024, hi=2048, hg=3072, aq=4096, ak=5120, av=6144, af=7168, ga=7184, gb=8208)

ENGS = ("pe", "act", "dve", "pool", "sp")
NDSEM = 8


class Prog:
    def __init__(self, nc, stack):
        self.nc = nc
        self.sem = {n: stack.enter_context(nc.semaphore("s_" + n)) for n in ENGS}
        self.dsem = {}
        for q in ("sp", "act", "pool"):
            for i in range(NDSEM):
                self.dsem[(q, i)] = stack.enter_context(nc.semaphore("d_%s%d" % (q, i)))
        self.cnt = {k: 0 for k in list(self.sem) + list(self.dsem)}
        self.seen = {n: {} for n in ENGS}
        self.last_w = {}
        self.readers = {}
        self.q = {n: [] for n in ENGS}
        self.dma_i = {"sp": 0, "act": 0, "pool": 0}
        self.dma_pending = {}
        self.n_inst = 0
        self.in_region = False
        self.cregs = {}
        self.creg_owner = {}

    def all_sems(self):
        return list(self.sem.values()) + list(self.dsem.values())

    def _deps(self, reads, writes):
        deps = []
        for k in reads:
            if k in self.last_w:
                deps.append(self.last_w[k])
        for k in writes:
            if k in self.last_w:
                deps.append(self.last_w[k])
            deps.extend(self.readers.get(k, ()))
        return deps

    def _commit(self, stamp, reads, writes):
        for k in reads:
            self.readers.setdefault(k, []).append(stamp)
        for k in writes:
            self.last_w[k] = stamp
            self.readers[k] = []

    def _waits(self, eng, deps):
        need = {}
        for (sk, v) in deps:
            if sk == "pe" and eng == "pe":
                continue
            if self.seen[eng].get(sk, 0) >= v:
                continue
            if need.get(sk, 0) < v:
                need[sk] = v
        for sk, v in need.items():
            self.seen[eng][sk] = v
        return list(need.items())

    def _semh(self, sk):
        return self.sem[sk] if isinstance(sk, str) else self.dsem[sk]

    def op(self, eng, fn, reads=(), writes=()):
        deps = self._deps(reads, writes)
        waits = self._waits(eng, deps)
        self.cnt[eng] += 1
        stamp = (eng, self.cnt[eng])
        self._commit(stamp, reads, writes)
        semh = self.sem[eng]
        wl = [(self._semh(sk), v) for sk, v in waits]

        def run(E, fn=fn, wl=wl, semh=semh):
            for h, v in wl:
                E.wait_ge(h, v)
            fn(E).then_inc(semh, 1)
        if self.in_region:
            self.reg_ops[eng].append(run)
            self._reg_note(eng, eng, self.cnt[eng] - 1, 1, waits)
        else:
            self.q[eng].append(run)
        self.n_inst += 1
        return stamp

    def dma(self, q, fn, reads=(), writes=()):
        i = self.dma_i[q] % NDSEM
        self.dma_i[q] += 1
        sk = (q, i)
        deps = self._deps(reads, writes)
        if sk in self.dma_pending:
            deps.append(self.dma_pending[sk])
        waits = self._waits(q, deps)
        self.cnt[sk] += 16
        stamp = (sk, self.cnt[sk])
        self.dma_pending[sk] = stamp
        self._commit(stamp, reads, writes)
        semh = self.dsem[sk]
        wl = [(self._semh(s), v) for s, v in waits]

        def run(E, fn=fn, wl=wl, semh=semh):
            for h, v in wl:
                E.wait_ge(h, v)
            fn(E).then_inc(semh, 16)
        if self.in_region:
            self.reg_ops[q].append(run)
            self._reg_note(q, sk, self.cnt[sk] - 16, 16, waits)
        else:
            self.q[q].append(run)
        self.n_inst += 1
        return stamp

    def _reg_note(self, eng, sk, before, inc, waits):
        d = self.reg_inc[eng].setdefault(sk, [before, 0])
        d[1] += inc
        for wsk, v in waits:
            if self.reg_wait[eng].get(wsk, 0) < v:
                self.reg_wait[eng][wsk] = v

    def begin_region(self):
        self.reg_ops = {e: [] for e in ENGS}
        self.reg_inc = {e: {} for e in ENGS}
        self.reg_wait = {e: {} for e in ENGS}
        self.in_region = True

    def end_region(self, cond_key, cond_ap, thr):
        self.in_region = False
        for eng in ENGS:
            ops = self.reg_ops[eng]
            if not ops:
                continue
            incs = [(self._semh(sk), b, i) for sk, (b, i) in self.reg_inc[eng].items()]
            waits = [(self._semh(sk), v) for sk, v in self.reg_wait[eng].items()]
            slot = cond_key % 3
            own = self.creg_owner.setdefault(eng, {})
            need_load = own.get(slot) != cond_key
            own[slot] = cond_key

            def run(E, ops=ops, incs=incs, waits=waits, eng=eng, slot=slot, need_load=need_load):
                if (eng, slot) not in self.cregs:
                    self.cregs[(eng, slot)] = E.alloc_register("creg_%s%d" % (eng, slot))
                reg = self.cregs[(eng, slot)]
                if need_load:
                    E.reg_load(reg, cond_ap)
                with E.If_lt(reg, thr + 1):
                    for h, b, i in incs:
                        if b > 0:
                            E.wait_ge(h, b)
                        E.sem_inc(h, i)
                    for h, v in waits:
                        E.wait_ge(h, v)
                with E.Else():
                    for r in ops:
                        r(E)
            self.q[eng].append(run)

    def barrier(self):
        for eng in ENGS:
            wl = []
            for sk, v in self.cnt.items():
                if v == 0 or sk == eng:
                    continue
                if self.seen[eng].get(sk, 0) >= v:
                    continue
                self.seen[eng][sk] = v
                wl.append((self._semh(sk), v))
            if wl:
                def run(E, wl=wl):
                    for h, v in wl:
                        E.wait_ge(h, v)
                self.q[eng].append(run)
        self.last_w.clear()
        self.readers.clear()

    def flush(self, block):
        for eng, dec in (("sp", block.sync), ("pe", block.tensor), ("act", block.scalar),
                         ("dve", block.vector), ("pool", block.gpsimd)):
            lst = self.q[eng]
            if not lst:
                continue

            def body(E, lst=lst):
                for r in lst:
                    r(E)
            dec(body)
            self.q[eng] = []


STAGES = ["proj", "hgrn", "fox", "merge", "ln1", "experts", "ln2"]


class K:
    def __init__(self, layers=DEPTH, upto="all", dbg=()):
        self.layers = L = layers
        self.upto = upto
        self.nst = len(STAGES) if upto == "all" else STAGES.index(upto) + 1
        nc = self.nc = bass.Bass("TRN2", target_bir_lowering=False)
        IN = lambda n, s, dt=F32: nc.dram_tensor(n, list(s), dt, kind="ExternalInput").ap()
        SC = lambda n, s, dt=F32: nc.dram_tensor(n, list(s), dt, kind=("ExternalOutput" if n in dbg else "Internal")).ap()
        self.x_in = IN("x", [S, D])
        self.cT = IN("cT", [128, 8])
        self.w_in = IN("w_in", [L, D, PIN])
        self.ffb = IN("ffb", [16, DEPTH])
        self.lbl = IN("lbl", [128, DEPTH, 8])
        self.normw = IN("normw", [128, DEPTH])
        self.w_branch = IN("w_branch", [L, 2 * D, D])
        self.w_out = IN("w_out", [L, D, D])
        self.ada_w = IN("ada_w", [L, D, 6 * D])
        self.adaB = IN("adaB", [128, DEPTH, 48])
        self.ln1_g = IN("ln1_g", [DEPTH, D]); self.ln1_b = IN("ln1_b", [DEPTH, D])
        self.ln2_g = IN("ln2_g", [DEPTH, D]); self.ln2_b = IN("ln2_b", [DEPTH, D])
        self.w_router = IN("w_router", [DEPTH, D, NE])
        self.b_router = IN("b_router", [DEPTH, NE])
        if self.nst >= 6:
            self.w_gate = IN("w_gate", [L, NE, D, D])
            self.w_up = IN("w_up", [L, NE, D, D])
            self.w_down = IN("w_down", [L, NE, D, D])
        self.bgu = IN("bgu", [DEPTH, NE, 128, 16])
        self.b_down = IN("b_down", [DEPTH, NE, D])
        self.cst = IN("cst", [128, NCST])
        self.cstu = IN("cstu", [128, 128], U32)
        self.out = nc.dram_tensor("out", [S, D], F32, kind="ExternalOutput").ap()
        self.XS = SC("XS", [S, D])
        self.MODT = SC("MODT", [DEPTH, 48, 128])
        self.PT = {n: SC("PT_" + n, [D, S]) for n in ("hq", "hf", "hg", "aq", "ak", "ga", "gb")}
        self.PT["af"] = SC("PT_af", [16, S])
        self.PR = {n: SC("PR_" + n, [S, D]) for n in ("hi", "av")}
        self.OAT = SC("OAT", [D, S])
        self.OBT = SC("OBT", [D, S])
        self.YS = SC("YS", [S, D])
        self.XG = SC("XG", [NE * CAP, D])
        self.YG = SC("YG", [NE * CAP, D])
        self.CNT = SC("CNT", [1, NE], I32)
        self.build()

    def dma(self, q, out, in_, r=(), w=(), **kw):
        self.P.dma(q, lambda E, o=out, i=in_, kw=kw: E.dma_start(out=o, in_=i, **kw), reads=r, writes=w)

    def act(self, out, in_, func, r=(), w=(), **kw):
        self.P.op("act", lambda E, o=out, i=in_, f=func, kw=kw: E.activation(out=o, in_=i, func=f, **kw), reads=r, writes=w)

    def mm(self, out, lhsT, rhs, start, stop, r=(), w=()):
        self.P.op("pe", lambda E, o=out, a=lhsT, b=rhs, s=start, t=stop: E.matmul(o, a, b, start=s, stop=t), reads=r, writes=w)

    def tr(self, out, in_, ident, r=(), w=()):
        self.P.op("pe", lambda E, o=out, i=in_, d=ident: E.transpose(o, i, d), reads=r, writes=w)

    def tt(self, eng, out, in0, in1, op, r=(), w=()):
        self.P.op(eng, lambda E, o=out, a=in0, b=in1, p=op: E.tensor_tensor(out=o, in0=a, in1=b, op=p), reads=r, writes=w)

    def ts(self, eng, out, in0, s1, op0, s2=None, op1=None, r=(), w=()):
        if op1 is None:
            self.P.op(eng, lambda E, o=out, a=in0, s1=s1, p0=op0: E.tensor_scalar(out=o, in0=a, scalar1=s1, scalar2=None, op0=p0), reads=r, writes=w)
        else:
            self.P.op(eng, lambda E, o=out, a=in0, s1=s1, s2=s2, p0=op0, p1=op1: E.tensor_scalar(out=o, in0=a, scalar1=s1, scalar2=s2, op0=p0, op1=p1), reads=r, writes=w)

    def stt(self, out, in0, scalar, in1, op0, op1, r=(), w=()):
        self.P.op("dve", lambda E, o=out, a=in0, s=scalar, b=in1, p0=op0, p1=op1: E.scalar_tensor_tensor(out=o, in0=a, scalar=s, in1=b, op0=p0, op1=p1), reads=r, writes=w)

    def cp(self, eng, out, in_, r=(), w=()):
        self.P.op(eng, lambda E, o=out, i=in_: E.tensor_copy(out=o, in_=i), reads=r, writes=w)

    def ms(self, eng, out, val, w=()):
        self.P.op(eng, lambda E, o=out, v=val: E.memset(o, v), writes=w)

    def build(self):
        nc = self.nc
        with ExitStack() as st:
            P = self.P = Prog(nc, st)
            with nc.Block() as b0:
                @b0.sync
                def _(E):
                    for h in P.all_sems():
                        E.sem_clear(h)
            self.ps = [st.enter_context(nc.psum_tensor("psb%d" % i, [128, 512], F32)) for i in range(8)]
            A = lambda n, s, dt=F32: st.enter_context(nc.sbuf_tensor(self.uid(n), list(s), dt))
            self.c_cst = A("c_cst", [128, NCST])
            self.c_ident = self.c_cst[:, 0:128]
            self.c_ones = self.c_cst[:, 128:256]
            self.c_identb = A("c_identb", [128, 128], BF16)
            self.c_onesb = A("c_onesb", [128, 128], BF16)
            self.c_trib = A("c_trib", [128, 128], BF16)
            self.c_strb = A("c_strb", [128, 128], BF16)
            self.c_onesr = A("c_onesr", [128, 128])
            self.c_bdc = A("c_bdc", [128, 128], U32)
            self.c_eps = A("c_eps", [128, 2])
            self.modc = A("modc", [128, DEPTH, 48])
            self.sc1p = A("sc1p", [128, DEPTH, 8])
            self.sc2p = A("sc2p", [128, DEPTH, 8])
            self.lbc = A("lbc", [128, DEPTH, 8])
            self.oml = A("oml", [128, DEPTH, 8])
            self.noml = A("noml", [128, DEPTH, 8])
            self.nw = A("nw", [128, DEPTH])
            self.fb = A("fb", [16, DEPTH])
            self.slot = A("slot", [128, NT, 4], I32)
            self.wk = A("wk", [128, NT, 4])
            with nc.Block() as blk:
                self.blk = blk
                self.stage0()
                fns = [self.st_proj, self.st_hgrn, self.st_fox, self.st_merge, self.st_ln1, self.st_experts, self.st_ln2]
                for l in range(self.layers):
                    for f in fns[:self.nst]:
                        f(l)
                P.barrier()
                P.flush(blk)

    def bc_reg(self, E):
        if getattr(self, "_bc", None) is None:
            self._bc = E.to_reg(NE * CAP - 1)
        return self._bc

    def uid(self, n):
        self._uid = getattr(self, "_uid", 0) + 1
        return "%s_u%d" % (n, self._uid)

    def end_stage(self):
        self.P.barrier()
        self.P.flush(self.blk)

    def stage0(self):
        nc, P = self.nc, self.P
        with ExitStack() as st:
            A = lambda n, s, dt=F32: st.enter_context(nc.sbuf_tensor(self.uid(n), list(s), dt))
            self.dma("sp", self.c_cst[:], self.cst, w=["cst"])
            self.dma("sp", self.c_bdc[:], self.cstu, w=["bdc"])
            self.dma("sp", self.nw[:], self.normw, w=["nw"])
            self.dma("sp", self.fb[:], self.ffb, w=["fb"])
            self.cp("dve", self.c_identb[:], self.c_ident, r=["cst"], w=["identb"])
            self.cp("dve", self.c_onesb[:], self.c_ones, r=["cst"], w=["onesb"])
            self.cp("dve", self.c_trib[:], self.c_cst[:, 256:384], r=["cst"], w=["trib"])
            self.cp("dve", self.c_strb[:], self.c_cst[:, 384:512], r=["cst"], w=["strb"])
            self.act(self.c_onesr[:].bitcast(F32R), self.c_ones, AF.Identity, r=["cst"], w=["onesr"])
            self.ms("dve", self.c_eps[:, 0:1], LN_EPS, w=["eps"])
            self.ms("dve", self.c_eps[:, 1:2], RMS_EPS, w=["eps"])
            for i in range(8):
                self.dma("sp" if i % 2 == 0 else "act", self.XS[i * 512:(i + 1) * 512, :], self.x_in[i * 512:(i + 1) * 512, :], w=["XS"])
            lg = A("lg", [128, DEPTH, 8]); ex = A("ex", [128, DEPTH, 8]); mx = A("mx", [128, 8]); sm = A("sm", [128, 8])
            self.dma("sp", lg[:], self.lbl, w=["lg"])
            self.tt("dve", mx[:], lg[:, 0, :], lg[:, 1, :], ALU.max, r=["lg"], w=["mx"])
            for l in (2, 3):
                self.tt("dve", mx[:], mx[:], lg[:, l, :], ALU.max, r=["lg", "mx"], w=["mx"])
            for l in range(DEPTH):
                self.tt("dve", lg[:, l, :], lg[:, l, :], mx[:], ALU.subtract, r=["lg", "mx"], w=["lg"])
            self.act(ex[:], lg[:], AF.Exp, r=["lg"], w=["ex"])
            self.tt("dve", sm[:], ex[:, 0, :], ex[:, 1, :], ALU.add, r=["ex"], w=["sm"])
            for l in (2, 3):
                self.tt("dve", sm[:], sm[:], ex[:, l, :], ALU.add, r=["ex", "sm"], w=["sm"])
            P.op("dve", lambda E: E.reciprocal(out=sm[:], in_=sm[:]), reads=["sm"], writes=["sm"])
            for l in range(DEPTH):
                self.tt("dve", ex[:, l, :], ex[:, l, :], sm[:], ALU.mult, r=["ex", "sm"], w=["ex"])
            self.ms("dve", self.lbc[:, 0, :], 0.0, w=["lbc"])
            self.cp("dve", self.lbc[:, 1, :], ex[:, 1, :], r=["ex"], w=["lbc"])
            for l in (2, 3):
                self.tt("dve", self.lbc[:, l, :], self.lbc[:, l - 1, :], ex[:, l, :], ALU.add, r=["ex", "lbc"], w=["lbc"])
            self.ts("dve", self.lbc[:], self.lbc[:], 0.0, ALU.max, 1.0, ALU.min, r=["lbc"], w=["lbc"])
            self.ts("dve", self.oml[:], self.lbc[:], -1.0, ALU.mult, 1.0, ALU.add, r=["lbc"], w=["oml"])
            self.ts("dve", self.noml[:], self.lbc[:], 1.0, ALU.mult, -1.0, ALU.add, r=["lbc"], w=["noml"])
            ct = A("ct", [128, 8]); ca = A("ca", [128, 8])
            self.dma("sp", ct[:], self.cT, w=["ct"])
            self.act(ca[:], ct[:], AF.Silu, r=["ct"], w=["ca"])
            self.dma("sp", self.modc[:], self.adaB, w=["modc_b"])
            wbuf = [A("adaw%d" % i, [128, 8, 1024]) for i in range(2)]
            mps = self.ps[0]
            n = 0
            for l in range(self.layers):
                for g in range(6):
                    wb = wbuf[n % 2]; key = "adaw%d" % (n % 2)
                    src = self.ada_w[l, :, g * 1024:(g + 1) * 1024].rearrange("(k p) f -> p k f", p=128)
                    self.dma("sp" if n % 2 == 0 else "act", wb[:], src, w=[key])
                    for m in range(8):
                        col = l * 48 + g * 8 + m
                        for k in range(8):
                            self.mm(mps[:, col:col + 1], wb[:, k, m * 128:(m + 1) * 128], ca[:, k:k + 1], k == 0, k == 7, r=[key, "ca"], w=["mps"])
                    n += 1
            L = self.layers
            self.tt("dve", self.modc[:, 0:L, :], self.modc[:, 0:L, :], mps[:, 0:L * 48].rearrange("p (l j) -> p l j", j=48), ALU.add,
                    r=["mps", "modc_b"], w=["modc"])
            self.ts("dve", self.sc1p[:, 0:L, :], self.modc[:, 0:L, 8:16], 1.0, ALU.add, r=["modc"], w=["sc1p"])
            self.ts("dve", self.sc2p[:, 0:L, :], self.modc[:, 0:L, 32:40], 1.0, ALU.add, r=["modc"], w=["sc2p"])
            mt = A("mt", [48, 128])
            for l in range(L):
                tp = self.ps[1]
                self.tr(tp[0:48, 0:128], self.modc[:, l, :], self.c_ident, r=["modc", "cst"], w=["tp"])
                self.act(mt[:], tp[0:48, 0:128], AF.Identity, r=["tp"], w=["mt"])
                self.dma("sp", self.MODT[l], mt[:], r=["mt"], w=["MODT"])
            self.end_stage()

    def st_proj(self, l):
        nc, P = self.nc, self.P
        HALF = S // 2
        NG = HALF // 512
        with ExitStack() as st:
            A = lambda n, s, dt=F32: st.enter_context(nc.sbuf_tensor(self.uid(n), list(s), dt))
            hT = A("hT", [128, 8, HALF])
            xb = [A("xb%d" % i, [128, 4, D]) for i in range(2)]
            wb = [A("wb%d" % i, [128, 8, 512]) for i in range(2)]
            sg = [A("sg%d" % i, [128, HALF]) for i in range(2)]
            so = [A("so%d" % i, [128, 512]) for i in range(2)]
            cnt = dict(w=0, sg=0, so=0, ps=0)

            def nxt(k, n):
                v = cnt[k] % n; cnt[k] += 1
                return v

            def load_w(c0, ncol):
                i = nxt("w", 2)
                src = self.w_in[l, :, c0:c0 + ncol].rearrange("(k p) f -> p k f", p=128)
                self.dma("pool", wb[i][:, :, 0:ncol].bitcast(F32R), src, w=["wb%d" % i])
                return wb[i], "wb%d" % i

            for half in range(2):
                t0 = half * HALF
                for g in range(NG):
                    xi = g % 2
                    src = self.XS[t0 + g * 512:t0 + (g + 1) * 512, :].rearrange("(j p) d -> p j d", p=128)
                    self.dma("sp", xb[xi][:], src, w=["xb%d" % xi])
                    for k in range(8):
                        pi = nxt("ps", 4); pb = self.ps[pi]; pk = "ps%d" % pi
                        for j in range(4):
                            self.tr(pb[:, j * 128:(j + 1) * 128], xb[xi][:, j, k * 128:(k + 1) * 128], self.c_ident, r=["xb%d" % xi], w=[pk])
                        self.act(hT[:, k, g * 512:(g + 1) * 512].bitcast(F32R), pb[:], AF.Identity, r=[pk], w=["hT%d" % g],
                                 scale=self.sc1p[:, l, k:k + 1], bias=self.modc[:, l, k:k + 1])
                fm = [("hq", AF.Silu), ("hf", AF.Sigmoid), ("hg", AF.Silu), ("aq", AF.Identity), ("ak", AF.Identity),
                      ("ga", AF.Sigmoid), ("gb", AF.Sigmoid)]
                for name, fn in fm:
                    for c4 in range(2):
                        w, wkey = load_w(OFF[name] + c4 * 512, 512)
                        for cc in range(4):
                            si = nxt("sg", 2)
                            for g in range(NG):
                                pi = nxt("ps", 4); pb = self.ps[pi]; pk = "ps%d" % pi
                                for k in range(8):
                                    self.mm(pb[:], w[:, k, cc * 128:(cc + 1) * 128].bitcast(F32R), hT[:, k, g * 512:(g + 1) * 512].bitcast(F32R),
                                            k == 0, k == 7, r=[wkey, "hT%d" % g], w=[pk])
                                self.act(sg[si][:, g * 512:(g + 1) * 512], pb[:], fn, r=[pk], w=["sg%d_%d" % (si, g)])
                            r0 = c4 * 512 + cc * 128
                            self.dma("sp", self.PT[name][r0:r0 + 128, t0:t0 + HALF], sg[si][:], r=["sg%d_%d" % (si, g) for g in range(NG)], w=["PT_" + name])
                w, wkey = load_w(OFF["af"], 16)
                si = nxt("sg", 2)
                for g in range(NG):
                    pi = nxt("ps", 4); pb = self.ps[pi]; pk = "ps%d" % pi
                    for k in range(8):
                        self.mm(pb[0:16, :], w[:, k, 0:16].bitcast(F32R), hT[:, k, g * 512:(g + 1) * 512].bitcast(F32R), k == 0, k == 7,
                                r=[wkey, "hT%d" % g], w=[pk])
                    self.act(sg[si][0:16, g * 512:(g + 1) * 512], pb[0:16, :], AF.Identity, r=[pk], w=["sg%d_%d" % (si, g)])
                self.dma("sp", self.PT["af"][:, t0:t0 + HALF], sg[si][0:16, :], r=["sg%d_%d" % (si, g) for g in range(NG)], w=["PT_af"])
                for name in ("hi", "av"):
                    for c4 in range(2):
                        w, wkey = load_w(OFF[name] + c4 * 512, 512)
                        for tt_ in range(HALF // 128):
                            g = tt_ // 4
                            pi = nxt("ps", 4); pb = self.ps[pi]; pk = "ps%d" % pi
                            for k in range(8):
                                self.mm(pb[:], hT[:, k, tt_ * 128:(tt_ + 1) * 128].bitcast(F32R), w[:, k, :].bitcast(F32R), k == 0, k == 7,
                                        r=[wkey, "hT%d" % g], w=[pk])
                            oi = nxt("so", 2)
                            self.cp("dve", so[oi][:], pb[:], r=[pk], w=["so%d" % oi])
                            r0 = t0 + tt_ * 128
                            self.dma("sp", self.PR[name][r0:r0 + 128, c4 * 512:(c4 + 1) * 512], so[oi][:], r=["so%d" % oi], w=["PR_" + name])
            self.end_stage()

    def st_hgrn(self, l):
        nc, P = self.nc, self.P
        NCH = 3
        with ExitStack() as st:
            A = lambda n, s, dt=F32: st.enter_context(nc.sbuf_tensor(self.uid(n), list(s), dt))
            Sst = [A("S%d" % h, [128, 128]) for h in range(HG_H)]
            Sbf = [A("Sb%d" % h, [128, 128], BF16) for h in range(HG_H)]
            ATm = [[A("ATm%d_%d" % (c, i), [128, 128], BF16) for i in range(2)] for c in range(NCH)]
            khT = [[A("khT%d_%d" % (c, i), [128, 128], BF16) for i in range(2)] for c in range(NCH)]
            ptn = ["SQ", "SF", "kk", "gg", "BB", "D1", "D4", "E1", "E2", "E3", "E4"]
            PTs = [{n: A("%s_%d" % (n, i), [128, 512]) for n in ptn} for i in range(2)]
            US = []
            for i in range(2 * NCH):
                d = {n: A("%s_%d" % (n, i), [128, 512], BF16) for n in ("qt", "kt", "qh", "kh")}
                d["dec"] = A("dec_%d" % i, [128, 8])
                for n in ("I", "IA", "IB"):
                    d[n] = A("%s_%d" % (n, i), [128, 4, 128], BF16)
                d["SG"] = A("SG_%d" % i, [128, 512]); d["Ob"] = A("Ob_%d" % i, [128, 512])
                US.append(d)
            EP = [{n: A("%s_%d" % (n, i), [128, 512]) for n in ("sq", "rs", "on")} for i in range(2)]
            for h in range(HG_H):
                self.ms("dve", Sst[h][:], 0.0, w=["S%d" % h])
                self.ms("pool", Sbf[h][:], 0.0, w=["Sb%d" % h])
            for c in range(NCH):
                for i in range(2):
                    self.ms("pool", ATm[c][i][:], 0.0, w=["ATm%d_%d" % (c, i)])
            for i in range(2 * NCH):
                self.ms("pool", US[i]["IA"][:], 0.0, w=["IA_%d" % i])
                self.ms("pool", US[i]["IB"][:], 0.0, w=["IB_%d" % i])
            scanmask = self.c_cst[:, 512:1024]
            NU = (S // 512) * HG_H

            def unit(n):
                bi, h = divmod(n, HG_H)
                return bi, h, slice(bi * 512, (bi + 1) * 512), slice(h * 128, (h + 1) * 128)

            def prologue(n):
                bi, h, cols, rows = unit(n)
                t = PTs[n % 2]; u = US[n % (2 * NCH)]
                tk = lambda nm: "%s_%d" % (nm, n % 2)
                uk = lambda nm: "%s_%d" % (nm, n % (2 * NCH))
                self.dma("sp", t["SQ"][:], self.PT["hq"][rows, cols], w=[tk("SQ")])
                self.dma("sp", t["SF"][:], self.PT["hf"][rows, cols], w=[tk("SF")])
                self.dma("sp", u["SG"][:], self.PT["hg"][rows, cols], w=[uk("SG")])
                isrc = self.PR["hi"][cols, rows].rearrange("(j p) v -> p j v", p=128)
                self.dma("pool", u["I"][:], isrc, w=[uk("I")])
                self.dma("pool", u["IA"][0:64, :, :], isrc[0:64], w=[uk("IA")])
                self.dma("pool", u["IB"][64:128, :, :], isrc[64:128], w=[uk("IB")])
                oml = self.oml[:, l, h:h + 1]; noml = self.noml[:, l, h:h + 1]; lb = self.lbc[:, l, h:h + 1]
                self.ts("dve", t["kk"][:], t["SF"][:], noml, ALU.mult, oml, ALU.add, r=[tk("SF")], w=[tk("kk")])
                self.act(t["gg"][:], t["SF"][:], AF.Ln, r=[tk("SF")], w=[tk("gg")], scale=oml, bias=lb)
                P.op("dve", lambda E, o=t["BB"][:], m=scanmask, g=t["gg"][:]: E.tensor_tensor_scan(out=o, data0=m, data1=g, initial=0.0, op0=ALU.mult, op1=ALU.add),
                     reads=[tk("gg")], writes=[tk("BB")])
                B3 = t["BB"][:].rearrange("p (c t) -> p c t", t=64)
                self.tt("pool", t["D1"][:].rearrange("p (c t) -> p c t", t=64), B3, B3[:, :, 31:32].to_broadcast([128, 8, 64]), ALU.subtract, r=[tk("BB")], w=[tk("D1")])
                self.tt("pool", t["D4"][:].rearrange("p (c t) -> p c t", t=64), B3, B3[:, :, 63:64].to_broadcast([128, 8, 64]), ALU.subtract, r=[tk("BB")], w=[tk("D4")])
                self.act(t["E1"][:], t["D1"][:], AF.Exp, r=[tk("D1")], w=[tk("E1")])
                self.act(t["E2"][:], t["D1"][:], AF.Exp, r=[tk("D1")], w=[tk("E2")], scale=-1.0)
                self.act(t["E3"][:], t["BB"][:], AF.Exp, r=[tk("BB")], w=[tk("E3")])
                self.act(t["E4"][:], t["D4"][:], AF.Exp, r=[tk("D4")], w=[tk("E4")], scale=-1.0)
                self.tt("dve", u["qt"][:], t["SQ"][:], t["E1"][:], ALU.mult, r=[tk("SQ"), tk("E1")], w=[uk("qt")])
                self.tt("pool", u["kt"][:], t["kk"][:], t["E2"][:], ALU.mult, r=[tk("kk"), tk("E2")], w=[uk("kt")])
                self.tt("dve", u["qh"][:], t["SQ"][:], t["E3"][:], ALU.mult, r=[tk("SQ"), tk("E3")], w=[uk("qh")])
                self.tt("pool", u["kh"][:], t["kk"][:], t["E4"][:], ALU.mult, r=[tk("kk"), tk("E4")], w=[uk("kh")])
                self.cp("dve", u["dec"][:], t["E3"][:].rearrange("p (c t) -> p c t", t=64)[:, :, 63], r=[tk("E3")], w=[uk("dec")])

            def tile_steps(n, c, j):
                bi, h, cols, rows = unit(n)
                u = US[n % (2 * NCH)]
                uk = lambda nm: "%s_%d" % (nm, n % (2 * NCH))
                jp = j % 2
                c0 = j * 128
                at_ps = self.ps[0][:, c * 128:(c + 1) * 128]; atk = "ps0"
                kh_ps = self.ps[1 + 2 * c][:, 0:64].bitcast(BF16); khk = "ps%d" % (1 + 2 * c)
                su = self.ps[1 + 2 * c][:, 128:256]; suk = khk
                o_ps = self.ps[2 + 2 * c][:, 0:128]; ok = "ps%d" % (2 + 2 * c)
                AT = ATm[c][jp]; ATk = "ATm%d_%d" % (c, jp)
                KT = khT[c][jp]; KTk = "khT%d_%d" % (c, jp)
                Sk, Sbk = "S%d" % h, "Sb%d" % h

                def s1():
                    self.mm(at_ps, u["kt"][:, c0:c0 + 128], u["qt"][:, c0:c0 + 128], True, True, r=[uk("kt"), uk("qt")], w=[atk])
                    self.tr(kh_ps, u["kh"][:, c0:c0 + 128], self.c_identb[:], r=[uk("kh")], w=[khk])

                def s2():
                    P.op("dve", lambda E, o=AT[:], m=self.c_bdc[:], d=at_ps: E.copy_predicated(out=o, mask=m, data=d), reads=[atk], writes=[ATk])
                    self.act(KT[:], kh_ps, AF.Identity, r=[khk], w=[KTk])

                def s3():
                    self.mm(o_ps, u["I"][:, j, :], AT[:], True, False, r=[uk("I"), ATk], w=[ok])
                    self.mm(o_ps[:, 0:64], Sbf[h][:], u["qh"][:, c0:c0 + 64], False, False, r=[Sbk, uk("qh")], w=[ok])
                    self.mm(su, KT[:], u["IA"][:, j, :], True, True, r=[KTk, uk("IA")], w=[suk])

                def s4():
                    self.stt(Sbf[h][:], Sst[h][:], u["dec"][:, 2 * j:2 * j + 1], su, ALU.mult, ALU.add, r=[suk, uk("dec"), Sk], w=[Sbk])
                    self.stt(Sst[h][:], Sst[h][:], u["dec"][:, 2 * j:2 * j + 1], su, ALU.mult, ALU.add, r=[suk, uk("dec"), Sk], w=[Sk])

                def s5():
                    self.mm(o_ps[:, 64:128], Sbf[h][:], u["qh"][:, c0 + 64:c0 + 128], False, True, r=[Sbk, uk("qh")], w=[ok])
                    self.mm(su, KT[:], u["IB"][:, j, :], True, True, r=[KTk, uk("IB")], w=[suk])

                def s6():
                    self.act(u["Ob"][:, c0:c0 + 128], o_ps, AF.Identity, r=[ok], w=[uk("Ob") + "_%d" % j])
                    self.stt(Sbf[h][:], Sst[h][:], u["dec"][:, 2 * j + 1:2 * j + 2], su, ALU.mult, ALU.add, r=[suk, uk("dec"), Sk], w=[Sbk])
                    self.stt(Sst[h][:], Sst[h][:], u["dec"][:, 2 * j + 1:2 * j + 2], su, ALU.mult, ALU.add, r=[suk, uk("dec"), Sk], w=[Sk])
                return [s1, s2, s3, s4, s5, s6]

            def epilogue_steps(n, c):
                bi, h, cols, rows = unit(n)
                u = US[n % (2 * NCH)]; e = EP[c % 2]
                uk = lambda nm: "%s_%d" % (nm, n % (2 * NCH))
                ek = lambda nm: "%s_%d" % (nm, c % 2)
                obk = [uk("Ob") + "_%d" % j for j in range(4)]

                def e1():
                    self.act(e["sq"][:].bitcast(F32R), u["Ob"][:], AF.Square, r=obk, w=[ek("sq")])

                def e2():
                    self.mm(self.ps[7][:], self.c_onesr[:].bitcast(F32R), e["sq"][:].bitcast(F32R), True, True, r=[ek("sq")], w=["ps7"])

                def e3():
                    self.act(e["rs"][:], self.ps[7][:], AF.Ln, r=["ps7"], w=[ek("rs")], scale=1.0 / 128.0, bias=self.c_eps[:, 1:2])
                    self.act(e["rs"][:], e["rs"][:], AF.Exp, r=[ek("rs")], w=[ek("rs")], scale=-0.5)

                def e4():
                    self.tt("dve", e["on"][:], u["Ob"][:], e["rs"][:], ALU.mult, r=obk + [ek("rs")], w=[ek("on")])
                    self.stt(e["on"][:], e["on"][:], self.nw[:, l:l + 1], u["SG"][:], ALU.mult, ALU.mult, r=[ek("on"), uk("SG")], w=[ek("on")])
                    self.dma("sp", self.OAT[rows, cols], e["on"][:], r=[ek("on")], w=["OAT"])
                return [e1, e2, e3, e4]

            for n in range(min(NCH, NU)):
                prologue(n)
            for g0 in range(0, NU, NCH):
                grp = list(range(g0, min(g0 + NCH, NU)))
                for j in range(4):
                    steps = [tile_steps(n, c, j) for c, n in enumerate(grp)]
                    for si in range(6):
                        for stp in steps:
                            stp[si]()
                    nxt = g0 + NCH + j
                    if j < NCH and nxt < NU:
                        prologue(nxt)
                for c, n in enumerate(grp):
                    for stp in epilogue_steps(n, c):
                        stp()
            self.end_stage()

    def st_fox(self, l):
        nc, P = self.nc, self.P
        with ExitStack() as st:
            A = lambda n, s, dt=F32: st.enter_context(nc.sbuf_tensor(self.uid(n), list(s), dt))
            ft = A("ft", [16, S]); Fc = A("Fc", [16, S])
            Ftok = A("Ftok", [128, NT, 16]); FrefB = A("FrefB", [128, 16, 8]); rbd = A("rbd", [16, 16, 8])
            vaug = [A("vaug%d" % i, [128, NT, 128], BF16) for i in range(2)]
            QA = [A("QA%d" % i, [128, S], BF16) for i in range(2)]
            QB = [A("QB%d" % i, [128, S], BF16) for i in range(2)]
            KP = [A("KP%d" % i, [128, S], BF16) for i in range(2)]
            e64 = A("e64", [128, 128]); e64r = A("e64r", [128, 128])
            bias = [A("bias%d" % i, [128, NT, 8]) for i in range(2)]
            Pb = [A("Pb%d" % i, [128, 512], BF16) for i in range(4)]
            Osb = [A("Osb%d" % i, [64, 512]) for i in range(2)]
            rc = [A("rc%d" % i, [128, 512]) for i in range(2)]
            zz = A("zz", [128, 512])
            ob = [A("ob%d" % i, [64, 512]) for i in range(2)]
            for i in range(2):
                self.ms("pool", vaug[i][:], 0.0, w=["vaug%d" % i])
                self.ms("pool", vaug[i][:, :, 64:65], 1.0, w=["vaug%d" % i])
                self.ms("pool", QA[i][64:128, :], 0.0, w=["QA%d" % i])
                self.ms("pool", QB[i][0:64, :], 0.0, w=["QB%d" % i])
            self.ms("dve", zz[:], 0.0, w=["zz"])
            self.ms("dve", e64[:], 0.0, w=["e64"])
            self.ms("dve", e64[64:65, :], 1.0, w=["e64"])
            self.act(e64r[:].bitcast(F32R), e64[:], AF.Identity, r=["e64"], w=["e64r"])
            for i in range(2):
                self.act(rc[i][:].bitcast(F32R), zz[:], AF.Identity, r=["zz"], w=["rc%d" % i])
            self.dma("sp", ft[:], self.PT["af"], w=["ft"])
            self.act(ft[:], ft[:], AF.Sigmoid, r=["ft"], w=["ft"], bias=self.fb[:, l:l + 1])
            self.act(ft[:], ft[:], AF.Ln, r=["ft"], w=["ft"])
            P.op("dve", lambda E: E.tensor_tensor_scan(out=Fc[:], data0=self.c_ones[0:16, 0:1].to_broadcast([16, S]), data1=ft[:], initial=0.0, op0=ALU.mult, op1=ALU.add),
                 reads=["ft"], writes=["Fc"])
            for i in range(NT):
                self.tr(self.ps[0][:, i * 16:(i + 1) * 16], Fc[0:16, i * 128:(i + 1) * 128], self.c_ident[0:16, 0:16], r=["Fc"], w=["ps0"])
            self.act(Ftok[:].rearrange("p i h -> p (i h)"), self.ps[0][:], AF.Identity, r=["ps0"], w=["Ftok"])
            fref = Fc[:].rearrange("h (j t) -> h j t", t=512)[:, :, 256]
            self.tt("dve", rbd[:], self.c_ident[0:16, 0:16].unsqueeze(2).to_broadcast([16, 16, 8]), fref.unsqueeze(1).to_broadcast([16, 16, 8]), ALU.mult,
                    r=["Fc"], w=["rbd"])
            self.mm(self.ps[1][:, 0:128], self.c_ones[0:16, :], rbd[:].rearrange("h a j -> h (a j)"), True, True, r=["rbd"], w=["ps1"])
            self.act(FrefB[:].rearrange("p a j -> p (a j)"), self.ps[1][:, 0:128], AF.Identity, r=["ps1"], w=["FrefB"])
            LA = 3
            its = []
            for h in range(FX_H):
                for j in range(S // 512):
                    n_i = 4 * j + 4
                    for i in range(n_i):
                        its.append((h, j, i, n_i))
            hstate = {}

            def head_loads(h):
                par = h % 2; pp = (h // 2) % 2
                rows = slice(h * 64, (h + 1) * 64)
                if h % 2 == 0:
                    self.dma("pool", KP[pp][:], self.PT["ak"][h * 64:(h + 2) * 64, :], w=["KP%d" % pp])
                    self.dma("pool", QA[pp][0:64, :], self.PT["aq"][rows, :], w=["QA%d" % pp])
                    qsrc = QA[pp]; qkey = "QA%d" % pp
                else:
                    self.dma("pool", QB[pp][64:128, :], self.PT["aq"][rows, :], w=["QB%d" % pp])
                    qsrc = QB[pp]; qkey = "QB%d" % pp
                self.dma("pool", vaug[par][:, :, 0:64], self.PR["av"][:, rows].rearrange("(i p) v -> p i v", p=128), w=["vaug%d" % par])
                self.tt("dve", bias[par][:], FrefB[:, h, :].unsqueeze(1).to_broadcast([128, NT, 8]), Ftok[:, :, h:h + 1].to_broadcast([128, NT, 8]), ALU.subtract,
                        r=["FrefB", "Ftok"], w=["bias%d" % par])
                hstate[h] = (par, pp, qsrc, qkey, rows)

            def emit_S(n):
                h, j, i, n_i = its[n]
                if h not in hstate:
                    head_loads(h)
                par, pp, qsrc, qkey, rows = hstate[h]
                cs = max(i - 4 * j, 0) * 128
                si = n % 4
                self.mm(self.ps[si][:, cs:512], KP[pp][:, i * 128:(i + 1) * 128], qsrc[:, j * 512 + cs:(j + 1) * 512], True, True,
                        r=["KP%d" % pp, qkey], w=["ps%d" % si])

            deferred = []

            def run_deferred(force=False):
                for d_ in list(deferred):
                    d_[0] -= 1
                    if d_[0] <= 0 or force:
                        d_[1]()
                        deferred.remove(d_)

            nblk = 0
            for n in range(min(LA, len(its))):
                emit_S(n)
            for n in range(len(its)):
                h, j, i, n_i = its[n]
                par, pp, qsrc, qkey, rows = hstate[h]
                if i == 0 and j == 0 and h + 1 < FX_H and (h + 1) not in hstate:
                    head_loads(h + 1)
                if n + LA < len(its):
                    emit_S(n + LA)
                r_ = i - 4 * j
                cs = max(r_, 0) * 128
                si = n % 4; skey = "ps%d" % si
                if i == 0:
                    oi = nblk % 2; nblk += 1
                o_ps = self.ps[4 + oi]; okey = "ps%d" % (4 + oi)
                pi = n % 4; pkey = "Pb%d" % pi
                pt = Pb[pi][:, cs:512]
                self.act(pt, self.ps[si][:, cs:512], AF.Exp, r=[skey, "bias%d" % par], w=[pkey], scale=0.125, bias=bias[par][:, i, j:j + 1])
                if r_ >= 0:
                    self.tt("pool", Pb[pi][:, cs:cs + 128], Pb[pi][:, cs:cs + 128], self.c_trib[:], ALU.mult, r=[pkey], w=[pkey])
                self.mm(o_ps[:, cs:512], vaug[par][:, i, :], pt, i == 0, i == n_i - 1, r=["vaug%d" % par, pkey], w=[okey])
                run_deferred()
                if i == n_i - 1:
                    self.cp("dve", Osb[oi][:], o_ps[0:64, :], r=[okey], w=["Osb%d" % oi])

                    def _rcp(E, o=rc[oi][64:65, :].bitcast(F32R), i_=o_ps[64:65, :]):
                        with nc.allow_low_precision(reason="fp32r operand for the broadcast matmul"):
                            return E.reciprocal(out=o, in_=i_)
                    P.op("dve", _rcp, reads=[okey], writes=["rc%d" % oi])

                    def part_b(oi=oi, rows=rows, j=j):
                        self.mm(self.ps[6][:], e64r[:].bitcast(F32R), rc[oi][:].bitcast(F32R), True, True, r=["rc%d" % oi, "e64r"], w=["ps6"])
                        self.tt("dve", ob[oi][:], Osb[oi][:], self.ps[6][0:64, :], ALU.mult, r=["Osb%d" % oi, "ps6"], w=["ob%d" % oi])
                        self.dma("sp", self.OBT[rows, j * 512:(j + 1) * 512], ob[oi][:], r=["ob%d" % oi], w=["OBT"])
                    deferred.append([3, part_b])
            run_deferred(force=True)
            self.end_stage()

    def st_merge(self, l):
        nc, P = self.nc, self.P
        with ExitStack() as st:
            A = lambda n, s, dt=F32: st.enter_context(nc.sbuf_tensor(self.uid(n), list(s), dt))
            wbr = A("wbr", [128, 16, D], BF16)
            wo = A("wo", [128, 8, D], BF16)
            inb = []
            for i in range(2):
                inb.append({n: A("%s%d" % (n, i), [128, 8, 512], BF16) for n in ("oa", "ob", "ga", "gb")})
            mg = [A("mg%d" % i, [128, 8, 512], BF16) for i in range(2)]
            t1 = [A("t1_%d" % i, [128, 512]) for i in range(2)]
            t2 = [A("t2_%d" % i, [128, 512]) for i in range(2)]
            ysb = [A("ysb%d" % i, [128, D]) for i in range(2)]
            for k2 in range(4):
                self.dma("pool", wbr[:, k2 * 4:(k2 + 1) * 4, :], self.w_branch[l, k2 * 512:(k2 + 1) * 512, :].rearrange("(k p) d -> p k d", p=128), w=["wbr"])
            for k2 in range(2):
                self.dma("pool", wo[:, k2 * 4:(k2 + 1) * 4, :], self.w_out[l, k2 * 512:(k2 + 1) * 512, :].rearrange("(k p) d -> p k d", p=128), w=["wo"])
            nt_ = 0; ny = 0
            for bi in range(S // 512):
                par = bi % 2
                cols = slice(bi * 512, (bi + 1) * 512)
                ib = inb[par]
                for nm, src in (("oa", self.OAT), ("ob", self.OBT), ("ga", self.PT["ga"]), ("gb", self.PT["gb"])):
                    self.dma("pool", ib[nm][:], src[:, cols].rearrange("(k p) t -> p k t", p=128), w=["%s%d" % (nm, par)])
                for m in range(8):
                    pa = self.ps[(2 * m) % 4]; pak = "ps%d" % ((2 * m) % 4)
                    pb_ = self.ps[(2 * m + 1) % 4]; pbk = "ps%d" % ((2 * m + 1) % 4)
                    for k in range(8):
                        self.mm(pa[:], wbr[:, k, m * 128:(m + 1) * 128], ib["oa"][:, k, :], k == 0, k == 7, r=["wbr", "oa%d" % par], w=[pak])
                    for k in range(8):
                        self.mm(pb_[:], wbr[:, 8 + k, m * 128:(m + 1) * 128], ib["ob"][:, k, :], k == 0, k == 7, r=["wbr", "ob%d" % par], w=[pbk])
                    ti = nt_ % 2; nt_ += 1
                    self.tt("dve", t1[ti][:], pa[:], ib["ga"][:, m, :], ALU.mult, r=[pak, "ga%d" % par], w=["t1_%d" % ti])
                    self.tt("dve", t2[ti][:], pb_[:], ib["gb"][:, m, :], ALU.mult, r=[pbk, "gb%d" % par], w=["t2_%d" % ti])
                    self.tt("pool", mg[par][:, m, :], t1[ti][:], t2[ti][:], ALU.add, r=["t1_%d" % ti, "t2_%d" % ti], w=["mg%d_%d" % (par, m)])
                for r_ in range(4):
                    yi = ny % 2; ny += 1
                    for hh in range(2):
                        yp = self.ps[4 + 2 * yi + hh]; ypk = "ps%d" % (4 + 2 * yi + hh)
                        for k in range(8):
                            self.mm(yp[:], mg[par][:, k, r_ * 128:(r_ + 1) * 128], wo[:, k, hh * 512:(hh + 1) * 512], k == 0, k == 7,
                                    r=["mg%d_%d" % (par, k), "wo"], w=[ypk])
                        self.act(ysb[yi][:, hh * 512:(hh + 1) * 512], yp[:], AF.Identity, r=[ypk], w=["ysb%d_%d" % (yi, hh)])
                    r0 = bi * 512 + r_ * 128
                    self.dma("sp", self.YS[r0:r0 + 128, :], ysb[yi][:], r=["ysb%d_0" % yi, "ysb%d_1" % yi], w=["YS"])
            self.end_stage()

    def layer_norm(self, z, zk, xh, xhk, tmp, mvk):
        P = self.P
        stt_, mv, sd = tmp["st"], tmp["mv"], tmp["sd"]
        for c in range(2):
            P.op("dve", lambda E, o=stt_[:, c * 6:(c + 1) * 6], i=z[:, c * 512:(c + 1) * 512]: E.bn_stats(out=o, in_=i), reads=[zk], writes=[mvk + "st%d" % c])
        P.op("dve", lambda E, o=mv[:], i=stt_[:]: E.bn_aggr(out=o, in_=i), reads=[mvk + "st0", mvk + "st1"], writes=[mvk + "mv"])
        self.act(sd[:, 0:1], mv[:, 1:2], AF.Sqrt, r=[mvk + "mv"], w=[mvk + "sd"], bias=self.c_eps[:, 0:1])
        P.op("dve", lambda E, o=sd[:, 1:2], i=sd[:, 0:1]: E.reciprocal(out=o, in_=i), reads=[mvk + "sd"], writes=[mvk + "rstd"])
        self.ts("dve", sd[:, 2:3], mv[:, 0:1], sd[:, 1:2], ALU.mult, -1.0, ALU.mult, r=[mvk + "mv", mvk + "rstd"], w=[mvk + "nmr"])
        self.act(xh, z[:], AF.Identity, r=[zk, mvk + "rstd", mvk + "nmr"], w=[xhk], scale=sd[:, 1:2], bias=sd[:, 2:3])

    def load_row(self, q, dst, src_row, key):
        self.dma(q, dst, src_row.partition_broadcast(128)[:, 0, :], w=[key])

    def st_ln1(self, l):
        nc, P = self.nc, self.P
        with ExitStack() as st:
            A = lambda n, s, dt=F32: st.enter_context(nc.sbuf_tensor(self.uid(n), list(s), dt))
            g1p = A("g1p", [128, D]); lg_ = A("lg_", [128, D]); lb_ = A("lb_", [128, D]); s2p = A("s2p", [128, D]); sh2 = A("sh2", [128, D])
            wr = A("wr", [128, 8, NE]); br = A("br", [128, NE])
            csel = A("csel", [128, NE], BF16)
            modrow = lambda j0: self.MODT[l:l + 1, j0:j0 + 8, :].rearrange("a j p -> a (j p)")
            self.load_row("sp", g1p[:], modrow(16), "g1p")
            self.load_row("sp", sh2[:], modrow(24), "sh2")
            self.load_row("sp", s2p[:], modrow(32), "s2p")
            self.load_row("sp", lg_[:], self.ln1_g[l:l + 1, :], "lg_")
            self.load_row("sp", lb_[:], self.ln1_b[l:l + 1, :], "lb_")
            self.load_row("sp", br[:], self.b_router[l:l + 1, :], "br")
            self.dma("sp", wr[:], self.w_router[l].rearrange("(k p) e -> p k e", p=128), w=["wr"])
            self.ts("dve", g1p[:], g1p[:], 1.0, ALU.add, r=["g1p"], w=["g1p"])
            self.ts("dve", s2p[:], s2p[:], 1.0, ALU.add, r=["s2p"], w=["s2p"])
            self.ms("dve", csel[:], 0.0, w=["csel"])
            iotaC = self.c_cst[:, 1024:1056]
            T = []
            NB = 4
            for i in range(NB):
                d = {n: A("%s%d" % (n, i), [128, D]) for n in ("x", "y", "z", "xh", "x1", "h2")}
                d["h2T"] = A("h2T%d" % i, [128, 8, 128])
                d["st"] = A("st%d" % i, [128, 12]); d["mv"] = A("mv%d" % i, [128, 2]); d["sd"] = A("sd%d" % i, [128, 3])
                d["lgt"] = A("lgt%d" % i, [128, NE]); d["top"] = A("top%d" % i, [128, 8]); d["sel"] = A("sel%d" % i, [128, NE], BF16)
                d["SL"] = A("SL%d" % i, [128, NE]); d["oh"] = A("oh%d" % i, [128, NE]); d["sf"] = A("sf%d" % i, [128, 4])
                d["e4"] = A("e4%d" % i, [128, 4]); d["nt"] = A("nt%d" % i, [128, 2]); d["ovf"] = A("ovf%d" % i, [128, NE])
                T.append(d)
            def loads1(t):
                i_ = t % NB
                self.dma("sp", T[i_]["x"][:], self.XS[t * 128:(t + 1) * 128, :], w=["x%d" % i_])
                self.dma("sp", T[i_]["y"][:], self.YS[t * 128:(t + 1) * 128, :], w=["y%d" % i_])
            loads1(0); loads1(1); loads1(2)
            for t in range(NT):
                if t + 3 < NT:
                    loads1(t + 3)
                i = t % NB; d = T[i]
                kx = lambda nm: "%s%d" % (nm, i)
                rows = slice(t * 128, (t + 1) * 128)
                self.tt("dve", d["z"][:], d["y"][:], g1p[:], ALU.mult, r=[kx("y"), "g1p"], w=[kx("z")])
                self.stt(d["z"][:], d["x"][:], ALPHA, d["z"][:], ALU.mult, ALU.add, r=[kx("x"), kx("z")], w=[kx("z")])
                self.layer_norm(d["z"], kx("z"), d["xh"][:], kx("xh"), d, kx("m"))
                self.tt("dve", d["x1"][:], d["xh"][:], lg_[:], ALU.mult, r=[kx("xh"), "lg_"], w=[kx("x1")])
                self.tt("pool", d["x1"][:], d["x1"][:], lb_[:], ALU.add, r=[kx("x1"), "lb_"], w=[kx("x1")])
                self.dma("sp", self.XS[rows, :], d["x1"][:], r=[kx("x1")], w=["XSw"])
                self.tt("dve", d["h2"][:], d["x1"][:], s2p[:], ALU.mult, r=[kx("x1"), "s2p"], w=[kx("h2")])
                self.tt("pool", d["h2"][:], d["h2"][:], sh2[:], ALU.add, r=[kx("h2"), "sh2"], w=[kx("h2")])
                if self.nst < 6:
                    continue
                for hh in range(2):
                    pb = self.ps[hh]; pk = "ps%d" % hh
                    for k4 in range(4):
                        k = hh * 4 + k4
                        self.tr(pb[:, k4 * 128:(k4 + 1) * 128], d["h2"][:, k * 128:(k + 1) * 128], self.c_ident, r=[kx("h2")], w=[pk])
                    self.act(d["h2T"][:, hh * 4:(hh + 1) * 4, :].rearrange("p k t -> p (k t)"), pb[:], AF.Identity, r=[pk], w=[kx("h2T") + "_%d" % hh])
                lps = self.ps[2][:, 0:NE]
                for k in range(8):
                    self.mm(lps, d["h2T"][:, k, :], wr[:, k, :], k == 0, k == 7, r=[kx("h2T") + "_%d" % (k // 4), "wr"], w=["ps2"])
                self.tt("dve", d["lgt"][:], lps, br[:], ALU.add, r=["ps2", "br"], w=[kx("lgt")])
                P.op("dve", lambda E, o=d["top"][:], i_=d["lgt"][:]: E.max(out=o, in_=i_), reads=[kx("lgt")], writes=[kx("top")])
                self.ts("dve", d["sel"][:], d["lgt"][:], d["top"][:, 3:4], ALU.is_ge, r=[kx("lgt"), kx("top")], w=[kx("sel")])
                pps = self.ps[3][:, 0:NE]
                self.mm(pps, self.c_strb[:], d["sel"][:], True, False, r=[kx("sel")], w=["ps3"])
                self.mm(pps, self.c_onesb[:], csel[:], False, True, r=["csel"], w=["ps3"])
                self.tt("pool", csel[:], csel[:], d["sel"][:], ALU.add, r=["csel", kx("sel")], w=["csel"])
                self.ts("dve", d["ovf"][:], pps, float(CAP), ALU.is_ge, 1.0e7, ALU.mult, r=["ps3"], w=[kx("ovf")])
                self.tt("dve", d["SL"][:], pps, iotaC, ALU.add, r=["ps3"], w=[kx("SL")])
                self.tt("dve", d["SL"][:], d["SL"][:], d["ovf"][:], ALU.add, r=[kx("SL"), kx("ovf")], w=[kx("SL")])
                for k in range(4):
                    self.ts("dve", d["oh"][:], d["lgt"][:], d["top"][:, k:k + 1], ALU.is_equal, r=[kx("lgt"), kx("top")], w=[kx("oh")])
                    self.tt("dve", d["oh"][:], d["oh"][:], d["SL"][:], ALU.mult, r=[kx("oh"), kx("SL")], w=[kx("oh")])
                    P.op("dve", lambda E, o=d["sf"][:, k:k + 1], i_=d["oh"][:]: E.tensor_reduce(out=o, in_=i_, axis=mybir.AxisListType.X, op=ALU.add),
                         reads=[kx("oh")], writes=[kx("sf")])
                self.cp("dve", self.slot[:, t, :], d["sf"][:], r=[kx("sf")], w=["slot%d" % t])
                self.ts("dve", d["nt"][:, 0:1], d["top"][:, 0:1], -1.0, ALU.mult, r=[kx("top")], w=[kx("nt")])
                self.act(d["e4"][:], d["top"][:, 0:4], AF.Exp, r=[kx("top"), kx("nt")], w=[kx("e4")], bias=d["nt"][:, 0:1])
                P.op("dve", lambda E, o=d["nt"][:, 1:2], i_=d["e4"][:]: E.tensor_reduce(out=o, in_=i_, axis=mybir.AxisListType.X, op=ALU.add),
                     reads=[kx("e4")], writes=[kx("nt") + "s"])
                P.op("dve", lambda E, o=d["nt"][:, 1:2]: E.reciprocal(out=o, in_=o), reads=[kx("nt") + "s"], writes=[kx("nt") + "s"])
                self.ts("dve", self.wk[:, t, :], d["e4"][:], d["nt"][:, 1:2], ALU.mult, r=[kx("e4"), kx("nt") + "s"], w=["wk%d" % t])
                for k in range(4):
                    P.dma("pool", lambda E, o=self.XG, ix=self.slot[:, t, k:k + 1], i_=d["h2"][:]: E.indirect_dma_start(
                        out=o, out_offset=bass.IndirectOffsetOnAxis(ap=ix, axis=0), in_=i_, in_offset=None,
                        bounds_check=self.bc_reg(E), oob_is_err=False), reads=[kx("h2"), "slot%d" % t], writes=["XG"])
            if self.nst >= 6:
                cnt_sb = A("cnt_sb", [128, NE], I32)
                cps = self.ps[2][:, 0:NE]
                self.mm(cps, self.c_onesb[:], csel[:], True, True, r=["csel"], w=["ps2"])
                self.cp("dve", cnt_sb[:], cps, r=["ps2"], w=["cnt_sb"])
                self.dma("sp", self.CNT, cnt_sb[0:1, :], r=["cnt_sb"], w=["CNT"])
            self.end_stage()

    def st_experts(self, l):
        nc, P = self.nc, self.P
        blocks = []
        r = 0
        for nr in (512, 384, 128):
            if r < CAP:
                blocks.append((r, min(nr, CAP - r)))
                r += min(nr, CAP - r)
        assert r == CAP
        with ExitStack() as st:
            A = lambda n, s, dt=F32: st.enter_context(nc.sbuf_tensor(self.uid(n), list(s), dt))
            W = [A("W%d" % i, [128, 8, D], BF16) for i in range(6)]
            bcol = [A("bcol%d" % i, [128, 16]) for i in range(2)]
            bd = [A("bd%d" % i, [128, D]) for i in range(2)]
            X = [A("X%d" % i, [128, 4, D], BF16) for i in range(2)]
            XT = [A("XT%d" % i, [128, 8, 512], BF16) for i in range(2)]
            AT = [A("AT%d" % i, [128, 8, 512], BF16) for i in range(2)]
            tm = [{n: A("%s%d" % (n, i), [128, 512]) for n in ("gsb", "ssb", "u1")} for i in range(2)]
            Ysb = [A("Ysb%d" % i, [128, D]) for i in range(2)]

            def load_w(e):
                s0 = (e % 2) * 3
                for i, src in enumerate((self.w_gate, self.w_up, self.w_down)):
                    for k2 in range(2):
                        self.dma("pool", W[s0 + i][:, k2 * 4:(k2 + 1) * 4, :], src[l, e, k2 * 512:(k2 + 1) * 512, :].rearrange("(k p) f -> p k f", p=128),
                                 w=["W%d_%d" % (s0 + i, k2)])
                self.dma("sp", bcol[e % 2][:], self.bgu[l, e], w=["bcol%d" % (e % 2)])
                self.load_row("sp", bd[e % 2][:], self.b_down[l, e:e + 1, :], "bd%d" % (e % 2))

            units = [(e, r0, nr) for e in range(NE) for (r0, nr) in blocks]
            NU = len(units)
            cnt = dict(m=0, y=0)

            def loadX(n):
                e, r0, nr = units[n]
                xi = n % 2
                g0 = e * CAP + r0
                self.dma("pool", X[xi][:, 0:nr // 128, :], self.XG[g0:g0 + nr, :].rearrange("(j p) d -> p j d", p=128), w=["X%d" % xi])

            def phT(n):
                e, r0, nr = units[n]
                xi = n % 2
                for k in range(8):
                    tp = self.ps[k % 2][:, 0:256].bitcast(BF16); tpk = "ps%d" % (k % 2)
                    for j in range(nr // 128):
                        self.tr(tp[:, j * 128:(j + 1) * 128], X[xi][:, j, k * 128:(k + 1) * 128], self.c_identb[:], r=["X%d" % xi], w=[tpk])
                    if k % 2 == 0:
                        self.act(XT[xi][:, k, 0:nr], tp[:, 0:nr], AF.Identity, r=[tpk], w=["XT%d_%d" % (xi, k)])
                    else:
                        self.cp("dve", XT[xi][:, k, 0:nr], tp[:, 0:nr], r=[tpk], w=["XT%d_%d" % (xi, k)])

            def phGU(n):
                e, r0, nr = units[n]
                xi = n % 2
                s0 = (e % 2) * 3
                Wg, Wu = W[s0], W[s0 + 1]
                wkeys = lambda i: ["W%d_0" % (s0 + i), "W%d_1" % (s0 + i)]
                bc = bcol[e % 2]; bck = "bcol%d" % (e % 2)
                xtk = ["XT%d_%d" % (xi, k) for k in range(8)]

                def tail(m, ti):
                    t_ = tm[ti]
                    self.tt("pool", t_["ssb"][:, 0:nr], t_["gsb"][:, 0:nr], t_["ssb"][:, 0:nr], ALU.mult, r=["gsb%d" % ti, "ssb%d" % ti], w=["ssb%d" % ti])
                    self.stt(AT[xi][:, m, 0:nr], t_["u1"][:, 0:nr], 1.0, t_["ssb"][:, 0:nr], ALU.add, ALU.mult, r=["u1%d" % ti, "ssb%d" % ti], w=["AT%d_%d" % (xi, m)])
                ti = 0
                for m in range(8):
                    ti = cnt["m"] % 2; cnt["m"] += 1
                    t_ = tm[ti]
                    gp = self.ps[2 + ti]; gpk = "ps%d" % (2 + ti)
                    up = self.ps[4 + ti]; upk = "ps%d" % (4 + ti)
                    for k in range(8):
                        self.mm(gp[:, 0:nr], Wg[:, k, m * 128:(m + 1) * 128], XT[xi][:, k, 0:nr], k == 0, k == 7, r=wkeys(0) + [xtk[k]], w=[gpk])
                    for k in range(8):
                        self.mm(up[:, 0:nr], Wu[:, k, m * 128:(m + 1) * 128], XT[xi][:, k, 0:nr], k == 0, k == 7, r=wkeys(1) + [xtk[k]], w=[upk])
                    self.ts("dve", t_["gsb"][:, 0:nr], gp[:, 0:nr], bc[:, m:m + 1], ALU.add, 7.0, ALU.min, r=[gpk, bck], w=["gsb%d" % ti])
                    self.act(t_["u1"][:, 0:nr], up[:, 0:nr], AF.Identity, r=[upk, bck], w=["u1%d" % ti], bias=bc[:, 8 + m:9 + m])
                    self.act(t_["ssb"][:, 0:nr], t_["gsb"][:, 0:nr], AF.Sigmoid, r=["gsb%d" % ti], w=["ssb%d" % ti], scale=1.702)
                    self.ts("dve", t_["u1"][:, 0:nr], t_["u1"][:, 0:nr], 7.0, ALU.min, -7.0, ALU.max, r=["u1%d" % ti], w=["u1%d" % ti])
                    if m > 0:
                        tail(m - 1, 1 - ti)
                tail(7, ti)

            def phY(n):
                e, r0, nr = units[n]
                xi = n % 2
                s0 = (e % 2) * 3
                Wd = W[s0 + 2]
                wk_ = ["W%d_0" % (s0 + 2), "W%d_1" % (s0 + 2)]
                atk = ["AT%d_%d" % (xi, k) for k in range(8)]
                g0 = e * CAP + r0
                for j in range(nr // 128):
                    yi = cnt["y"] % 2; cnt["y"] += 1
                    for hh in range(2):
                        yb = 6 + hh
                        yp = self.ps[yb]; ypk = "ps%d" % yb
                        for k in range(8):
                            self.mm(yp[:], AT[xi][:, k, j * 128:(j + 1) * 128], Wd[:, k, hh * 512:(hh + 1) * 512], k == 0, k == 7, r=wk_ + [atk[k]], w=[ypk])
                        self.tt("dve", Ysb[yi][:, hh * 512:(hh + 1) * 512], yp[:], bd[e % 2][:, hh * 512:(hh + 1) * 512], ALU.add,
                                r=[ypk, "bd%d" % (e % 2)], w=["Ysb%d_%d" % (yi, hh)])
                    self.dma("sp", self.YG[g0 + j * 128:g0 + (j + 1) * 128, :], Ysb[yi][:], r=["Ysb%d_0" % yi, "Ysb%d_1" % yi], w=["YG"])

            def cond(n, fn):
                e, r0, nr = units[n]
                if r0 == 0 or not DYN_SKIP:
                    fn(n)
                else:
                    P.begin_region()
                    fn(n)
                    P.end_region(l * NE + e, self.CNT[0:1, e:e + 1], r0)

            load_w(0)
            cond(0, loadX)
            if NU > 1:
                cond(1, loadX)
            cond(0, phT)
            for n in range(NU):
                e, r0, nr = units[n]
                if r0 == 0 and e + 1 < NE:
                    load_w(e + 1)
                cond(n, phGU)
                if n + 1 < NU:
                    cond(n + 1, phT)
                if n + 2 < NU:
                    cond(n + 2, loadX)
                cond(n, phY)
            self.end_stage()

    def st_ln2(self, l):
        nc, P = self.nc, self.P
        last = (l == self.layers - 1)
        with ExitStack() as st:
            A = lambda n, s, dt=F32: st.enter_context(nc.sbuf_tensor(self.uid(n), list(s), dt))
            g2p = A("g2p", [128, D]); lg_ = A("lg2", [128, D]); lb_ = A("lb2", [128, D])
            self.load_row("sp", g2p[:], self.MODT[l:l + 1, 40:48, :].rearrange("a j p -> a (j p)"), "g2p")
            self.load_row("sp", lg_[:], self.ln2_g[l:l + 1, :], "lg2")
            self.load_row("sp", lb_[:], self.ln2_b[l:l + 1, :], "lb2")
            self.ts("dve", g2p[:], g2p[:], 1.0, ALU.add, r=["g2p"], w=["g2p"])
            T = []
            NB = 4
            for i in range(NB):
                d = {n: A("%s%d" % (n, i), [128, D]) for n in ("x", "Y0", "Y1", "Y2", "Y3", "z", "xh")}
                d["st"] = A("st%d" % i, [128, 12]); d["mv"] = A("mv%d" % i, [128, 2]); d["sd"] = A("sd%d" % i, [128, 3])
                T.append(d)
            def loads2(t):
                i_ = t % NB
                self.dma("sp", T[i_]["x"][:], self.XS[t * 128:(t + 1) * 128, :], w=["x%d" % i_])
                for k in range(4):
                    P.dma("pool", lambda E, o=T[i_]["Y%d" % k][:], ix=self.slot[:, t, k:k + 1], i_=self.YG: E.indirect_dma_start(
                        out=o, out_offset=None, in_=i_, in_offset=bass.IndirectOffsetOnAxis(ap=ix, axis=0),
                        bounds_check=self.bc_reg(E), oob_is_err=False), reads=[], writes=["Y%d%d" % (k, i_)])
            loads2(0); loads2(1); loads2(2)
            for t in range(NT):
                if t + 3 < NT:
                    loads2(t + 3)
                i = t % NB; d = T[i]
                kx = lambda nm: "%s%d" % (nm, i)
                rows = slice(t * 128, (t + 1) * 128)
                self.act(d["z"][:], d["Y0"][:], AF.Identity, r=[kx("Y0")], w=[kx("z")], scale=self.wk[:, t, 0:1])
                for k in range(1, 4):
                    self.stt(d["z"][:], d["Y%d" % k][:], self.wk[:, t, k:k + 1], d["z"][:], ALU.mult, ALU.add, r=[kx("Y%d" % k), kx("z")], w=[kx("z")])
                self.tt("pool", d["z"][:], d["z"][:], g2p[:], ALU.mult, r=[kx("z"), "g2p"], w=[kx("z")])
                self.stt(d["z"][:], d["x"][:], ALPHA, d["z"][:], ALU.mult, ALU.add, r=[kx("x"), kx("z")], w=[kx("z")])
                self.layer_norm(d["z"], kx("z"), d["xh"][:], kx("xh"), d, kx("m"))
                self.tt("dve", d["xh"][:], d["xh"][:], lg_[:], ALU.mult, r=[kx("xh"), "lg2"], w=[kx("xh")])
                self.tt("pool", d["xh"][:], d["xh"][:], lb_[:], ALU.add, r=[kx("xh"), "lb2"], w=[kx("xh")])
                dst = self.out if (last and self.upto == "all" and self.layers == DEPTH) else self.XS
                self.dma("sp", dst[rows, :], d["xh"][:], r=[kx("xh")], w=["XSw"])
            self.end_stage()


def make_consts():
    c = np.zeros((128, NCST), np.float32)
    c[:, 0:128] = np.eye(128, dtype=np.float32)
    c[:, 128:256] = 1.0
    s = np.arange(128)[:, None]; t = np.arange(128)[None, :]
    c[:, 256:384] = (s <= t)
    c[:, 384:512] = (s < t)
    m = np.ones(512, np.float32); m[0::64] = 0.0
    c[:, 512:1024] = m[None, :]
    c[:, 1024:1056] = (np.arange(NE, dtype=np.float32) * CAP)[None, :]
    u = ((s <= t) & ((s // 64) == (t // 64))).astype(np.uint32)
    return c, u


def host_inputs(inp, b, layers=DEPTH, experts=True):
    f = lambda a: np.ascontiguousarray(a, dtype=np.float32)
    c, u = make_consts()
    L = layers
    m = {
        "x": f(inp["x"][b]),
        "cT": f(inp["c"][b].reshape(8, 128).T),
        "w_in": inp["w_in"][:L],
        "ffb": f(inp["fox_f_bias"].T),
        "lbl": f(inp["hg_lb_logits"].reshape(DEPTH, 8, 128).transpose(2, 0, 1)),
        "normw": f(inp["hg_norm_w"].T),
        "w_branch": inp["w_branch"][:L], "w_out": inp["w_out"][:L], "ada_w": inp["ada_w"][:L],
        "adaB": f(inp["ada_b"].reshape(DEPTH, 48, 128).transpose(2, 0, 1)),
        "ln1_g": f(inp["ln1_g"]), "ln1_b": f(inp["ln1_b"]), "ln2_g": f(inp["ln2_g"]), "ln2_b": f(inp["ln2_b"]),
        "w_router": f(inp["w_router"]), "b_router": f(inp["b_router"]),
        "bgu": f(np.concatenate([inp["b_gate"].reshape(DEPTH, NE, 8, 128).transpose(0, 1, 3, 2),
                                 inp["b_up"].reshape(DEPTH, NE, 8, 128).transpose(0, 1, 3, 2)], axis=3)),
        "b_down": f(inp["b_down"]),
        "cst": c, "cstu": u,
    }
    if experts:
        m["w_gate"] = inp["w_gate"][:L]; m["w_up"] = inp["w_up"][:L]; m["w_down"] = inp["w_down"][:L]
    return m


_CACHE = {}


def kernel(**inputs):
    inp = {k: np.asarray(v) for k, v in inputs.items()}
    if "nc" not in _CACHE:
        _CACHE["nc"] = K().nc
    nc = _CACHE["nc"]
    shared = host_inputs(inp, 0)
    in_maps = []
    for b in range(8):
        m = dict(shared)
        m["x"] = np.ascontiguousarray(inp["x"][b], dtype=np.float32)
        m["cT"] = np.ascontiguousarray(inp["c"][b].reshape(8, 128).T, dtype=np.float32)
        in_maps.append(m)
    res = run_bass_kernel_spmd(nc, in_maps, core_ids=list(range(8)))
    return np.stack([r["out"] for r in res.results], axis=0).astype(np.float32)
```
